# Optimizing a Trainium2 kernel written in Bass

```python
import math
import jax, jax.numpy as jnp
from jax import lax
import numpy as np

D_MODEL = 4096
BATCH = 2
SEQ = 8192
DEPTH = 1

DA_HEADS = 8
DA_HEAD_DIM = 128
DA_V_DIM = 2 * DA_HEAD_DIM
MLA_HEADS = 16
MLA_Q_RANK = 1024
MLA_KV_RANK = 512
MLA_NOPE_DIM = 128
MLA_ROPE_DIM = 64
MLA_V_DIM = 128
ROPE_THETA = 10000.0
D_FF = 11008
CONV_WIDTH = 3
Q_BLOCK = 128
LN_EPS = 1e-5
RMS_EPS = 1e-6
DEEPNORM_ALPHA = (2.0 * DEPTH) ** 0.25
DEEPNORM_BETA = (8.0 * DEPTH) ** -0.25

DA_Q_COLS = DA_HEADS * 2 * DA_HEAD_DIM
DA_K_COLS = DA_HEADS * 2 * DA_HEAD_DIM
DA_V_COLS = DA_HEADS * DA_V_DIM
GATE_COLS = 2 * D_MODEL
IN_SPLITS = [DA_Q_COLS, DA_K_COLS, DA_V_COLS, MLA_Q_RANK, MLA_KV_RANK, MLA_ROPE_DIM, GATE_COLS]
IN_COLS = sum(IN_SPLITS)
IN_OFFSETS = list(np.cumsum(IN_SPLITS)[:-1])
DA_WIDTH = DA_HEADS * DA_V_DIM
MLA_WIDTH = MLA_HEADS * MLA_V_DIM

kernel_name = "hybrid_diffattn_mla_convffn_deepnorm"


def _layernorm(x, g, b):
    xf = x.astype(jnp.float32)
    mu = jnp.mean(xf, axis=-1, keepdims=True)
    var = jnp.mean(jnp.square(xf - mu), axis=-1, keepdims=True)
    return ((xf - mu) * lax.rsqrt(var + LN_EPS) * g + b).astype(x.dtype)


def _rmsnorm(x, g):
    xf = x.astype(jnp.float32)
    return (xf * lax.rsqrt(jnp.mean(jnp.square(xf), axis=-1, keepdims=True) + RMS_EPS) * g).astype(x.dtype)


def _rope(x, positions):
    half = x.shape[-1] // 2
    inv = ROPE_THETA ** (-jnp.arange(half, dtype=jnp.float32) / half)
    ang = positions.astype(jnp.float32)[..., None] * inv
    ang = ang.reshape(ang.shape[:2] + (1,) * (x.ndim - 3) + (half,))
    cos, sin = jnp.cos(ang), jnp.sin(ang)
    xf = x.astype(jnp.float32)
    x1, x2 = xf[..., :half], xf[..., half:]
    return jnp.concatenate([x1 * cos - x2 * sin, x1 * sin + x2 * cos], axis=-1).astype(x.dtype)


def _to_blocks(a):
    b, s = a.shape[:2]
    return jnp.moveaxis(a.reshape((b, s // Q_BLOCK, Q_BLOCK) + a.shape[2:]), 1, 0)


def _from_blocks(a):
    nb, b, qb = a.shape[:3]
    return jnp.moveaxis(a, 0, 1).reshape((b, nb * qb) + a.shape[3:])


def _diff_attention(q, k, v, positions, lam):
    slopes = 2.0 ** (-8.0 * jnp.arange(1, DA_HEADS + 1, dtype=jnp.float32) / DA_HEADS)
    scale = DA_HEAD_DIM ** -0.5

    def block(args):
        qb, pq = args
        s = jnp.einsum('bqhmd,bkhmd->bhmqk', qb, k).astype(jnp.float32) * scale
        dist = pq[:, :, None] - positions[:, None, :]
        distf = dist.astype(jnp.float32)[:, None, None]
        s = jnp.where(dist[:, None, None] >= 0, s - slopes[None, :, None, None, None] * distf, -jnp.inf)
        p = jax.nn.softmax(s, axis=-1)
        a = p[:, :, 0] - lam * p[:, :, 1]
        return jnp.einsum('bhqk,bkhe->bqhe', a.astype(v.dtype), v)

    return _from_blocks(lax.map(block, (_to_blocks(q), _to_blocks(positions))))


def _mla_attention(q_nope, q_rope, k_nope, k_rope, v, positions):
    scale = (MLA_NOPE_DIM + MLA_ROPE_DIM) ** -0.5

    def block(args):
        qnb, qrb, pq = args
        s = (jnp.einsum('bqhd,bkhd->bhqk', qnb, k_nope)
             + jnp.einsum('bqhr,bkr->bhqk', qrb, k_rope)).astype(jnp.float32) * scale
        mask = (pq[:, :, None] >= positions[:, None, :])[:, None]
        p = jax.nn.softmax(jnp.where(mask, s, -jnp.inf), axis=-1)
        return jnp.einsum('bhqk,bkhe->bqhe', p.astype(v.dtype), v)

    return _from_blocks(lax.map(block, (_to_blocks(q_nope), _to_blocks(q_rope), _to_blocks(positions))))


def _conv_ffn(h, w_up, conv_w, conv_b, w_down):
    u = h @ w_up
    up = jnp.pad(u, ((0, 0), (CONV_WIDTH - 1, 0), (0, 0)))
    s = u.shape[1]
    c = conv_b + sum(conv_w[j] * up[:, j:j + s] for j in range(CONV_WIDTH))
    gate, val = c[..., :D_FF], c[..., D_FF:]
    return (jax.nn.silu(gate) * val) @ w_down


def setup_inputs(seed: int = 0) -> dict:
    key = jax.random.key(seed)
    ks = jax.random.split(key, 24)
    L, D, F = DEPTH, D_MODEL, D_FF
    nrm = lambda k, shape, fan_in: jax.random.normal(k, shape, jnp.float32) * fan_in ** -0.5
    x = jax.random.normal(ks[0], (BATCH, SEQ, D), jnp.float32)
    offset = jax.random.randint(ks[1], (BATCH, 1), 0, 1024, dtype=jnp.int32)
    positions = jnp.arange(SEQ, dtype=jnp.int32)[None, :] + offset
    col_scale = jnp.concatenate([
        jnp.ones((DA_Q_COLS + DA_K_COLS,), jnp.float32),
        jnp.full((DA_V_COLS,), DEEPNORM_BETA, jnp.float32),
        jnp.ones((MLA_Q_RANK + MLA_KV_RANK + MLA_ROPE_DIM + GATE_COLS,), jnp.float32)])
    w_in = nrm(ks[2], (L, D, IN_COLS), D) * col_scale
    b_gate = 0.02 * jax.random.normal(ks[3], (L, GATE_COLS), jnp.float32)
    da_lambda_q1 = 0.1 * jax.random.normal(ks[4], (L, DA_HEAD_DIM), jnp.float32)
    da_lambda_k1 = 0.1 * jax.random.normal(ks[5], (L, DA_HEAD_DIM), jnp.float32)
    da_lambda_q2 = 0.1 * jax.random.normal(ks[6], (L, DA_HEAD_DIM), jnp.float32)
    da_lambda_k2 = 0.1 * jax.random.normal(ks[7], (L, DA_HEAD_DIM), jnp.float32)
    da_subln_g = 1.0 + 0.02 * jax.random.normal(ks[8], (L, DA_V_DIM), jnp.float32)
    mla_q_norm_g = 1.0 + 0.02 * jax.random.normal(ks[9], (L, MLA_Q_RANK), jnp.float32)
    mla_kv_norm_g = 1.0 + 0.02 * jax.random.normal(ks[10], (L, MLA_KV_RANK), jnp.float32)
    w_uq = nrm(ks[11], (L, MLA_Q_RANK, MLA_HEADS * (MLA_NOPE_DIM + MLA_ROPE_DIM)), MLA_Q_RANK)
    ukv_scale = jnp.tile(jnp.concatenate([jnp.ones((MLA_NOPE_DIM,), jnp.float32),
                                          jnp.full((MLA_V_DIM,), DEEPNORM_BETA, jnp.float32)]), MLA_HEADS)
    w_ukv = nrm(ks[12], (L, MLA_KV_RANK, MLA_HEADS * (MLA_NOPE_DIM + MLA_V_DIM)), MLA_KV_RANK) * ukv_scale
    w_proj_a = nrm(ks[13], (L, DA_WIDTH, D), DA_WIDTH) * DEEPNORM_BETA
    w_proj_b = nrm(ks[14], (L, MLA_WIDTH, D), MLA_WIDTH) * DEEPNORM_BETA
    w_out = nrm(ks[15], (L, D, D), D) * DEEPNORM_BETA
    ln1_g = 1.0 + 0.02 * jax.random.normal(ks[16], (L, D), jnp.float32)
    ln1_b = 0.02 * jax.random.normal(ks[17], (L, D), jnp.float32)
    w_up = nrm(ks[18], (L, D, 2 * F), D) * DEEPNORM_BETA
    conv_w = nrm(ks[19], (L, CONV_WIDTH, 2 * F), CONV_WIDTH)
    conv_b = 0.02 * jax.random.normal(ks[20], (L, 2 * F), jnp.float32)
    w_down = nrm(ks[21], (L, F, D), F) * DEEPNORM_BETA
    ln2_g = 1.0 + 0.02 * jax.random.normal(ks[22], (L, D), jnp.float32)
    ln2_b = 0.02 * jax.random.normal(ks[23], (L, D), jnp.float32)
    return {"x": x, "positions": positions, "w_in": w_in, "b_gate": b_gate,
            "da_lambda_q1": da_lambda_q1, "da_lambda_k1": da_lambda_k1,
            "da_lambda_q2": da_lambda_q2, "da_lambda_k2": da_lambda_k2,
            "da_subln_g": da_subln_g, "mla_q_norm_g": mla_q_norm_g, "mla_kv_norm_g": mla_kv_norm_g,
            "w_uq": w_uq, "w_ukv": w_ukv, "w_proj_a": w_proj_a, "w_proj_b": w_proj_b,
            "w_out": w_out, "ln1_g": ln1_g, "ln1_b": ln1_b, "w_up": w_up, "conv_w": conv_w,
            "conv_b": conv_b, "w_down": w_down, "ln2_g": ln2_g, "ln2_b": ln2_b}


def reference(x, positions, w_in, b_gate, da_lambda_q1, da_lambda_k1, da_lambda_q2, da_lambda_k2,
              da_subln_g, mla_q_norm_g, mla_kv_norm_g, w_uq, w_ukv, w_proj_a, w_proj_b, w_out,
              ln1_g, ln1_b, w_up, conv_w, conv_b, w_down, ln2_g, ln2_b):
    b, s, _ = x.shape
    h = x
    for l in range(DEPTH):
        proj = h @ w_in[l]
        da_q, da_k, da_v, c_q, c_kv, k_r, gate_logits = jnp.split(proj, IN_OFFSETS, axis=-1)

        lambda_init = 0.8 - 0.6 * math.exp(-0.3 * l)
        lam = (jnp.exp(jnp.sum(da_lambda_q1[l].astype(jnp.float32) * da_lambda_k1[l].astype(jnp.float32)))
               - jnp.exp(jnp.sum(da_lambda_q2[l].astype(jnp.float32) * da_lambda_k2[l].astype(jnp.float32)))
               + lambda_init)
        qa = da_q.reshape(b, s, DA_HEADS, 2, DA_HEAD_DIM)
        ka = da_k.reshape(b, s, DA_HEADS, 2, DA_HEAD_DIM)
        va = da_v.reshape(b, s, DA_HEADS, DA_V_DIM)
        oa = _diff_attention(qa, ka, va, positions, lam)
        oa = (_rmsnorm(oa, da_subln_g[l]) * (1.0 - lambda_init)).reshape(b, s, DA_WIDTH)

        qb = (_rmsnorm(c_q, mla_q_norm_g[l]) @ w_uq[l]).reshape(b, s, MLA_HEADS, MLA_NOPE_DIM + MLA_ROPE_DIM)
        q_nope, q_rope = qb[..., :MLA_NOPE_DIM], _rope(qb[..., MLA_NOPE_DIM:], positions)
        kv = (_rmsnorm(c_kv, mla_kv_norm_g[l]) @ w_ukv[l]).reshape(b, s, MLA_HEADS, MLA_NOPE_DIM + MLA_V_DIM)
        k_nope, vb = kv[..., :MLA_NOPE_DIM], kv[..., MLA_NOPE_DIM:]
        k_rope = _rope(k_r, positions)
        ob = _mla_attention(q_nope, q_rope, k_nope, k_rope, vb, positions).reshape(b, s, MLA_WIDTH)

        g = jax.nn.sigmoid(gate_logits + b_gate[l])
        merged = g[..., :D_MODEL] * (oa @ w_proj_a[l]) + g[..., D_MODEL:] * (ob @ w_proj_b[l])
        h = _layernorm(DEEPNORM_ALPHA * h + merged @ w_out[l], ln1_g[l], ln1_b[l])

        f = _conv_ffn(h, w_up[l], conv_w[l], conv_b[l], w_down[l])
        h = _layernorm(DEEPNORM_ALPHA * h + f, ln2_g[l], ln2_b[l])
    return h
```

```python
import math
import numpy as np
import ml_dtypes
from contextlib import ExitStack
import concourse.bass as bass
import concourse.mybir as mybir
from concourse.bass_utils import run_bass_kernel_spmd

F32 = mybir.dt.float32
BF16 = mybir.dt.bfloat16
I32 = mybir.dt.int32
AF = mybir.ActivationFunctionType
ALU = mybir.AluOpType

D = 4096
SEQ = 8192
NCORE = 8
W = 410
NT = 5
NQ = W * NT
NKV = 8192
NB = 64
DFF = 11008
LN_EPS = 1e-5
RMS_EPS = 1e-6
ALPHA = 2.0 ** 0.25
LAMBDA_INIT = 0.8 - 0.6 * math.exp(0.0)
OFF_Q, OFF_K, OFF_V, OFF_CQ, OFF_CKV, OFF_KR, OFF_G = 0, 2048, 4096, 6144, 7168, 7680, 7744
NEG = -30000.0
ALIBI_MM_HEADS = 2
BG_CONVERT = True
BG_EVERY = 4

ENGS = ("pe", "act", "dve", "pool", "sp")
NDMA = {"sp": 24, "pool": 24}


class Res:
    __slots__ = ("lw", "rd", "excl")

    def __init__(self, excl=False):
        self.lw = None
        self.rd = {}
        self.excl = excl


class Op:
    __slots__ = ("eng", "fn", "deps", "sig", "dma", "sem", "val", "dq")

    def __init__(self, eng, fn, dma):
        self.eng = eng
        self.fn = fn
        self.dma = dma
        self.deps = []
        self.sig = False
        self.sem = None
        self.val = 0
        self.dq = None


class Sched:
    def __init__(self, nc, stack):
        self.nc = nc
        self.ops = {e: [] for e in ENGS}
        self.esem = {e: stack.enter_context(nc.semaphore("es_" + e)) for e in ENGS}
        self.ecnt = {e: 0 for e in ENGS}
        self.dsem = {q: [stack.enter_context(nc.semaphore("ds_%s%d" % (q, i))) for i in range(n)]
                     for q, n in NDMA.items()}
        self.dcnt = {q: 0 for q in NDMA}
        self.dhist = {q: [] for q in NDMA}
        self.known = {e: {x: 0 for x in ENGS} for e in ENGS}
        self.dknown = {e: set() for e in ENGS}
        self.nops = 0

    def op(self, eng, fn, reads=(), writes=(), dma=False):
        o = Op(eng, fn, dma)
        deps = []
        if any(r.excl for r in reads):
            writes = list(writes) + [r for r in reads if r.excl]
            reads = [r for r in reads if not r.excl]
        for r in reads:
            if r.lw is not None:
                deps.append(r.lw)
        for w in writes:
            if w.lw is not None:
                deps.append(w.lw)
            deps.extend(w.rd.values())
        if dma:
            q = eng
            n = self.dcnt[q]
            P = len(self.dsem[q])
            o.dq = (q, n)
            if n >= P:
                deps.append(self.dhist[q][n - P])
            self.dhist[q].append(o)
            if len(self.dhist[q]) > 4 * P:
                pass
            self.dcnt[q] = n + 1
        seen = set()
        for d in deps:
            if eng == "pe" and d.eng == "pe" and not d.dma:
                continue
            if id(d) not in seen:
                seen.add(id(d))
                o.deps.append(d)
                d.sig = True
        for r in reads:
            if dma:
                r.rd[("d", id(o))] = o
            else:
                r.rd[eng] = o
        for w in writes:
            w.lw = o
            w.rd = {}
        self.ops[eng].append(o)
        self.nops += 1
        return o

    def flush(self):
        nc = self.nc
        for e in ENGS:
            c = self.ecnt[e]
            for o in self.ops[e]:
                if o.dma:
                    q, n = o.dq
                    P = len(self.dsem[q])
                    o.sem = self.dsem[q][n % P]
                    o.val = 16 * (n // P + 1)
                elif o.sig:
                    c += 1
                    o.sem = self.esem[e]
                    o.val = c
            self.ecnt[e] = c
        last_dma = []
        for q in NDMA:
            P = len(self.dsem[q])
            n = self.dcnt[q]
            for i in range(max(0, n - P), n):
                last_dma.append((self.dsem[q][i % P], 16 * (i // P + 1)))
        ecnt_final = {}
        for e in ENGS:
            lastc = None
            for o in reversed(self.ops[e]):
                if not o.dma:
                    lastc = o
                    break
            if lastc is not None and not lastc.sig:
                lastc.sig = True
                self.ecnt[e] += 1
                lastc.sem = self.esem[e]
                lastc.val = self.ecnt[e]
            ecnt_final[e] = self.ecnt[e]

        def emit_engine(e, eng):
            known = self.known[e]
            dknown = self.dknown[e]
            for o in self.ops[e]:
                for d in o.deps:
                    if d.dma:
                        key = (id(d.sem), d.val)
                        if key in dknown:
                            continue
                        dknown.add(key)
                        eng.wait_ge(d.sem, d.val)
                    else:
                        if known[d.eng] >= d.val:
                            continue
                        known[d.eng] = d.val
                        eng.wait_ge(d.sem, d.val)
                ins = o.fn(eng)
                if o.dma:
                    ins.then_inc(o.sem, 16)
                elif o.sig:
                    ins.then_inc(o.sem, 1)
            for x in ENGS:
                if known[x] < ecnt_final[x]:
                    known[x] = ecnt_final[x]
                    eng.wait_ge(self.esem[x], ecnt_final[x])
            for sem, val in last_dma:
                eng.wait_ge(sem, val)
            dknown.clear()

        with nc.Block() as block:
            @block.tensor
            def _(eng):
                emit_engine("pe", eng)

            @block.scalar
            def _(eng):
                emit_engine("act", eng)

            @block.vector
            def _(eng):
                emit_engine("dve", eng)

            @block.gpsimd
            def _(eng):
                emit_engine("pool", eng)

            @block.sync
            def _(eng):
                emit_engine("sp", eng)
        self.ops = {e: [] for e in ENGS}
        for q in NDMA:
            P = len(self.dsem[q])
            self.dhist[q] = self.dhist[q]


def tile_blocks(i):
    out = []
    nmin = W * i - (NQ - 1)
    nmax = W * i + W - 1 - (NQ - 1)
    for rb in range(NB):
        dmin = nmin + 128 * rb
        dmax = nmax + 128 * rb + 127
        if dmax < 0:
            continue
        out.append((rb, dmin >= 0))
    return out


def build_masks():
    masks = []
    index = {}
    for i in range(NT):
        for rb, full in tile_blocks(i):
            if not full:
                n = np.arange(W)[None, :] + W * i - (NQ - 1)
                ik = np.arange(128)[:, None]
                m = np.where((n + 128 * rb + ik) >= 0, 0.0, NEG).astype(np.float32)
                index[(i, rb)] = len(masks)
                masks.append(m)
    return np.stack(masks, axis=1), index


MASKS_NP, MASK_IDX = build_masks()
NMASK = MASKS_NP.shape[1]

VEC = {}
_o = 0
for _name, _n in (("b_gate", 64), ("ln1_g", 32), ("ln1_b", 32), ("ln2_g", 32), ("ln2_b", 32), ("cw0", 172),
                  ("cw1", 172), ("cw2", 172), ("cb", 172), ("subln", 2), ("qn_g", 8), ("kvn_g", 4)):
    VEC[_name] = _o
    _o += _n
NV = _o


def build_program(upto="E", dbg=()):
    nc = bass.Bass("TRN2", target_bir_lowering=False)
    dt = nc.dram_tensor

    def din(name, shape, dtype=F32):
        return dt(name, list(shape), dtype, kind="ExternalInput").ap()

    def dscr(name, shape, dtype=BF16):
        return dt(name, list(shape), dtype, kind=("ExternalOutput" if name in dbg else "Internal")).ap()

    xqT = din("xqT", [D, NQ])
    xkT = din("xkT", [D, NKV])
    posq = din("posq", [64, NQ], I32)
    posk = din("posk", [64, NKV], I32)
    w_in = din("w_in", [D, 15936])
    w_uq = din("w_uq", [1024, 3072])
    w_ukv = din("w_ukv", [512, 4096])
    w_pa = din("w_pa", [2048, D])
    w_pb = din("w_pb", [2048, D])
    w_out = din("w_out", [D, D])
    w_up = din("w_up", [D, 2 * DFF])
    w_dn = din("w_dn", [DFF, D])
    vecs = din("vecs", [128, NV])
    lamv = din("lamv", [128, 512])
    inv2 = din("inv2", [64, 1])
    rotm = din("rotm", [64, 64], BF16)
    alk = din("alk", [3, 8, 128], BF16)
    alq = din("alq", [8, 3, NQ], BF16)
    abias = din("abias", [128, 8, NT, NB])
    kvalid = din("kvalid", [128, NB])
    masks_d = din("masks", [128, NMASK, W], BF16)
    hv = din("hv", [128, 1])
    outT = dt("outT", [D, 2048], F32, kind="ExternalOutput").ap()

    def wscr(name, K, M, cw):
        ng = (M + cw - 1) // cw
        return dscr(name, [ng, 128, K // 128, cw])

    wq_s = wscr("wq_s", D, 2048, 512)
    wk_s = wscr("wk_s", D, 2048, 512)
    wv_s = wscr("wv_s", D, 2048, 512)
    wcq_s = wscr("wcq_s", D, 1024, 512)
    wckv_s = wscr("wckv_s", D, 512, 512)
    wkr_s = wscr("wkr_s", D, 64, 64)
    wg_s = wscr("wg_s", D, 8192, 256)
    wuq_s = wscr("wuq_s", 1024, 3072, 192)
    wukv_s = wscr("wukv_s", 512, 4096, 512)
    wpa_s = wscr("wpa_s", 2048, D, 256)
    wpb_s = wscr("wpb_s", 2048, D, 256)
    wout_s = wscr("wout_s", D, D, 256)
    wupg_s = wscr("wupg_s", D, DFF, 256)
    wupv_s = wscr("wupv_s", D, DFF, 256)
    wdn_s = wscr("wdn_s", DFF, D, 128)

    kTda = dscr("kTda", [16, 128, NKV])
    vda = dscr("vda", [8, 128, NB, 256])
    kTm = dscr("kTm", [16, 128, NKV])
    kTr = dscr("kTr", [64, NKV])
    vm = dscr("vm", [16, 128, NB, 128])
    qTda = dscr("qTda", [16, 128, NQ])
    qTm = dscr("qTm", [16, 128, NQ])
    qTr = dscr("qTr", [16, 64, NQ])
    oaT = dscr("oaT", [16, 128, NQ])
    obT = dscr("obT", [16, 128, NQ])
    yscr = dscr("yscr", [32, 128, NQ], F32)
    h1f = dscr("h1f", [32, 128, NQ], F32)
    h1b = dscr("h1b", [32, 128, NQ], BF16)
    zscr = dscr("zscr", [32, 128, NQ], F32)
    adbg = dscr("adbg", [86, 128, W], BF16) if "adbg" in dbg else None

    with ExitStack() as top:
        S = Sched(nc, top)
        sb = lambda st, name, shape, dtype: st.enter_context(nc.sbuf_tensor(name, list(shape), dtype))
        ps = [top.enter_context(nc.psum_tensor("ps%d" % i, [128, 512], F32)) for i in range(8)]
        rps = [Res(True) for _ in range(8)]
        psn = [0]

        def next_ps():
            b = psn[0] % 8
            psn[0] += 1
            return b

        vec_t = sb(top, "vec_t", [128, NV], F32)
        ones_b = sb(top, "ones_b", [128, 128], BF16)
        rot_t = sb(top, "rot_t", [64, 64], BF16)
        inv_t = sb(top, "inv_t", [64, 1], F32)
        hv_t = sb(top, "hv_t", [128, 1], F32)
        nlam_t = sb(top, "nlam_t", [128, 1], F32)
        sgs_t = sb(top, "sgs_t", [128, 2], F32)
        eps_t = sb(top, "eps_t", [128, 2], F32)
        r_const = Res()

        def vcol(name, c):
            k = VEC[name] + c
            return vec_t[:, k:k + 1]

        def ld(eng_q, out, in_, writes, reads=()):
            return S.op(eng_q, lambda e: e.dma_start(out=out, in_=in_), reads=reads, writes=writes, dma=True)

        with ExitStack() as st:
            lam_t = sb(st, "lam_t", [128, 512], F32)
            lt1 = sb(st, "lt1", [128, 256], F32)
            ls = sb(st, "ls", [128, 4], F32)
            rl = Res()
            ld("sp", vec_t[:], vecs, [r_const])
            ld("sp", rot_t[:], rotm, [r_const])
            ld("sp", inv_t[:], inv2, [r_const])
            ld("sp", hv_t[:], hv, [r_const])
            ld("sp", lam_t[:], lamv, [rl])
            S.op("dve", lambda e: e.memset(ones_b[:], 1.0), writes=[r_const])
            S.op("dve", lambda e: e.memset(eps_t[:, 0:1], RMS_EPS), writes=[r_const])
            S.op("dve", lambda e: e.memset(eps_t[:, 1:2], LN_EPS), writes=[r_const])
            S.op("dve", lambda e: e.tensor_tensor(out=lt1[:, 0:128], in0=lam_t[:, 0:128], in1=lam_t[:, 128:256],
                                                  op=ALU.mult), reads=[rl], writes=[rl])
            S.op("dve", lambda e: e.tensor_tensor(out=lt1[:, 128:256], in0=lam_t[:, 256:384], in1=lam_t[:, 384:512],
                                                  op=ALU.mult), reads=[rl], writes=[rl])
            S.op("dve", lambda e: e.reduce_sum(out=ls[:, 0:1], in_=lt1[:, 0:128], axis=mybir.AxisListType.X),
                 reads=[rl], writes=[rl])
            S.op("dve", lambda e: e.reduce_sum(out=ls[:, 1:2], in_=lt1[:, 128:256], axis=mybir.AxisListType.X),
                 reads=[rl], writes=[rl])
            S.op("act", lambda e: e.activation(out=ls[:, 2:4], in_=ls[:, 0:2], func=AF.Exp), reads=[rl], writes=[rl])
            S.op("dve", lambda e: e.tensor_tensor(out=ls[:, 0:1], in0=ls[:, 3:4], in1=ls[:, 2:3], op=ALU.subtract),
                 reads=[rl], writes=[rl])
            S.op("dve", lambda e: e.tensor_scalar(out=nlam_t[:], in0=ls[:, 0:1], scalar1=-LAMBDA_INIT, scalar2=None,
                                                  op0=ALU.add), reads=[rl], writes=[r_const])
            S.op("dve", lambda e: e.tensor_scalar(out=sgs_t[:], in0=vec_t[:, VEC["subln"]:VEC["subln"] + 2],
                                                  scalar1=1.0 - LAMBDA_INIT, scalar2=None, op0=ALU.mult),
                 reads=[r_const], writes=[r_const])
            S.flush()

        def make_converter(st, KH, nbuf, engines, tag):
            stg = [sb(st, "cvf%s%d" % (tag, i), [128, KH * 512], F32) for i in range(nbuf)]
            stb = [sb(st, "cvb%s%d" % (tag, i), [128, KH * 512], BF16) for i in range(nbuf)]
            rf = [Res() for _ in range(nbuf)]
            rb_ = [Res() for _ in range(nbuf)]
            cnt = [0]

            def convert(src, col0, K, M, cw, dst):
                KC = K // 128
                ng = (M + cw - 1) // cw
                kh = max(1, min(KC, (KH * 512) // cw))
                for g in range(ng):
                    wd = min(cw, M - g * cw)
                    for k0 in range(0, KC, kh):
                        k1 = min(KC, k0 + kh)
                        nk = k1 - k0
                        i = cnt[0] % nbuf
                        ce = engines[cnt[0] % len(engines)]
                        cnt[0] += 1
                        fv = stg[i][:, 0:nk * wd].rearrange("p (k c) -> p k c", c=wd)
                        bv = stb[i][:, 0:nk * wd].rearrange("p (k c) -> p k c", c=wd)
                        sv = src[k0 * 128:k1 * 128, col0 + g * cw:col0 + g * cw + wd].rearrange(
                            "(k p) c -> p k c", p=128)
                        ld("sp", fv, sv, [rf[i]])
                        if ce == "act":
                            S.op("act", lambda e, bv=bv, fv=fv: e.copy(out=bv, in_=fv), reads=[rf[i]], writes=[rb_[i]])
                        else:
                            S.op(ce, lambda e, bv=bv, fv=fv: e.tensor_copy(out=bv, in_=fv), reads=[rf[i]],
                                 writes=[rb_[i]])
                        ld("pool", dst[g, :, k0:k1, 0:wd], bv, [], reads=[rb_[i]])
                        yield
            return convert

        def late_weights(convert):
            yield from convert(w_in, OFF_G, D, 8192, 256, wg_s)
            yield from convert(w_pa, 0, 2048, D, 256, wpa_s)
            yield from convert(w_pb, 0, 2048, D, 256, wpb_s)
            yield from convert(w_out, 0, D, D, 256, wout_s)
            yield from convert(w_up, 0, D, DFF, 256, wupg_s)
            yield from convert(w_up, DFF, D, DFF, 256, wupv_s)
            yield from convert(w_dn, 0, DFF, D, 128, wdn_s)

        with ExitStack() as st:
            convert = make_converter(st, 16, 3, ("act", "dve", "pool"), "p")
            for args in ((w_in, OFF_K, D, 2048, 512, wk_s), (w_in, OFF_V, D, 2048, 512, wv_s),
                         (w_in, OFF_CKV, D, 512, 512, wckv_s), (w_in, OFF_KR, D, 64, 64, wkr_s),
                         (w_ukv, 0, 512, 4096, 512, wukv_s), (w_in, OFF_Q, D, 2048, 512, wq_s),
                         (w_in, OFF_CQ, D, 1024, 512, wcq_s), (w_uq, 0, 1024, 3072, 192, wuq_s)):
                for _ in convert(*args):
                    pass
            if not BG_CONVERT:
                for _ in late_weights(convert):
                    pass
            S.flush()

        if upto == "P":
            return nc
        def rope_tables(st, tag, posd, c0, n, cs_t, rcs):
            pi_t, pf_t, pa_t, rr = tag
            TWO_PI = 2.0 * math.pi
            C1 = 6.28125
            C2 = TWO_PI - C1
            ki_t = pi_t
            ld("sp", pi_t[:, 0:n], posd[:, c0:c0 + n], [rr])
            S.op("dve", lambda e: e.tensor_copy(out=pf_t[:, 0:n], in_=pi_t[:, 0:n]), reads=[rr], writes=[rr])
            S.op("dve", lambda e: e.tensor_scalar(out=pa_t[:, 0:n], in0=pf_t[:, 0:n], scalar1=inv_t[:, 0:1],
                                                  scalar2=None, op0=ALU.mult), reads=[rr, r_const], writes=[rr])
            S.op("dve", lambda e: e.tensor_scalar(out=ki_t[:, 0:n], in0=pa_t[:, 0:n], scalar1=1.0 / TWO_PI,
                                                  scalar2=None, op0=ALU.mult), reads=[rr], writes=[rr])
            S.op("dve", lambda e: e.tensor_copy(out=pf_t[:, 0:n], in_=ki_t[:, 0:n]), reads=[rr], writes=[rr])
            S.op("dve", lambda e: e.scalar_tensor_tensor(out=pa_t[:, 0:n], in0=pf_t[:, 0:n], scalar=-C1,
                                                         in1=pa_t[:, 0:n], op0=ALU.mult, op1=ALU.add),
                 reads=[rr], writes=[rr])
            S.op("dve", lambda e: e.scalar_tensor_tensor(out=pa_t[:, 0:n], in0=pf_t[:, 0:n], scalar=-C2,
                                                         in1=pa_t[:, 0:n], op0=ALU.mult, op1=ALU.add),
                 reads=[rr], writes=[rr])

            def wrap():
                S.op("dve", lambda e: e.tensor_scalar(out=pf_t[:, 0:n], in0=pa_t[:, 0:n], scalar1=math.pi,
                                                      scalar2=None, op0=ALU.is_gt), reads=[rr], writes=[rr])
                S.op("dve", lambda e: e.scalar_tensor_tensor(out=pa_t[:, 0:n], in0=pf_t[:, 0:n], scalar=-TWO_PI,
                                                             in1=pa_t[:, 0:n], op0=ALU.mult, op1=ALU.add),
                     reads=[rr], writes=[rr])
                S.op("dve", lambda e: e.tensor_scalar(out=pf_t[:, 0:n], in0=pa_t[:, 0:n], scalar1=-math.pi,
                                                      scalar2=None, op0=ALU.is_lt), reads=[rr], writes=[rr])
                S.op("dve", lambda e: e.scalar_tensor_tensor(out=pa_t[:, 0:n], in0=pf_t[:, 0:n], scalar=TWO_PI,
                                                             in1=pa_t[:, 0:n], op0=ALU.mult, op1=ALU.add),
                     reads=[rr], writes=[rr])

            wrap()
            S.op("act", lambda e: e.activation(out=cs_t[:, 1, 0:n], in_=pa_t[:, 0:n], func=AF.Sin), reads=[rr],
                 writes=[rcs, rr])
            S.op("dve", lambda e: e.tensor_scalar(out=pa_t[:, 0:n], in0=pa_t[:, 0:n], scalar1=0.5 * math.pi,
                                                  scalar2=None, op0=ALU.add), reads=[rr], writes=[rr])
            wrap()
            S.op("act", lambda e: e.activation(out=cs_t[:, 0, 0:n], in_=pa_t[:, 0:n], func=AF.Sin), reads=[rr],
                 writes=[rcs, rr])

        with ExitStack() as st:
            xkb = [sb(st, "xkb%d" % i, [128, 32, 512], BF16) for i in range(2)]
            rxk = [Res() for _ in range(2)]
            xst = [sb(st, "xst%d" % i, [128, 4, 512], F32) for i in range(2)]
            rxst = [Res() for _ in range(2)]
            wsl = [sb(st, "wsl%d" % i, [128, 16384], BF16) for i in range(2)]
            rws = [Res() for _ in range(2)]
            wcnt = [0]
            stgb = [sb(st, "stgb%d" % i, [128, 512], BF16) for i in range(4)]
            rstg = [Res() for _ in range(4)]
            scnt = [0]
            ckv = sb(st, "ckv", [128, 4, 512], F32)
            sq = sb(st, "sqa", [128, 4, 512], BF16)
            rstd = sb(st, "rstda", [128, 512], F32)
            ckvn = sb(st, "ckvn", [128, 4, 512], BF16)
            r_ckv, r_sq, r_rstd, r_ckvn = Res(), Res(), Res(), Res()
            pi_t = sb(st, "pi_a", [64, 512], I32)
            pf_t = sb(st, "pf_a", [64, 512], F32)
            pa_t = sb(st, "pa_a", [64, 512], F32)
            cs_t = sb(st, "cs_a", [64, 2, 512], F32)
            krf = sb(st, "krf", [64, 512], F32)
            krb = sb(st, "krb", [64, 512], BF16)
            kro = sb(st, "kro", [64, 512], F32)
            r_rope, r_cs, r_kr = Res(), Res(), Res()

            def load_slab(scr_g, nelem):
                i = wcnt[0] % 2
                wcnt[0] += 1
                ld("sp", wsl[i][:, 0:nelem], scr_g.rearrange("p k c -> p (k c)"), [rws[i]])
                return i

            def stage_out(psb, M, N, dst, scale=None, view=None):
                k = scnt[0] % 4
                scnt[0] += 1
                if scale is None:
                    S.op("act", lambda e: e.copy(out=stgb[k][0:M, 0:N], in_=ps[psb][0:M, 0:N]), reads=[rps[psb]],
                         writes=[rstg[k]])
                else:
                    S.op("act", lambda e: e.activation(out=stgb[k][0:M, 0:N], in_=ps[psb][0:M, 0:N], func=AF.Copy,
                                                       scale=scale), reads=[rps[psb]], writes=[rstg[k]])
                srcv = stgb[k][0:M, 0:N]
                if view is not None:
                    srcv = view(srcv)
                ld("pool", dst, srcv, [], reads=[rstg[k]])

            for g in range(16):
                xi = g % 2
                t0 = g * 512
                for q4 in range(8):
                    si = q4 % 2
                    ld("sp", xst[si][:], xkT[q4 * 512:(q4 + 1) * 512, t0:t0 + 512].rearrange("(k p) n -> p k n", p=128),
                       [rxst[si]])
                    eng = "dve" if q4 % 2 == 0 else "pool"
                    S.op(eng, lambda e, si=si, q4=q4, xi=xi: e.tensor_copy(out=xkb[xi][:, q4 * 4:(q4 + 1) * 4, :],
                                                                          in_=xst[si][:]),
                         reads=[rxst[si]], writes=[rxk[xi]])
                xk = xkb[xi]
                wi = load_slab(wckv_s[0], 16384)
                wv_ = wsl[wi][:].rearrange("p (k c) -> p k c", c=512)
                for c in range(4):
                    pb = next_ps()
                    for kc in range(32):
                        S.op("pe", lambda e, wv_=wv_, kc=kc, c=c, pb=pb, xk=xk: e.matmul(
                            ps[pb][:, :], wv_[:, kc, c * 128:(c + 1) * 128], xk[:, kc, :],
                            start=(kc == 0), stop=(kc == 31)), reads=[rws[wi], rxk[xi]], writes=[rps[pb]])
                    S.op("dve", lambda e, c=c, pb=pb: e.tensor_copy(out=ckv[:, c, :], in_=ps[pb][:, :]),
                         reads=[rps[pb]], writes=[r_ckv])
                    S.op("act", lambda e, c=c, pb=pb: e.activation(out=sq[:, c, :], in_=ps[pb][:, :], func=AF.Square),
                         reads=[rps[pb]], writes=[r_sq])
                pb = next_ps()
                for c in range(4):
                    S.op("pe", lambda e, c=c, pb=pb: e.matmul(ps[pb][:, :], ones_b[:, :], sq[:, c, :], start=(c == 0),
                                                             stop=(c == 3)), reads=[r_sq, r_const], writes=[rps[pb]])
                S.op("act", lambda e, pb=pb: e.activation(out=rstd[:], in_=ps[pb][:, :], func=AF.Sqrt,
                                                          bias=eps_t[:, 0:1], scale=1.0 / 512.0),
                     reads=[rps[pb], r_const], writes=[r_rstd])
                S.op("dve", lambda e: e.reciprocal(out=rstd[:], in_=rstd[:]), reads=[r_rstd], writes=[r_rstd])
                for c in range(4):
                    S.op("dve", lambda e, c=c: e.scalar_tensor_tensor(out=ckvn[:, c, :], in0=ckv[:, c, :],
                                                                      scalar=vcol("kvn_g", c), in1=rstd[:],
                                                                      op0=ALU.mult, op1=ALU.mult),
                         reads=[r_ckv, r_rstd, r_const], writes=[r_ckvn])
                i_ = wcnt[0] % 2
                wcnt[0] += 1
                ld("sp", wsl[i_][:, 0:32 * 64], wkr_s[0].rearrange("p k c -> p (k c)"), [rws[i_]])
                wr = wsl[i_][:, 0:32 * 64].rearrange("p (k c) -> p k c", c=64)
                pb = next_ps()
                for kc in range(32):
                    S.op("pe", lambda e, kc=kc, pb=pb, wr=wr, xk=xk: e.matmul(ps[pb][0:64, :], wr[:, kc, :], xk[:, kc, :],
                                                                            start=(kc == 0), stop=(kc == 31)),
                         reads=[rws[i_], rxk[xi]], writes=[rps[pb]])
                rope_tables(st, (pi_t, pf_t, pa_t, r_rope), posk, t0, 512, cs_t, r_cs)
                S.op("dve", lambda e, pb=pb: e.tensor_copy(out=krf[:], in_=ps[pb][0:64, :]), reads=[rps[pb]],
                     writes=[r_kr])
                S.op("act", lambda e, pb=pb: e.copy(out=krb[:], in_=ps[pb][0:64, :]), reads=[rps[pb]], writes=[r_kr])
                pb2 = next_ps()
                S.op("pe", lambda e, pb2=pb2: e.matmul(ps[pb2][0:64, :], rot_t[:, :], krb[:, :], start=True, stop=True),
                     reads=[r_kr, r_const], writes=[rps[pb2]])
                S.op("dve", lambda e: e.tensor_tensor(out=krf[:], in0=krf[:], in1=cs_t[:, 0, :], op=ALU.mult),
                     reads=[r_kr, r_cs], writes=[r_kr])
                S.op("dve", lambda e, pb2=pb2: e.tensor_tensor(out=kro[:], in0=ps[pb2][0:64, :], in1=cs_t[:, 1, :],
                                                               op=ALU.mult), reads=[rps[pb2], r_cs], writes=[r_kr])
                k = scnt[0] % 4
                scnt[0] += 1
                S.op("dve", lambda e, k=k: e.tensor_tensor(out=stgb[k][0:64, :], in0=krf[:], in1=kro[:], op=ALU.add),
                     reads=[r_kr], writes=[rstg[k]])
                ld("pool", kTr[:, t0:t0 + 512], stgb[k][0:64, :], [], reads=[rstg[k]])
                for s in range(4):
                    wi = load_slab(wk_s[s], 16384)
                    wv_ = wsl[wi][:].rearrange("p (k c) -> p k c", c=512)
                    for c in range(4):
                        hm = s * 4 + c
                        pb = next_ps()
                        for kc in range(32):
                            S.op("pe", lambda e, wv_=wv_, kc=kc, c=c, pb=pb, xk=xk: e.matmul(
                                ps[pb][:, :], wv_[:, kc, c * 128:(c + 1) * 128], xk[:, kc, :],
                                start=(kc == 0), stop=(kc == 31)), reads=[rws[wi], rxk[xi]], writes=[rps[pb]])
                        stage_out(pb, 128, 512, kTda[hm, :, t0:t0 + 512])
                for s in range(4):
                    wi = load_slab(wv_s[s], 16384)
                    wv_ = wsl[wi][:].rearrange("p (k c) -> p k c", c=512)
                    for tb in range(4):
                        pb = next_ps()
                        for kc in range(32):
                            S.op("pe", lambda e, wv_=wv_, kc=kc, tb=tb, pb=pb, xk=xk: e.matmul(
                                ps[pb][:, :], xk[:, kc, tb * 128:(tb + 1) * 128], wv_[:, kc, :],
                                start=(kc == 0), stop=(kc == 31)), reads=[rws[wi], rxk[xi]], writes=[rps[pb]])
                        rbk = g * 4 + tb
                        stage_out(pb, 128, 512, vda[2 * s:2 * s + 2, :, rbk, :].rearrange("h p e -> p h e"),
                                  view=lambda a: a.rearrange("p (h e) -> p h e", h=2))
                i_ = wcnt[0] % 2
                wcnt[0] += 1
                ld("sp", wsl[i_][:].rearrange("p (g f) -> p g f", g=8), wukv_s.rearrange("g p k c -> p g (k c)"),
                   [rws[i_]])
                wu = wsl[i_][:].rearrange("p (g k c) -> p g k c", g=8, k=4)
                for h in range(16):
                    pb = next_ps()
                    for kc in range(4):
                        S.op("pe", lambda e, h=h, kc=kc, pb=pb, wu=wu: e.matmul(
                            ps[pb][:, :], wu[:, h // 2, kc, (h % 2) * 256:(h % 2) * 256 + 128], ckvn[:, kc, :],
                            start=(kc == 0), stop=(kc == 3)), reads=[rws[i_], r_ckvn], writes=[rps[pb]])
                    stage_out(pb, 128, 512, kTm[h, :, t0:t0 + 512])
                for tb in range(4):
                    for s8 in range(8):
                        pb = next_ps()
                        for kc in range(4):
                            S.op("pe", lambda e, s8=s8, kc=kc, pb=pb, tb=tb, wu=wu: e.matmul(
                                ps[pb][:, :], ckvn[:, kc, tb * 128:(tb + 1) * 128], wu[:, s8, kc, :],
                                start=(kc == 0), stop=(kc == 3)), reads=[rws[i_], r_ckvn], writes=[rps[pb]])
                        k = scnt[0] % 4
                        scnt[0] += 1
                        S.op("act", lambda e, k=k, pb=pb: e.copy(
                            out=stgb[k][:, 0:256].rearrange("p (h e) -> p h e", h=2),
                            in_=ps[pb][:, :].rearrange("p (h t e) -> p h t e", h=2, t=2)[:, :, 1, :]),
                             reads=[rps[pb]], writes=[rstg[k]])
                        ld("pool", vm[2 * s8:2 * s8 + 2, :, g * 4 + tb, :].rearrange("h p e -> p h e"),
                           stgb[k][:, 0:256].rearrange("p (h e) -> p h e", h=2), [], reads=[rstg[k]])
            S.flush()

        if upto == "A":
            return nc
        SC_DA = 128.0 ** -0.5
        SC_M = 192.0 ** -0.5
        with ExitStack() as st:
            xqb = [sb(st, "xqb%d" % i, [128, 32, W], BF16) for i in range(2)]
            rxq = [Res() for _ in range(2)]
            xst = [sb(st, "xsq%d" % i, [128, 8, W], F32) for i in range(2)]
            rxst = [Res() for _ in range(2)]
            wsl = [sb(st, "wsq%d" % i, [128, 16384], BF16) for i in range(2)]
            rws = [Res() for _ in range(2)]
            wcnt = [0]
            stgb = [sb(st, "stq%d" % i, [128, W], BF16) for i in range(4)]
            rstg = [Res() for _ in range(4)]
            scnt = [0]
            cq = sb(st, "cq", [128, 8, W], F32)
            sq = sb(st, "sqq", [128, 8, W], BF16)
            rstd = sb(st, "rstdq", [128, W], F32)
            cqn = sb(st, "cqn", [128, 8, W], BF16)
            r_cq, r_sq, r_rstd, r_cqn = Res(), Res(), Res(), Res()
            pi_t = sb(st, "pi_q", [64, W], I32)
            pf_t = sb(st, "pf_q", [64, W], F32)
            pa_t = sb(st, "pa_q", [64, W], F32)
            cs_t = sb(st, "cs_q", [64, 2, W], F32)
            qrf = sb(st, "qrf", [64, W], F32)
            qrb = sb(st, "qrb", [64, W], BF16)
            qro = sb(st, "qro", [64, W], F32)
            r_rope, r_cs, r_qr = Res(), Res(), Res()

            def stage_out_q(psb, M, dst, scale):
                k = scnt[0] % 4
                scnt[0] += 1
                S.op("act", lambda e: e.activation(out=stgb[k][0:M, :], in_=ps[psb][0:M, 0:W], func=AF.Copy,
                                                   scale=scale), reads=[rps[psb]], writes=[rstg[k]])
                ld("pool", dst, stgb[k][0:M, :], [], reads=[rstg[k]])

            for i in range(NT):
                xi = i % 2
                t0 = i * W
                for q4 in range(4):
                    si = (i * 4 + q4) % 2
                    ld("sp", xst[si][:], xqT[q4 * 1024:(q4 + 1) * 1024, t0:t0 + W].rearrange("(k p) n -> p k n", p=128),
                       [rxst[si]])
                    eng = "dve" if q4 % 2 == 0 else "pool"
                    S.op(eng, lambda e, si=si, q4=q4, xi=xi: e.tensor_copy(out=xqb[xi][:, q4 * 8:(q4 + 1) * 8, :],
                                                                          in_=xst[si][:]),
                         reads=[rxst[si]], writes=[rxq[xi]])
                xq = xqb[xi]
                rope_tables(st, (pi_t, pf_t, pa_t, r_rope), posq, t0, W, cs_t, r_cs)
                for s in range(4):
                    wi = wcnt[0] % 2
                    wcnt[0] += 1
                    ld("sp", wsl[wi][:], wq_s[s].rearrange("p k c -> p (k c)"), [rws[wi]])
                    wv_ = wsl[wi][:].rearrange("p (k c) -> p k c", c=512)
                    for c in range(4):
                        hm = s * 4 + c
                        pb = next_ps()
                        for kc in range(32):
                            S.op("pe", lambda e, wv_=wv_, kc=kc, c=c, pb=pb, xq=xq: e.matmul(
                                ps[pb][:, 0:W], wv_[:, kc, c * 128:(c + 1) * 128], xq[:, kc, :],
                                start=(kc == 0), stop=(kc == 31)), reads=[rws[wi], rxq[xi]], writes=[rps[pb]])
                        stage_out_q(pb, 128, qTda[hm, :, t0:t0 + W], SC_DA)
                for s in range(2):
                    wi = wcnt[0] % 2
                    wcnt[0] += 1
                    ld("sp", wsl[wi][:], wcq_s[s].rearrange("p k c -> p (k c)"), [rws[wi]])
                    wv_ = wsl[wi][:].rearrange("p (k c) -> p k c", c=512)
                    for c in range(4):
                        cc = s * 4 + c
                        pb = next_ps()
                        for kc in range(32):
                            S.op("pe", lambda e, wv_=wv_, kc=kc, c=c, pb=pb, xq=xq: e.matmul(
                                ps[pb][:, 0:W], wv_[:, kc, c * 128:(c + 1) * 128], xq[:, kc, :],
                                start=(kc == 0), stop=(kc == 31)), reads=[rws[wi], rxq[xi]], writes=[rps[pb]])
                        S.op("dve", lambda e, cc=cc, pb=pb: e.tensor_copy(out=cq[:, cc, :], in_=ps[pb][:, 0:W]),
                             reads=[rps[pb]], writes=[r_cq])
                        S.op("act", lambda e, cc=cc, pb=pb: e.activation(out=sq[:, cc, :], in_=ps[pb][:, 0:W],
                                                                         func=AF.Square),
                             reads=[rps[pb]], writes=[r_sq])
                pb = next_ps()
                for c in range(8):
                    S.op("pe", lambda e, c=c, pb=pb: e.matmul(ps[pb][:, 0:W], ones_b[:, :], sq[:, c, :], start=(c == 0),
                                                             stop=(c == 7)), reads=[r_sq, r_const], writes=[rps[pb]])
                S.op("act", lambda e, pb=pb: e.activation(out=rstd[:], in_=ps[pb][:, 0:W], func=AF.Sqrt,
                                                          bias=eps_t[:, 0:1], scale=1.0 / 1024.0),
                     reads=[rps[pb], r_const], writes=[r_rstd])
                S.op("dve", lambda e: e.reciprocal(out=rstd[:], in_=rstd[:]), reads=[r_rstd], writes=[r_rstd])
                for c in range(8):
                    S.op("dve", lambda e, c=c: e.scalar_tensor_tensor(out=cqn[:, c, :], in0=cq[:, c, :],
                                                                      scalar=vcol("qn_g", c), in1=rstd[:],
                                                                      op0=ALU.mult, op1=ALU.mult),
                         reads=[r_cq, r_rstd, r_const], writes=[r_cqn])
                for half in range(2):
                    wi = wcnt[0] % 2
                    wcnt[0] += 1
                    ld("sp", wsl[wi][:, 0:8 * 1536].rearrange("p (g f) -> p g f", g=8),
                       wuq_s[half * 8:half * 8 + 8].rearrange("g p k c -> p g (k c)"), [rws[wi]])
                    wu = wsl[wi][:, 0:8 * 1536].rearrange("p (g k c) -> p g k c", g=8, k=8)
                    for hh in range(8):
                        h = half * 8 + hh
                        pb = next_ps()
                        for kc in range(8):
                            S.op("pe", lambda e, hh=hh, kc=kc, pb=pb, wu=wu: e.matmul(
                                ps[pb][:, 0:W], wu[:, hh, kc, 0:128], cqn[:, kc, :], start=(kc == 0), stop=(kc == 7)),
                                 reads=[rws[wi], r_cqn], writes=[rps[pb]])
                        stage_out_q(pb, 128, qTm[h, :, t0:t0 + W], SC_M)
                        pb = next_ps()
                        for kc in range(8):
                            S.op("pe", lambda e, hh=hh, kc=kc, pb=pb, wu=wu: e.matmul(
                                ps[pb][0:64, 0:W], wu[:, hh, kc, 128:192], cqn[:, kc, :], start=(kc == 0),
                                stop=(kc == 7)), reads=[rws[wi], r_cqn], writes=[rps[pb]])
                        S.op("dve", lambda e, pb=pb: e.tensor_copy(out=qrf[:], in_=ps[pb][0:64, 0:W]), reads=[rps[pb]],
                             writes=[r_qr])
                        S.op("act", lambda e, pb=pb: e.copy(out=qrb[:], in_=ps[pb][0:64, 0:W]), reads=[rps[pb]],
                             writes=[r_qr])
                        pb2 = next_ps()
                        S.op("pe", lambda e, pb2=pb2: e.matmul(ps[pb2][0:64, 0:W], rot_t[:, :], qrb[:, :], start=True,
                                                               stop=True), reads=[r_qr, r_const], writes=[rps[pb2]])
                        S.op("dve", lambda e: e.scalar_tensor_tensor(out=qrf[:], in0=qrf[:], scalar=SC_M,
                                                                     in1=cs_t[:, 0, :], op0=ALU.mult, op1=ALU.mult),
                             reads=[r_qr, r_cs], writes=[r_qr])
                        S.op("dve", lambda e, pb2=pb2: e.scalar_tensor_tensor(out=qro[:], in0=ps[pb2][0:64, 0:W],
                                                                              scalar=SC_M, in1=cs_t[:, 1, :],
                                                                              op0=ALU.mult, op1=ALU.mult),
                             reads=[rps[pb2], r_cs], writes=[r_qr])
                        k = scnt[0] % 4
                        scnt[0] += 1
                        S.op("dve", lambda e, k=k: e.tensor_tensor(out=stgb[k][0:64, :], in0=qrf[:], in1=qro[:],
                                                                   op=ALU.add), reads=[r_qr], writes=[rstg[k]])
                        ld("pool", qTr[h, :, t0:t0 + W], stgb[k][0:64, :], [], reads=[rstg[k]])
            S.flush()

        if upto == "B":
            return nc
        with ExitStack() as st:
            kbuf = [sb(st, "kbuf%d" % i, [128, NKV], BF16) for i in range(2)]
            rk = [Res() for _ in range(2)]
            vbuf = [sb(st, "vbuf%d" % i, [128, NB * 256], BF16) for i in range(2)]
            rv = [Res() for _ in range(2)]
            krb_t = sb(st, "krbuf", [64, NKV], BF16)
            r_kr = Res()
            qbuf = [sb(st, "qbuf%d" % i, [128, NQ], BF16) for i in range(2)]
            rq = [Res() for _ in range(2)]
            qrbuf = [sb(st, "qrbuf%d" % i, [64, NQ], BF16) for i in range(2)]
            alqb = [sb(st, "alqb%d" % i, [3, NQ], BF16) for i in range(2)]
            alk_t = sb(st, "alk_t", [3, 8, 128], BF16)
            mask_t = sb(st, "mask_t", [128, NMASK, W], BF16)
            NPT = 6
            pt = [sb(st, "pt%d" % i, [128, W], BF16) for i in range(NPT)]
            rpt = [Res() for _ in range(NPT)]
            stmp = [sb(st, "stmp%d" % i, [128, W], F32) for i in range(2)]
            rstmp = [Res() for _ in range(2)]
            stcnt = [0]
            pcnt = [0]
            rden = sb(st, "rden", [128, W], F32)
            o1 = sb(st, "o1", [128, 2, W], F32)
            od = sb(st, "od", [128, 2, W], F32)
            osq = sb(st, "osq", [128, 2, W], BF16)
            orst = sb(st, "orst", [128, W], F32)
            ostg = [sb(st, "ostg%d" % i, [128, 2, W], BF16) for i in range(2)]
            rostg = [Res() for _ in range(2)]
            ocnt = [0]
            r_rden, r_o1, r_od, r_osq, r_orst, r_catt = Res(), Res(), Res(), Res(), Res(), Res()
            kval_t = sb(st, "kval_t", [128, NB], F32)
            abias_t = sb(st, "abias_t", [128, 8, NT, NB], F32)
            ld("sp", kval_t[:], kvalid, [r_catt])
            ld("sp", abias_t[:], abias, [r_catt])
            ld("sp", alk_t[:], alk, [r_catt])
            ld("sp", mask_t[:], masks_d, [r_catt])
            ld("sp", krb_t[:], kTr, [r_kr])
            bg = late_weights(make_converter(st, 2, 2, ("pool",), "c")) if BG_CONVERT else iter(())

            def bg_step(n):
                for _ in range(n):
                    if next(bg, "done") == "done":
                        return
            S_BANKS = (0, 1, 2, 3)
            O_SETS = ((4, 5, 6), (4, 5, 6))
            LN_BANK = 7
            SK = 3
            scnt = [0]
            setcnt = [0]
            kcnt = [0]
            vcnt = [0]
            qcnt = [0]

            def sweep(i, kT, rkres, qT, rqres, extra, V, rvres, ne, esz, biasfn, oset):
                blocks = tile_blocks(i)
                nblk = len(blocks)
                pks = {}
                for bi in range(nblk + SK):
                    if bi < nblk:
                        rb, full = blocks[bi]
                        sbk = S_BANKS[scnt[0] % len(S_BANKS)]
                        scnt[0] += 1
                        S.op("pe", lambda e, sbk=sbk, rb=rb: e.matmul(ps[sbk][:, 0:W], kT[:, rb * 128:(rb + 1) * 128],
                                                                       qT[:, i * W:(i + 1) * W], start=True,
                                                                       stop=(extra is None)),
                             reads=[rkres, rqres], writes=[rps[sbk]])
                        if extra is not None:
                            S.op("pe", lambda e, sbk=sbk, rb=rb: e.matmul(ps[sbk][:, 0:W], extra[0](rb),
                                                                           extra[1][:, i * W:(i + 1) * W],
                                                                           start=False, stop=True),
                                 reads=list(extra[2]), writes=[rps[sbk]])
                        pk = pcnt[0] % NPT
                        pcnt[0] += 1
                        pks[bi] = pk
                        if full:
                            S.op("act", lambda e, sbk=sbk, rb=rb, pk=pk: e.activation(
                                out=pt[pk][:], in_=ps[sbk][:, 0:W], func=AF.Exp, bias=biasfn(rb), scale=1.0),
                                 reads=[rps[sbk], r_catt], writes=[rpt[pk]])
                        else:
                            mi = MASK_IDX[(i, rb)]
                            tk = stcnt[0] % 2
                            stcnt[0] += 1
                            S.op("dve", lambda e, sbk=sbk, mi=mi, tk=tk: e.tensor_tensor(
                                out=stmp[tk][:], in0=ps[sbk][:, 0:W], in1=mask_t[:, mi, :], op=ALU.add),
                                 reads=[rps[sbk], r_catt], writes=[rstmp[tk]])
                            S.op("act", lambda e, tk=tk, rb=rb, pk=pk: e.activation(
                                out=pt[pk][:], in_=stmp[tk][:], func=AF.Exp, bias=biasfn(rb), scale=1.0),
                                 reads=[rstmp[tk], r_catt], writes=[rpt[pk]])
                    if bi % BG_EVERY == BG_EVERY - 1:
                        bg_step(1)
                    bj = bi - SK
                    if bj >= 0:
                        rb, full = blocks[bj]
                        pk = pks[bj]
                        for e_ in range(ne):
                            S.op("pe", lambda e, e_=e_, rb=rb, pk=pk, bj=bj: e.matmul(
                                ps[oset[e_]][:, 0:W], V[:, rb * esz + e_ * 128: rb * esz + (e_ + 1) * 128], pt[pk][:],
                                start=(bj == 0), stop=(bj == nblk - 1)), reads=[rvres, rpt[pk]],
                                 writes=[rps[oset[e_]]])
                        S.op("pe", lambda e, pk=pk, bj=bj: e.matmul(ps[oset[2]][:, 0:W], ones_b[:, :], pt[pk][:],
                                                                    start=(bj == 0), stop=(bj == nblk - 1)),
                             reads=[rpt[pk], r_const], writes=[rps[oset[2]]])
                S.op("dve", lambda e: e.tensor_scalar(out=rden[:], in0=ps[oset[2]][:, 0:W], scalar1=1e-30, scalar2=None,
                                                      op0=ALU.add), reads=[rps[oset[2]]], writes=[r_rden])
                S.op("dve", lambda e: e.reciprocal(out=rden[:], in_=rden[:]), reads=[r_rden], writes=[r_rden])

            for h in range(8):
                vi = vcnt[0] % 2
                vcnt[0] += 1
                ld("sp", vbuf[vi][:], vda[h].rearrange("p b e -> p (b e)"), [rv[vi]])
                for m in range(2):
                    hm = 2 * h + m
                    ki = kcnt[0] % 2
                    kcnt[0] += 1
                    ld("sp", kbuf[ki][:], kTda[hm], [rk[ki]])
                    qi = qcnt[0] % 2
                    qcnt[0] += 1
                    ld("sp", qbuf[qi][:], qTda[hm], [rq[qi]])
                    ld("sp", alqb[qi][:], alq[h], [rq[qi]])
                    for i in range(NT):
                        oset = O_SETS[setcnt[0] % 2]
                        setcnt[0] += 1
                        sweep(i, kbuf[ki], rk[ki], qbuf[qi], rq[qi],
                              ((lambda rb, h=h: alk_t[:, h, :], alqb[qi], (r_catt, rq[qi]))
                               if h < ALIBI_MM_HEADS else None),
                              vbuf[vi], rv[vi], 2, 256, (lambda rb, h=h, i=i: abias_t[:, h, i, rb:rb + 1]), oset)
                        if m == 0:
                            for e_ in range(2):
                                S.op("dve", lambda e, e_=e_, i=i, oset=oset: e.tensor_tensor(
                                    out=o1[:, e_, :], in0=ps[oset[e_]][:, 0:W], in1=rden[:], op=ALU.mult),
                                     reads=[rps[oset[e_]], r_rden], writes=[r_o1])
                        else:
                            pass
                        if m == 0:
                            ld("pool", yscr[0:2, :, i * W:(i + 1) * W].rearrange("c p n -> p c n"), o1[:], [],
                               reads=[r_o1])
                        else:
                            ld("sp", o1[:], yscr[0:2, :, i * W:(i + 1) * W].rearrange("c p n -> p c n"), [r_o1])
                            for e_ in range(2):
                                S.op("dve", lambda e, e_=e_, oset=oset: e.tensor_tensor(
                                    out=od[:, e_, :], in0=ps[oset[e_]][:, 0:W], in1=rden[:], op=ALU.mult),
                                     reads=[rps[oset[e_]], r_rden], writes=[r_od])
                                S.op("dve", lambda e, e_=e_: e.scalar_tensor_tensor(
                                    out=od[:, e_, :], in0=od[:, e_, :], scalar=nlam_t[:, 0:1], in1=o1[:, e_, :],
                                    op0=ALU.mult, op1=ALU.add), reads=[r_od, r_o1, r_const], writes=[r_od])
                                S.op("act", lambda e, e_=e_: e.activation(out=osq[:, e_, :], in_=od[:, e_, :],
                                                                          func=AF.Square), reads=[r_od], writes=[r_osq])
                            sbk = LN_BANK
                            for e_ in range(2):
                                S.op("pe", lambda e, e_=e_, sbk=sbk: e.matmul(ps[sbk][:, 0:W], ones_b[:, :], osq[:, e_, :],
                                                                             start=(e_ == 0), stop=(e_ == 1)),
                                     reads=[r_osq, r_const], writes=[rps[sbk]])
                            S.op("act", lambda e, sbk=sbk: e.activation(out=orst[:], in_=ps[sbk][:, 0:W], func=AF.Sqrt,
                                                                        bias=eps_t[:, 0:1], scale=1.0 / 256.0),
                                 reads=[rps[sbk], r_const], writes=[r_orst])
                            S.op("dve", lambda e: e.reciprocal(out=orst[:], in_=orst[:]), reads=[r_orst],
                                 writes=[r_orst])
                            ok = ocnt[0] % 2
                            ocnt[0] += 1
                            for e_ in range(2):
                                S.op("dve", lambda e, e_=e_, ok=ok: e.scalar_tensor_tensor(
                                    out=ostg[ok][:, e_, :], in0=od[:, e_, :], scalar=sgs_t[:, e_:e_ + 1], in1=orst[:],
                                    op0=ALU.mult, op1=ALU.mult), reads=[r_od, r_orst, r_const], writes=[rostg[ok]])
                            ld("pool", oaT[2 * h:2 * h + 2, :, i * W:(i + 1) * W].rearrange("c p n -> p c n"),
                               ostg[ok][:], [], reads=[rostg[ok]])
            for h in range(16):
                vi = vcnt[0] % 2
                vcnt[0] += 1
                ld("sp", vbuf[vi][:, 0:NB * 128], vm[h].rearrange("p b e -> p (b e)"), [rv[vi]])
                ki = kcnt[0] % 2
                kcnt[0] += 1
                ld("sp", kbuf[ki][:], kTm[h], [rk[ki]])
                qi = qcnt[0] % 2
                qcnt[0] += 1
                ld("sp", qbuf[qi][:], qTm[h], [rq[qi]])
                ld("sp", qrbuf[qi][:], qTr[h], [rq[qi]])
                for i in range(NT):
                    oset = O_SETS[setcnt[0] % 2]
                    setcnt[0] += 1
                    oset2 = (oset[0], oset[1], oset[2])
                    sweep(i, kbuf[ki], rk[ki], qbuf[qi], rq[qi],
                          (lambda rb: krb_t[:, rb * 128:(rb + 1) * 128], qrbuf[qi], (r_kr, rq[qi])),
                          vbuf[vi], rv[vi], 1, 128, (lambda rb: kval_t[:, rb:rb + 1]), oset2)
                    ok = ocnt[0] % 2
                    ocnt[0] += 1
                    S.op("dve", lambda e, ok=ok, oset=oset: e.tensor_tensor(out=ostg[ok][:, 0, :], in0=ps[oset[0]][:, 0:W],
                                                                            in1=rden[:], op=ALU.mult),
                         reads=[rps[oset[0]], r_rden], writes=[rostg[ok]])
                    ld("pool", obT[h, :, i * W:(i + 1) * W], ostg[ok][:, 0, :], [], reads=[rostg[ok]])
            bg_step(1 << 30)
            S.flush()

        if upto == "C":
            return nc
        with ExitStack() as st:
            oab = sb(st, "oab", [128, 16, W], BF16)
            obb = sb(st, "obb", [128, 16, W], BF16)
            xqb = sb(st, "xqd", [128, 32, W], BF16)
            xst = [sb(st, "xsd%d" % i, [128, 4, W], F32) for i in range(2)]
            rxst = [Res() for _ in range(2)]
            r_oa, r_ob, r_xq = Res(), Res(), Res()
            wsl = [sb(st, "wsd%d" % i, [128, 8192], BF16) for i in range(4)]
            rws = [Res() for _ in range(4)]
            wcnt = [0]
            mg = sb(st, "mg", [128, 32, W], BF16)
            r_mg = Res()
            gS = [sb(st, "gS%d" % i, [128, 2, W], F32) for i in range(1)]
            tS = [sb(st, "tS%d" % i, [128, 2, W], F32) for i in range(1)]
            gS2 = [sb(st, "gSb%d" % i, [128, 2, W], F32) for i in range(1)]
            uS = [sb(st, "uS%d" % i, [128, 2, W], F32) for i in range(1)]
            rgS = [Res() for _ in range(2)]
            rtS = [Res() for _ in range(2)]
            rgS2 = [Res() for _ in range(2)]
            ruS = [Res() for _ in range(2)]
            xres = [sb(st, "xres%d" % i, [128, W], F32) for i in range(3)]
            rxres = [Res() for _ in range(3)]
            ych = [sb(st, "ych%d" % i, [128, W], F32) for i in range(3)]
            rych = [Res() for _ in range(3)]
            ybf = [sb(st, "ybf%d" % i, [128, 2, W], BF16) for i in range(3)]
            rybf = [Res() for _ in range(3)]
            mean = sb(st, "mean", [128, W], F32)
            rstd = sb(st, "rstdd", [128, W], F32)
            msq = sb(st, "msq", [128, W], F32)
            r_stat = Res()
            hch = [sb(st, "hch%d" % i, [128, W], F32) for i in range(2)]
            hcb = [sb(st, "hcb%d" % i, [128, W], BF16) for i in range(2)]
            rhch = [Res() for _ in range(2)]
            r_y = [[Res() for _ in range(32)] for _ in range(NT)]
            PS_M1, PS_M2 = 6, 7

            def dps():
                b = psn[0] % 6
                psn[0] += 1
                return b

            def slab(scr_g, n):
                wi = wcnt[0] % 4
                wcnt[0] += 1
                ld("sp", wsl[wi][:, 0:n], scr_g.rearrange("p k c -> p (k c)"), [rws[wi]])
                return wi, wsl[wi][:, 0:n].rearrange("p (k c) -> p k c", c=256)

            def mm_chunk(wi, wv_, c, KC, rhs, rrhs):
                pb = dps()
                for kc in range(KC):
                    S.op("pe", lambda e, kc=kc: e.matmul(ps[pb][:, 0:W], wv_[:, kc, c * 128:(c + 1) * 128],
                                                         rhs[:, kc, :], start=(kc == 0), stop=(kc == KC - 1)),
                         reads=[rws[wi], rrhs], writes=[rps[pb]])
                return pb

            for i in range(NT):
                t0 = i * W
                ld("sp", oab[:], oaT[:, :, t0:t0 + W].rearrange("c p n -> p c n"), [r_oa])
                ld("sp", obb[:], obT[:, :, t0:t0 + W].rearrange("c p n -> p c n"), [r_ob])
                for q8 in range(8):
                    si = q8 % 2
                    ld("sp", xst[si][:], xqT[q8 * 512:(q8 + 1) * 512, t0:t0 + W].rearrange("(k p) n -> p k n", p=128),
                       [rxst[si]])
                    eng = "dve" if q8 % 2 == 0 else "pool"
                    S.op(eng, lambda e, si=si, q8=q8: e.tensor_copy(out=xqb[:, q8 * 4:(q8 + 1) * 4, :], in_=xst[si][:]),
                         reads=[rxst[si]], writes=[r_xq])
                for s in range(16):
                    p2 = 0
                    wi, wv_ = slab(wg_s[s], 32 * 256)
                    for c in range(2):
                        mc = 2 * s + c
                        pb = mm_chunk(wi, wv_, c, 32, xqb, r_xq)
                        S.op("act", lambda e, pb=pb, mc=mc, c=c, p2=p2: e.activation(
                            out=gS[p2][:, c, :], in_=ps[pb][:, 0:W], func=AF.Sigmoid, bias=vcol("b_gate", mc),
                            scale=1.0), reads=[rps[pb], r_const], writes=[rgS[p2]])
                    wi, wv_ = slab(wpa_s[s], 16 * 256)
                    for c in range(2):
                        pb = mm_chunk(wi, wv_, c, 16, oab, r_oa)
                        S.op("dve", lambda e, pb=pb, c=c, p2=p2: e.tensor_tensor(
                            out=tS[p2][:, c, :], in0=gS[p2][:, c, :], in1=ps[pb][:, 0:W], op=ALU.mult),
                             reads=[rps[pb], rgS[p2]], writes=[rtS[p2]])
                    wi, wv_ = slab(wg_s[16 + s], 32 * 256)
                    for c in range(2):
                        mc = 2 * s + c
                        pb = mm_chunk(wi, wv_, c, 32, xqb, r_xq)
                        S.op("act", lambda e, pb=pb, mc=mc, c=c, p2=p2: e.activation(
                            out=gS2[p2][:, c, :], in_=ps[pb][:, 0:W], func=AF.Sigmoid, bias=vcol("b_gate", 32 + mc),
                            scale=1.0), reads=[rps[pb], r_const], writes=[rgS2[p2]])
                    wi, wv_ = slab(wpb_s[s], 16 * 256)
                    for c in range(2):
                        mc = 2 * s + c
                        pb = mm_chunk(wi, wv_, c, 16, obb, r_ob)
                        S.op("dve", lambda e, pb=pb, c=c, p2=p2: e.tensor_tensor(
                            out=uS[p2][:, c, :], in0=gS2[p2][:, c, :], in1=ps[pb][:, 0:W], op=ALU.mult),
                             reads=[rps[pb], rgS2[p2]], writes=[ruS[p2]])
                        S.op("pool", lambda e, c=c, p2=p2, mc=mc: e.tensor_tensor(
                            out=mg[:, mc, :], in0=tS[p2][:, c, :], in1=uS[p2][:, c, :], op=ALU.add),
                             reads=[rtS[p2], ruS[p2]], writes=[r_mg])
                for s in range(16):
                    wi, wv_ = slab(wout_s[s], 32 * 256)
                    for c in range(2):
                        mc = 2 * s + c
                        po = mm_chunk(wi, wv_, c, 32, mg, r_mg)
                        xi = mc % 3
                        ld("sp", xres[xi][:], xqT[mc * 128:(mc + 1) * 128, t0:t0 + W], [rxres[xi]])
                        S.op("dve", lambda e, xi=xi, po=po: e.scalar_tensor_tensor(
                            out=ych[xi][:], in0=xres[xi][:], scalar=ALPHA, in1=ps[po][:, 0:W], op0=ALU.mult,
                            op1=ALU.add), reads=[rxres[xi], rps[po]], writes=[rych[xi]])
                        S.op("act", lambda e, xi=xi: e.copy(out=ybf[xi][:, 0, :], in_=ych[xi][:]), reads=[rych[xi]],
                             writes=[rybf[xi]])
                        S.op("act", lambda e, xi=xi: e.activation(out=ybf[xi][:, 1, :], in_=ych[xi][:], func=AF.Square),
                             reads=[rych[xi]], writes=[rybf[xi]])
                        S.op("pe", lambda e, xi=xi, mc=mc: e.matmul(ps[PS_M1][:, 0:W], ones_b[:, :], ybf[xi][:, 0, :],
                                                                    start=(mc == 0), stop=(mc == 31)),
                             reads=[rybf[xi], r_const], writes=[rps[PS_M1]])
                        S.op("pe", lambda e, xi=xi, mc=mc: e.matmul(ps[PS_M2][:, 0:W], ones_b[:, :], ybf[xi][:, 1, :],
                                                                    start=(mc == 0), stop=(mc == 31)),
                             reads=[rybf[xi], r_const], writes=[rps[PS_M2]])
                        ld("pool", yscr[mc, :, t0:t0 + W], ych[xi][:], [r_y[i][mc]], reads=[rych[xi]])
                S.op("dve", lambda e: e.tensor_scalar(out=mean[:], in0=ps[PS_M1][:, 0:W], scalar1=1.0 / D, scalar2=None,
                                                      op0=ALU.mult), reads=[rps[PS_M1]], writes=[r_stat])
                S.op("dve", lambda e: e.tensor_tensor(out=msq[:], in0=mean[:], in1=mean[:], op=ALU.mult),
                     reads=[r_stat], writes=[r_stat])
                S.op("dve", lambda e: e.scalar_tensor_tensor(out=rstd[:], in0=ps[PS_M2][:, 0:W], scalar=1.0 / D,
                                                             in1=msq[:], op0=ALU.mult, op1=ALU.subtract),
                     reads=[rps[PS_M2], r_stat], writes=[r_stat])
                S.op("act", lambda e: e.activation(out=rstd[:], in_=rstd[:], func=AF.Sqrt, bias=eps_t[:, 1:2],
                                                   scale=1.0), reads=[r_stat, r_const], writes=[r_stat])
                S.op("dve", lambda e: e.reciprocal(out=rstd[:], in_=rstd[:]), reads=[r_stat], writes=[r_stat])
                for mc in range(32):
                    xi = mc % 3
                    hi = mc % 2
                    ld("sp", ych[xi][:], yscr[mc, :, t0:t0 + W], [rych[xi]], reads=[r_y[i][mc]])
                    S.op("dve", lambda e, xi=xi: e.tensor_tensor(out=ych[xi][:], in0=ych[xi][:], in1=mean[:],
                                                                 op=ALU.subtract), reads=[rych[xi], r_stat],
                         writes=[rych[xi]])
                    S.op("pool", lambda e, xi=xi: e.tensor_tensor(out=ych[xi][:], in0=ych[xi][:], in1=rstd[:],
                                                                  op=ALU.mult), reads=[rych[xi], r_stat],
                         writes=[rych[xi]])
                    S.op("dve", lambda e, xi=xi, hi=hi, mc=mc: e.tensor_scalar(
                        out=hch[hi][:], in0=ych[xi][:], scalar1=vcol("ln1_g", mc), scalar2=vcol("ln1_b", mc),
                        op0=ALU.mult, op1=ALU.add), reads=[rych[xi], r_const], writes=[rhch[hi]])
                    S.op("act", lambda e, hi=hi: e.copy(out=hcb[hi][:], in_=hch[hi][:]), reads=[rhch[hi]],
                         writes=[rhch[hi]])
                    ld("pool", h1f[mc, :, t0:t0 + W], hch[hi][:], [], reads=[rhch[hi]])
                    ld("pool", h1b[mc, :, t0:t0 + W], hcb[hi][:], [], reads=[rhch[hi]])
            S.flush()

        if upto == "D":
            return nc
        with ExitStack() as st:
            hb = sb(st, "hb", [128, 32, W], BF16)
            r_hb = Res()
            wsl = [sb(st, "wse%d" % i, [128, 8192], BF16) for i in range(4)]
            rws = [Res() for _ in range(4)]
            wcnt = [0]
            abuf = sb(st, "abuf", [128, 86, W], BF16)
            r_a = Res()
            ucar = sb(st, "ucar", [128, 172, 2], F32)
            r_ucar = Res()
            ue = [sb(st, "ue%d" % i, [128, W + 2], F32) for i in range(4)]
            rue = [Res() for _ in range(4)]
            cgS = [sb(st, "cgS%d" % i, [128, 2, W], F32) for i in range(2)]
            rcgS = [Res() for _ in range(2)]
            cv = [sb(st, "cvv%d" % i, [128, W], F32) for i in range(2)]
            rcv = [Res() for _ in range(2)]
            hres = [sb(st, "hres%d" % i, [128, W], F32) for i in range(3)]
            rhres = [Res() for _ in range(3)]
            zch = [sb(st, "zch%d" % i, [128, W], F32) for i in range(3)]
            rzch = [Res() for _ in range(3)]
            zbf = [sb(st, "zbf%d" % i, [128, 2, W], BF16) for i in range(3)]
            rzbf = [Res() for _ in range(3)]
            mean = sb(st, "meane", [128, W], F32)
            rstd = sb(st, "rstde", [128, W], F32)
            msq = sb(st, "msqe", [128, W], F32)
            r_stat = Res()
            och = [sb(st, "och%d" % i, [128, W], F32) for i in range(2)]
            roch = [Res() for _ in range(2)]
            r_z = [[Res() for _ in range(32)] for _ in range(NT)]
            PS_M1, PS_M2 = 6, 7
            S.op("dve", lambda e: e.memset(ucar[:], 0.0), writes=[r_ucar])
            uecnt = [0]

            def dps():
                b = psn[0] % 6
                psn[0] += 1
                return b

            def slab(src2d, n, cw):
                wi = wcnt[0] % 4
                wcnt[0] += 1
                ld("sp", wsl[wi][:, 0:n], src2d, [rws[wi]])
                return wi, wsl[wi][:, 0:n].rearrange("p (k c) -> p k c", c=cw)

            def conv_chunk(pb, ch, out_ap, rout):
                k = uecnt[0] % 4
                uecnt[0] += 1
                S.op("pool", lambda e: e.tensor_copy(out=ue[k][:, 0:2], in_=ucar[:, ch, :]),
                     reads=[r_ucar], writes=[rue[k]])
                S.op("act", lambda e: e.copy(out=ue[k][:, 2:W + 2], in_=ps[pb][:, 0:W]), reads=[rps[pb]],
                     writes=[rue[k]])
                S.op("pool", lambda e: e.tensor_copy(out=ucar[:, ch, :], in_=ue[k][:, W:W + 2]),
                     reads=[rue[k]], writes=[r_ucar])
                S.op("dve", lambda e: e.tensor_scalar(out=out_ap, in0=ue[k][:, 2:W + 2],
                                                      scalar1=vcol("cw2", ch), scalar2=vcol("cb", ch),
                                                      op0=ALU.mult, op1=ALU.add),
                     reads=[rue[k], r_const], writes=[rout])
                S.op("dve", lambda e: e.scalar_tensor_tensor(out=out_ap, in0=ue[k][:, 1:W + 1],
                                                             scalar=vcol("cw1", ch), in1=out_ap,
                                                             op0=ALU.mult, op1=ALU.add),
                     reads=[rue[k], rout, r_const], writes=[rout])
                S.op("dve", lambda e: e.scalar_tensor_tensor(out=out_ap, in0=ue[k][:, 0:W],
                                                              scalar=vcol("cw0", ch), in1=out_ap,
                                                              op0=ALU.mult, op1=ALU.add),
                     reads=[rue[k], rout, r_const], writes=[rout])

            def up_chunk(wi, wv_, c):
                pb = dps()
                for kc in range(32):
                    S.op("pe", lambda e, kc=kc: e.matmul(ps[pb][:, 0:W], wv_[:, kc, c * 128:(c + 1) * 128],
                                                         hb[:, kc, :], start=(kc == 0), stop=(kc == 31)),
                         reads=[rws[wi], r_hb], writes=[rps[pb]])
                return pb

            for i in range(NT):
                t0 = i * W
                ld("sp", hb[:], h1b[:, :, t0:t0 + W].rearrange("c p n -> p c n"), [r_hb])
                if i == 0:
                    S.op("dve", lambda e: e.tensor_scalar(out=hb[:, :, 0:2], in0=hb[:, :, 0:2], scalar1=hv_t[:, 0:1],
                                                          scalar2=None, op0=ALU.mult), reads=[r_hb, r_const],
                         writes=[r_hb])
                for s in range(43):
                    p2 = s % 2
                    wi, wv_ = slab(wupg_s[s].rearrange("p k c -> p (k c)"), 32 * 256, 256)
                    for c in range(2):
                        fc = 2 * s + c
                        pb = up_chunk(wi, wv_, c)
                        conv_chunk(pb, fc, cgS[p2][:, c, :], rcgS[p2])
                        S.op("act", lambda e, c=c, p2=p2: e.activation(out=cgS[p2][:, c, :], in_=cgS[p2][:, c, :],
                                                                       func=AF.Silu), reads=[rcgS[p2]],
                             writes=[rcgS[p2]])
                    wi, wv_ = slab(wupv_s[s].rearrange("p k c -> p (k c)"), 32 * 256, 256)
                    for c in range(2):
                        fc = 2 * s + c
                        pb = up_chunk(wi, wv_, c)
                        ci = fc % 2
                        conv_chunk(pb, 86 + fc, cv[ci][:], rcv[ci])
                        S.op("dve", lambda e, c=c, p2=p2, ci=ci, fc=fc: e.tensor_tensor(
                            out=abuf[:, fc, :], in0=cgS[p2][:, c, :], in1=cv[ci][:], op=ALU.mult),
                             reads=[rcgS[p2], rcv[ci]], writes=[r_a])
                if adbg is not None and i == 1:
                    ld("pool", adbg.rearrange("c p n -> p c n"), abuf[:], [], reads=[r_a])
                for mc in range(32):
                    po = dps()
                    for half in range(2):
                        wi, vd = slab(wdn_s[mc, :, half * 43:(half + 1) * 43, :].rearrange("p k c -> p (k c)"),
                                      43 * 128, 128)
                        for kk in range(43):
                            kc = half * 43 + kk
                            S.op("pe", lambda e, kk=kk, kc=kc, vd=vd, po=po: e.matmul(
                                ps[po][:, 0:W], vd[:, kk, :], abuf[:, kc, :], start=(kc == 0), stop=(kc == 85)),
                                 reads=[rws[wi], r_a], writes=[rps[po]])
                    xi = mc % 3
                    ld("sp", hres[xi][:], h1f[mc, :, t0:t0 + W], [rhres[xi]])
                    S.op("dve", lambda e, xi=xi, po=po: e.scalar_tensor_tensor(
                        out=zch[xi][:], in0=hres[xi][:], scalar=ALPHA, in1=ps[po][:, 0:W], op0=ALU.mult, op1=ALU.add),
                         reads=[rhres[xi], rps[po]], writes=[rzch[xi]])
                    S.op("act", lambda e, xi=xi: e.copy(out=zbf[xi][:, 0, :], in_=zch[xi][:]), reads=[rzch[xi]],
                         writes=[rzbf[xi]])
                    S.op("act", lambda e, xi=xi: e.activation(out=zbf[xi][:, 1, :], in_=zch[xi][:], func=AF.Square),
                         reads=[rzch[xi]], writes=[rzbf[xi]])
                    S.op("pe", lambda e, xi=xi, mc=mc: e.matmul(ps[PS_M1][:, 0:W], ones_b[:, :], zbf[xi][:, 0, :],
                                                                start=(mc == 0), stop=(mc == 31)),
                         reads=[rzbf[xi], r_const], writes=[rps[PS_M1]])
                    S.op("pe", lambda e, xi=xi, mc=mc: e.matmul(ps[PS_M2][:, 0:W], ones_b[:, :], zbf[xi][:, 1, :],
                                                                start=(mc == 0), stop=(mc == 31)),
                         reads=[rzbf[xi], r_const], writes=[rps[PS_M2]])
                    ld("pool", zscr[mc, :, t0:t0 + W], zch[xi][:], [r_z[i][mc]], reads=[rzch[xi]])
                S.op("dve", lambda e: e.tensor_scalar(out=mean[:], in0=ps[PS_M1][:, 0:W], scalar1=1.0 / D, scalar2=None,
                                                      op0=ALU.mult), reads=[rps[PS_M1]], writes=[r_stat])
                S.op("dve", lambda e: e.tensor_tensor(out=msq[:], in0=mean[:], in1=mean[:], op=ALU.mult),
                     reads=[r_stat], writes=[r_stat])
                S.op("dve", lambda e: e.scalar_tensor_tensor(out=rstd[:], in0=ps[PS_M2][:, 0:W], scalar=1.0 / D,
                                                             in1=msq[:], op0=ALU.mult, op1=ALU.subtract),
                     reads=[rps[PS_M2], r_stat], writes=[r_stat])
                S.op("act", lambda e: e.activation(out=rstd[:], in_=rstd[:], func=AF.Sqrt, bias=eps_t[:, 1:2],
                                                   scale=1.0), reads=[r_stat, r_const], writes=[r_stat])
                S.op("dve", lambda e: e.reciprocal(out=rstd[:], in_=rstd[:]), reads=[r_stat], writes=[r_stat])
                lo = 2 if i == 0 else 0
                for mc in range(32):
                    xi = mc % 3
                    oi = mc % 2
                    ld("sp", zch[xi][:], zscr[mc, :, t0:t0 + W], [rzch[xi]], reads=[r_z[i][mc]])
                    S.op("dve", lambda e, xi=xi: e.tensor_tensor(out=zch[xi][:], in0=zch[xi][:], in1=mean[:],
                                                                 op=ALU.subtract), reads=[rzch[xi], r_stat],
                         writes=[rzch[xi]])
                    S.op("pool", lambda e, xi=xi: e.tensor_tensor(out=zch[xi][:], in0=zch[xi][:], in1=rstd[:],
                                                                  op=ALU.mult), reads=[rzch[xi], r_stat],
                         writes=[rzch[xi]])
                    S.op("dve", lambda e, xi=xi, oi=oi, mc=mc: e.tensor_scalar(
                        out=och[oi][:], in0=zch[xi][:], scalar1=vcol("ln2_g", mc), scalar2=vcol("ln2_b", mc),
                        op0=ALU.mult, op1=ALU.add), reads=[rzch[xi], r_const], writes=[roch[oi]])
                    ld("pool", outT[mc * 128:(mc + 1) * 128, t0 + lo - 2:t0 + W - 2], och[oi][:, lo:W], [],
                       reads=[roch[oi]])
            S.flush()
    return nc


_PROG = [None]
_DEBUG_HOOK = [None]


def _feat_cols(v):
    v = np.asarray(v, np.float32).reshape(-1)
    return np.ascontiguousarray(v.reshape(-1, 128).T)


def kernel(x, positions, w_in, b_gate, da_lambda_q1, da_lambda_k1, da_lambda_q2, da_lambda_k2, da_subln_g,
           mla_q_norm_g, mla_kv_norm_g, w_uq, w_ukv, w_proj_a, w_proj_b, w_out, ln1_g, ln1_b, w_up, conv_w, conv_b,
           w_down, ln2_g, ln2_b):
    x = np.asarray(x, np.float32)
    positions = np.asarray(positions, np.int32)
    bf = ml_dtypes.bfloat16
    if _PROG[0] is None:
        _PROG[0] = build_program()
    nc = _PROG[0]

    vecs = np.zeros((128, NV), np.float32)

    def put(name, v):
        c = _feat_cols(v)
        vecs[:, VEC[name]:VEC[name] + c.shape[1]] = c

    put("b_gate", b_gate[0]); put("ln1_g", ln1_g[0]); put("ln1_b", ln1_b[0]); put("ln2_g", ln2_g[0])
    put("ln2_b", ln2_b[0]); put("cw0", conv_w[0, 0]); put("cw1", conv_w[0, 1]); put("cw2", conv_w[0, 2])
    put("cb", conv_b[0]); put("subln", da_subln_g[0]); put("qn_g", mla_q_norm_g[0]); put("kvn_g", mla_kv_norm_g[0])
    lamv = np.concatenate([np.asarray(a, np.float32).reshape(-1) for a in
                           (da_lambda_q1[0], da_lambda_k1[0], da_lambda_q2[0], da_lambda_k2[0])])
    lamv = np.ascontiguousarray(np.broadcast_to(lamv[None, :], (128, 512)))
    inv = (np.float32(10000.0) ** (-np.arange(32, dtype=np.float32) / np.float32(32))).astype(np.float32)
    inv2 = np.concatenate([inv, inv]).reshape(64, 1).astype(np.float32)
    rot = np.zeros((64, 64), np.float32)
    for i in range(32):
        rot[32 + i, i] = -1.0
        rot[i, 32 + i] = 1.0
    slopes = 2.0 ** (-(np.arange(1, 9, dtype=np.float64)))
    alk = np.zeros((3, 8, 128), np.float32)
    alq = np.zeros((8, 3, NQ), np.float32)
    npr = np.arange(NQ) - (NQ - 1)
    a64 = np.floor(npr / 64.0) * 64.0
    b64 = npr - a64
    for h in range(8):
        alk[0, h, :] = 1.0
        alk[1, h, :] = 1.0
        alk[2, h, :] = -slopes[h] * np.arange(128)
        alq[h, 0, :] = -slopes[h] * a64
        alq[h, 1, :] = -slopes[h] * b64
        alq[h, 2, :] = 1.0
    masks = MASKS_NP.astype(bf)

    shared = {"w_in": np.ascontiguousarray(w_in[0], np.float32), "w_uq": np.ascontiguousarray(w_uq[0], np.float32),
              "w_ukv": np.ascontiguousarray(w_ukv[0], np.float32), "w_pa": np.ascontiguousarray(w_proj_a[0], np.float32),
              "w_pb": np.ascontiguousarray(w_proj_b[0], np.float32), "w_out": np.ascontiguousarray(w_out[0], np.float32),
              "w_up": np.ascontiguousarray(w_up[0], np.float32), "w_dn": np.ascontiguousarray(w_down[0], np.float32),
              "vecs": vecs, "lamv": lamv, "inv2": inv2, "rotm": rot.astype(bf), "alk": alk.astype(bf),
              "alq": alq.astype(bf), "masks": masks}
    in_maps = []
    for c in range(NCORE):
        b, j = c // 4, c % 4
        c0 = 2048 * j
        xb = x[b]
        pb = positions[b]
        xqT = np.zeros((D, NQ), np.float32)
        posq = np.zeros((NQ,), np.int32)
        lo = 2 if j == 0 else 0
        xqT[:, lo:] = xb[c0 - 2 + lo:c0 + 2048].T
        posq[lo:] = pb[c0 - 2 + lo:c0 + 2048]
        nvalid = c0 + 2048
        xkT = np.zeros((D, NKV), np.float32)
        posk = np.zeros((NKV,), np.int32)
        xkT[:, :nvalid] = xb[:nvalid][::-1].T
        posk[:nvalid] = pb[:nvalid][::-1]
        r = np.arange(NKV).reshape(NB, 128).T
        kvalid = np.where(r < nvalid, 0.0, NEG).astype(np.float32)
        abias = np.zeros((128, 8, NT, NB), np.float32)
        for h in range(8):
            for i in range(NT):
                if h < ALIBI_MM_HEADS:
                    abias[:, h, i, :] = kvalid + (-slopes[h] * 128.0 * np.arange(NB))[None, :]
                else:
                    nref = W * i + W - 1 - (NQ - 1)
                    abias[:, h, i, :] = kvalid - slopes[h] * (r.astype(np.float64) + nref)
        m = dict(shared)
        m.update({"xqT": xqT, "xkT": xkT,
                  "posq": np.ascontiguousarray(np.broadcast_to(posq[None, :], (64, NQ))),
                  "posk": np.ascontiguousarray(np.broadcast_to(posk[None, :], (64, NKV))),
                  "abias": abias.astype(np.float32), "kvalid": kvalid,
                  "hv": np.full((128, 1), 0.0 if j == 0 else 1.0, np.float32)})
        in_maps.append(m)
    if _DEBUG_HOOK[0] is not None:
        return _DEBUG_HOOK[0](in_maps)
    res = run_bass_kernel_spmd(nc, in_maps, core_ids=list(range(NCORE)))
    out = np.empty((2, SEQ, D), np.float32)
    for c in range(NCORE):
        b, j = c // 4, c % 4
        out[b, 2048 * j:2048 * (j + 1), :] = res.results[c]["outT"].T
    return out
```

```python
import math
import numpy as np
import ml_dtypes
from contextlib import ExitStack
import concourse.bass as bass
import concourse.mybir as mybir
from concourse.bass_utils import run_bass_kernel_spmd

F32 = mybir.dt.float32
BF16 = mybir.dt.bfloat16
I32 = mybir.dt.int32
AF = mybir.ActivationFunctionType
ALU = mybir.AluOpType

D = 4096
SEQ = 8192
NCORE = 8
W = 410
NT = 5
NQ = W * NT
NKV = 8192
NB = 64
DFF = 11008
LN_EPS = 1e-5
RMS_EPS = 1e-6
ALPHA = 2.0 ** 0.25
LAMBDA_INIT = 0.8 - 0.6 * math.exp(0.0)
OFF_Q, OFF_K, OFF_V, OFF_CQ, OFF_CKV, OFF_KR, OFF_G = 0, 2048, 4096, 6144, 7168, 7680, 7744
NEG = -30000.0
ALIBI_MM_HEADS = 2
BG_CONVERT = True
BG_EVERY = 6

ENGS = ("pe", "act", "dve", "pool", "sp")
NDMA = {"sp": 24, "pool": 24}


class Res:
    __slots__ = ("lw", "rd", "excl")

    def __init__(self, excl=False):
        self.lw = None
        self.rd = {}
        self.excl = excl


class Op:
    __slots__ = ("eng", "fn", "deps", "sig", "dma", "sem", "val", "dq")

    def __init__(self, eng, fn, dma):
        self.eng = eng
        self.fn = fn
        self.dma = dma
        self.deps = []
        self.sig = False
        self.sem = None
        self.val = 0
        self.dq = None


class Sched:
    def __init__(self, nc, stack):
        self.nc = nc
        self.ops = {e: [] for e in ENGS}
        self.esem = {e: stack.enter_context(nc.semaphore("es_" + e)) for e in ENGS}
        self.ecnt = {e: 0 for e in ENGS}
        self.dsem = {q: [stack.enter_context(nc.semaphore("ds_%s%d" % (q, i))) for i in range(n)]
                     for q, n in NDMA.items()}
        self.dcnt = {q: 0 for q in NDMA}
        self.dhist = {q: [] for q in NDMA}
        self.known = {e: {x: 0 for x in ENGS} for e in ENGS}
        self.dknown = {e: set() for e in ENGS}
        self.nops = 0

    def op(self, eng, fn, reads=(), writes=(), dma=False):
        o = Op(eng, fn, dma)
        deps = []
        if any(r.excl for r in reads):
            writes = list(writes) + [r for r in reads if r.excl]
            reads = [r for r in reads if not r.excl]
        for r in reads:
            if r.lw is not None:
                deps.append(r.lw)
        for w in writes:
            if w.lw is not None:
                deps.append(w.lw)
            deps.extend(w.rd.values())
        if dma:
            q = eng
            n = self.dcnt[q]
            P = len(self.dsem[q])
            o.dq = (q, n)
            if n >= P:
                deps.append(self.dhist[q][n - P])
            self.dhist[q].append(o)
            if len(self.dhist[q]) > 4 * P:
                pass
            self.dcnt[q] = n + 1
        seen = set()
        for d in deps:
            if eng == "pe" and d.eng == "pe" and not d.dma:
                continue
            if id(d) not in seen:
                seen.add(id(d))
                o.deps.append(d)
                d.sig = True
        for r in reads:
            if dma:
                r.rd[("d", id(o))] = o
            else:
                r.rd[eng] = o
        for w in writes:
            w.lw = o
            w.rd = {}
        self.ops[eng].append(o)
        self.nops += 1
        return o

    def flush(self):
        nc = self.nc
        for e in ENGS:
            c = self.ecnt[e]
            for o in self.ops[e]:
                if o.dma:
                    q, n = o.dq
                    P = len(self.dsem[q])
                    o.sem = self.dsem[q][n % P]
                    o.val = 16 * (n // P + 1)
                elif o.sig:
                    c += 1
                    o.sem = self.esem[e]
                    o.val = c
            self.ecnt[e] = c
        last_dma = []
        for q in NDMA:
            P = len(self.dsem[q])
            n = self.dcnt[q]
            for i in range(max(0, n - P), n):
                last_dma.append((self.dsem[q][i % P], 16 * (i // P + 1)))
        ecnt_final = {}
        for e in ENGS:
            lastc = None
            for o in reversed(self.ops[e]):
                if not o.dma:
                    lastc = o
                    break
            if lastc is not None and not lastc.sig:
                lastc.sig = True
                self.ecnt[e] += 1
                lastc.sem = self.esem[e]
                lastc.val = self.ecnt[e]
            ecnt_final[e] = self.ecnt[e]

        def emit_engine(e, eng):
            known = self.known[e]
            dknown = self.dknown[e]
            for o in self.ops[e]:
                for d in o.deps:
                    if d.dma:
                        key = (id(d.sem), d.val)
                        if key in dknown:
                            continue
                        dknown.add(key)
                        eng.wait_ge(d.sem, d.val)
                    else:
                        if known[d.eng] >= d.val:
                            continue
                        known[d.eng] = d.val
                        eng.wait_ge(d.sem, d.val)
                ins = o.fn(eng)
                if o.dma:
                    ins.then_inc(o.sem, 16)
                elif o.sig:
                    ins.then_inc(o.sem, 1)
            for x in ENGS:
                if known[x] < ecnt_final[x]:
                    known[x] = ecnt_final[x]
                    eng.wait_ge(self.esem[x], ecnt_final[x])
            for sem, val in last_dma:
                eng.wait_ge(sem, val)
            dknown.clear()

        with nc.Block() as block:
            @block.tensor
            def _(eng):
                emit_engine("pe", eng)

            @block.scalar
            def _(eng):
                emit_engine("act", eng)

            @block.vector
            def _(eng):
                emit_engine("dve", eng)

            @block.gpsimd
            def _(eng):
                emit_engine("pool", eng)

            @block.sync
            def _(eng):
                emit_engine("sp", eng)
        self.ops = {e: [] for e in ENGS}
        for q in NDMA:
            P = len(self.dsem[q])
            self.dhist[q] = self.dhist[q]


def tile_blocks(i):
    out = []
    nmin = W * i - (NQ - 1)
    nmax = W * i + W - 1 - (NQ - 1)
    for rb in range(NB):
        dmin = nmin + 128 * rb
        dmax = nmax + 128 * rb + 127
        if dmax < 0:
            continue
        out.append((rb, dmin >= 0))
    return out


def build_masks():
    masks = []
    index = {}
    for i in range(NT):
        for rb, full in tile_blocks(i):
            if not full:
                n = np.arange(W)[None, :] + W * i - (NQ - 1)
                ik = np.arange(128)[:, None]
                m = np.where((n + 128 * rb + ik) >= 0, 0.0, NEG).astype(np.float32)
                index[(i, rb)] = len(masks)
                masks.append(m)
    return np.stack(masks, axis=1), index


MASKS_NP, MASK_IDX = build_masks()
NMASK = MASKS_NP.shape[1]

VEC = {}
_o = 0
for _name, _n in (("b_gate", 64), ("ln1_g", 32), ("ln1_b", 32), ("ln2_g", 32), ("ln2_b", 32), ("cw0", 172),
                  ("cw1", 172), ("cw2", 172), ("cb", 172), ("subln", 2), ("qn_g", 8), ("kvn_g", 4)):
    VEC[_name] = _o
    _o += _n
NV = _o


def build_program(upto="E", dbg=()):
    nc = bass.Bass("TRN2", target_bir_lowering=False)
    dt = nc.dram_tensor

    def din(name, shape, dtype=F32):
        return dt(name, list(shape), dtype, kind="ExternalInput").ap()

    def dscr(name, shape, dtype=BF16):
        return dt(name, list(shape), dtype, kind=("ExternalOutput" if name in dbg else "Internal")).ap()

    xqT = din("xqT", [D, NQ])
    xkT = din("xkT", [D, NKV])
    posq = din("posq", [64, NQ], I32)
    posk = din("posk", [64, NKV], I32)
    w_in = din("w_in", [D, 15936])
    w_uq = din("w_uq", [1024, 3072])
    w_ukv = din("w_ukv", [512, 4096])
    w_pa = din("w_pa", [2048, D])
    w_pb = din("w_pb", [2048, D])
    w_out = din("w_out", [D, D])
    w_up = din("w_up", [D, 2 * DFF])
    w_dn = din("w_dn", [DFF, D])
    vecs = din("vecs", [128, NV])
    lamv = din("lamv", [128, 512])
    inv2 = din("inv2", [64, 1])
    rotm = din("rotm", [64, 64], BF16)
    alk = din("alk", [3, 8, 128], BF16)
    alq = din("alq", [8, 3, NQ], BF16)
    abias = din("abias", [128, 8, NT, NB])
    kvalid = din("kvalid", [128, NB])
    masks_d = din("masks", [128, NMASK, W], BF16)
    hv = din("hv", [128, 1])
    outT = dt("outT", [D, 2048], F32, kind="ExternalOutput").ap()

    def wscr(name, K, M, cw):
        ng = (M + cw - 1) // cw
        return dscr(name, [ng, 128, K // 128, cw])

    wq_s = wscr("wq_s", D, 2048, 512)
    wk_s = wscr("wk_s", D, 2048, 512)
    wv_s = wscr("wv_s", D, 2048, 512)
    wcq_s = wscr("wcq_s", D, 1024, 512)
    wckv_s = wscr("wckv_s", D, 512, 512)
    wkr_s = wscr("wkr_s", D, 64, 64)
    wg_s = wscr("wg_s", D, 8192, 256)
    wuq_s = wscr("wuq_s", 1024, 3072, 192)
    wukv_s = wscr("wukv_s", 512, 4096, 512)
    wpa_s = wscr("wpa_s", 2048, D, 256)
    wpb_s = wscr("wpb_s", 2048, D, 256)
    wout_s = wscr("wout_s", D, D, 256)
    wupg_s = wscr("wupg_s", D, DFF, 256)
    wupv_s = wscr("wupv_s", D, DFF, 256)
    wdn_s = wscr("wdn_s", DFF, D, 128)

    kTda = dscr("kTda", [16, 128, NKV])
    vda = dscr("vda", [8, 128, NB, 256])
    kTm = dscr("kTm", [16, 128, NKV])
    kTr = dscr("kTr", [64, NKV])
    vm = dscr("vm", [16, 128, NB, 128])
    qTda = dscr("qTda", [16, 128, NQ])
    qTm = dscr("qTm", [16, 128, NQ])
    qTr = dscr("qTr", [16, 64, NQ])
    oaT = dscr("oaT", [16, 128, NQ])
    obT = dscr("obT", [16, 128, NQ])
    yscr = dscr("yscr", [32, 128, NQ], F32)
    h1f = dscr("h1f", [32, 128, NQ], F32)
    h1b = dscr("h1b", [32, 128, NQ], BF16)
    zscr = dscr("zscr", [32, 128, NQ], F32)
    adbg = dscr("adbg", [86, 128, W], BF16) if "adbg" in dbg else None

    with ExitStack() as top:
        S = Sched(nc, top)
        sb = lambda st, name, shape, dtype: st.enter_context(nc.sbuf_tensor(name, list(shape), dtype))
        ps = [top.enter_context(nc.psum_tensor("ps%d" % i, [128, 512], F32)) for i in range(8)]
        rps = [Res(True) for _ in range(8)]
        psn = [0]

        def next_ps():
            b = psn[0] % 8
            psn[0] += 1
            return b

        vec_t = sb(top, "vec_t", [128, NV], F32)
        ones_b = sb(top, "ones_b", [128, 128], BF16)
        rot_t = sb(top, "rot_t", [64, 64], BF16)
        inv_t = sb(top, "inv_t", [64, 1], F32)
        hv_t = sb(top, "hv_t", [128, 1], F32)
        nlam_t = sb(top, "nlam_t", [128, 1], F32)
        sgs_t = sb(top, "sgs_t", [128, 2], F32)
        eps_t = sb(top, "eps_t", [128, 2], F32)
        r_const = Res()

        def vcol(name, c):
            k = VEC[name] + c
            return vec_t[:, k:k + 1]

        def ld(eng_q, out, in_, writes, reads=()):
            return S.op(eng_q, lambda e: e.dma_start(out=out, in_=in_), reads=reads, writes=writes, dma=True)

        with ExitStack() as st:
            lam_t = sb(st, "lam_t", [128, 512], F32)
            lt1 = sb(st, "lt1", [128, 256], F32)
            ls = sb(st, "ls", [128, 4], F32)
            rl = Res()
            ld("sp", vec_t[:], vecs, [r_const])
            ld("sp", rot_t[:], rotm, [r_const])
            ld("sp", inv_t[:], inv2, [r_const])
            ld("sp", hv_t[:], hv, [r_const])
            ld("sp", lam_t[:], lamv, [rl])
            S.op("dve", lambda e: e.memset(ones_b[:], 1.0), writes=[r_const])
            S.op("dve", lambda e: e.memset(eps_t[:, 0:1], RMS_EPS), writes=[r_const])
            S.op("dve", lambda e: e.memset(eps_t[:, 1:2], LN_EPS), writes=[r_const])
            S.op("dve", lambda e: e.tensor_tensor(out=lt1[:, 0:128], in0=lam_t[:, 0:128], in1=lam_t[:, 128:256],
                                                  op=ALU.mult), reads=[rl], writes=[rl])
            S.op("dve", lambda e: e.tensor_tensor(out=lt1[:, 128:256], in0=lam_t[:, 256:384], in1=lam_t[:, 384:512],
                                                  op=ALU.mult), reads=[rl], writes=[rl])
            S.op("dve", lambda e: e.reduce_sum(out=ls[:, 0:1], in_=lt1[:, 0:128], axis=mybir.AxisListType.X),
                 reads=[rl], writes=[rl])
            S.op("dve", lambda e: e.reduce_sum(out=ls[:, 1:2], in_=lt1[:, 128:256], axis=mybir.AxisListType.X),
                 reads=[rl], writes=[rl])
            S.op("act", lambda e: e.activation(out=ls[:, 2:4], in_=ls[:, 0:2], func=AF.Exp), reads=[rl], writes=[rl])
            S.op("dve", lambda e: e.tensor_tensor(out=ls[:, 0:1], in0=ls[:, 3:4], in1=ls[:, 2:3], op=ALU.subtract),
                 reads=[rl], writes=[rl])
            S.op("dve", lambda e: e.tensor_scalar(out=nlam_t[:], in0=ls[:, 0:1], scalar1=-LAMBDA_INIT, scalar2=None,
                                                  op0=ALU.add), reads=[rl], writes=[r_const])
            S.op("dve", lambda e: e.tensor_scalar(out=sgs_t[:], in0=vec_t[:, VEC["subln"]:VEC["subln"] + 2],
                                                  scalar1=1.0 - LAMBDA_INIT, scalar2=None, op0=ALU.mult),
                 reads=[r_const], writes=[r_const])
            S.flush()

        def make_converter(st, KH, nbuf, engines, tag):
            stg = [sb(st, "cvf%s%d" % (tag, i), [128, KH * 512], F32) for i in range(nbuf)]
            stb = [sb(st, "cvb%s%d" % (tag, i), [128, KH * 512], BF16) for i in range(nbuf)]
            rf = [Res() for _ in range(nbuf)]
            rb_ = [Res() for _ in range(nbuf)]
            cnt = [0]

            def convert(src, col0, K, M, cw, dst):
                KC = K // 128
                ng = (M + cw - 1) // cw
                kh = max(1, min(KC, (KH * 512) // cw))
                for g in range(ng):
                    wd = min(cw, M - g * cw)
                    for k0 in range(0, KC, kh):
                        k1 = min(KC, k0 + kh)
                        nk = k1 - k0
                        i = cnt[0] % nbuf
                        ce = engines[cnt[0] % len(engines)]
                        cnt[0] += 1
                        fv = stg[i][:, 0:nk * wd].rearrange("p (k c) -> p k c", c=wd)
                        bv = stb[i][:, 0:nk * wd].rearrange("p (k c) -> p k c", c=wd)
                        sv = src[k0 * 128:k1 * 128, col0 + g * cw:col0 + g * cw + wd].rearrange(
                            "(k p) c -> p k c", p=128)
                        ld("sp", fv, sv, [rf[i]])
                        if ce == "act":
                            S.op("act", lambda e, bv=bv, fv=fv: e.copy(out=bv, in_=fv), reads=[rf[i]], writes=[rb_[i]])
                        else:
                            S.op(ce, lambda e, bv=bv, fv=fv: e.tensor_copy(out=bv, in_=fv), reads=[rf[i]],
                                 writes=[rb_[i]])
                        ld("pool", dst[g, :, k0:k1, 0:wd], bv, [], reads=[rb_[i]])
                        yield
            return convert

        def late_weights(convert):
            yield from convert(w_in, OFF_G, D, 8192, 256, wg_s)
            yield from convert(w_pa, 0, 2048, D, 256, wpa_s)
            yield from convert(w_pb, 0, 2048, D, 256, wpb_s)
            yield from convert(w_out, 0, D, D, 256, wout_s)
            yield from convert(w_up, 0, D, DFF, 256, wupg_s)
            yield from convert(w_up, DFF, D, DFF, 256, wupv_s)
            yield from convert(w_dn, 0, DFF, D, 128, wdn_s)

        with ExitStack() as st:
            convert = make_converter(st, 16, 3, ("act", "dve", "pool"), "p")
            for args in ((w_in, OFF_K, D, 2048, 512, wk_s), (w_in, OFF_V, D, 2048, 512, wv_s),
                         (w_in, OFF_CKV, D, 512, 512, wckv_s), (w_in, OFF_KR, D, 64, 64, wkr_s),
                         (w_ukv, 0, 512, 4096, 512, wukv_s), (w_in, OFF_Q, D, 2048, 512, wq_s),
                         (w_in, OFF_CQ, D, 1024, 512, wcq_s), (w_uq, 0, 1024, 3072, 192, wuq_s)):
                for _ in convert(*args):
                    pass
            if not BG_CONVERT:
                for _ in late_weights(convert):
                    pass
            S.flush()

        if upto == "P":
            return nc
        def rope_tables(st, tag, posd, c0, n, cs_t, rcs):
            pi_t, pf_t, pa_t, rr = tag
            TWO_PI = 2.0 * math.pi
            C1 = 6.28125
            C2 = TWO_PI - C1
            ki_t = pi_t
            ld("sp", pi_t[:, 0:n], posd[:, c0:c0 + n], [rr])
            S.op("dve", lambda e: e.tensor_copy(out=pf_t[:, 0:n], in_=pi_t[:, 0:n]), reads=[rr], writes=[rr])
            S.op("dve", lambda e: e.tensor_scalar(out=pa_t[:, 0:n], in0=pf_t[:, 0:n], scalar1=inv_t[:, 0:1],
                                                  scalar2=None, op0=ALU.mult), reads=[rr, r_const], writes=[rr])
            S.op("dve", lambda e: e.tensor_scalar(out=ki_t[:, 0:n], in0=pa_t[:, 0:n], scalar1=1.0 / TWO_PI,
                                                  scalar2=None, op0=ALU.mult), reads=[rr], writes=[rr])
            S.op("dve", lambda e: e.tensor_copy(out=pf_t[:, 0:n], in_=ki_t[:, 0:n]), reads=[rr], writes=[rr])
            S.op("dve", lambda e: e.scalar_tensor_tensor(out=pa_t[:, 0:n], in0=pf_t[:, 0:n], scalar=-C1,
                                                         in1=pa_t[:, 0:n], op0=ALU.mult, op1=ALU.add),
                 reads=[rr], writes=[rr])
            S.op("dve", lambda e: e.scalar_tensor_tensor(out=pa_t[:, 0:n], in0=pf_t[:, 0:n], scalar=-C2,
                                                         in1=pa_t[:, 0:n], op0=ALU.mult, op1=ALU.add),
                 reads=[rr], writes=[rr])

            def wrap():
                S.op("dve", lambda e: e.tensor_scalar(out=pf_t[:, 0:n], in0=pa_t[:, 0:n], scalar1=math.pi,
                                                      scalar2=None, op0=ALU.is_gt), reads=[rr], writes=[rr])
                S.op("dve", lambda e: e.scalar_tensor_tensor(out=pa_t[:, 0:n], in0=pf_t[:, 0:n], scalar=-TWO_PI,
                                                             in1=pa_t[:, 0:n], op0=ALU.mult, op1=ALU.add),
                     reads=[rr], writes=[rr])
                S.op("dve", lambda e: e.tensor_scalar(out=pf_t[:, 0:n], in0=pa_t[:, 0:n], scalar1=-math.pi,
                                                      scalar2=None, op0=ALU.is_lt), reads=[rr], writes=[rr])
                S.op("dve", lambda e: e.scalar_tensor_tensor(out=pa_t[:, 0:n], in0=pf_t[:, 0:n], scalar=TWO_PI,
                                                             in1=pa_t[:, 0:n], op0=ALU.mult, op1=ALU.add),
                     reads=[rr], writes=[rr])

            wrap()
            S.op("act", lambda e: e.activation(out=cs_t[:, 1, 0:n], in_=pa_t[:, 0:n], func=AF.Sin), reads=[rr],
                 writes=[rcs, rr])
            S.op("dve", lambda e: e.tensor_scalar(out=pa_t[:, 0:n], in0=pa_t[:, 0:n], scalar1=0.5 * math.pi,
                                                  scalar2=None, op0=ALU.add), reads=[rr], writes=[rr])
            wrap()
            S.op("act", lambda e: e.activation(out=cs_t[:, 0, 0:n], in_=pa_t[:, 0:n], func=AF.Sin), reads=[rr],
                 writes=[rcs, rr])

        with ExitStack() as st:
            xkb = [sb(st, "xkb%d" % i, [128, 32, 512], BF16) for i in range(2)]
            rxk = [Res() for _ in range(2)]
            xst = [sb(st, "xst%d" % i, [128, 8, 512], F32) for i in range(2)]
            rxst = [Res() for _ in range(2)]
            wsl = [sb(st, "wsl%d" % i, [128, 16384], BF16) for i in range(2)]
            rws = [Res() for _ in range(2)]
            wcnt = [0]
            stgb = [sb(st, "stgb%d" % i, [128, 512], BF16) for i in range(4)]
            rstg = [Res() for _ in range(4)]
            scnt = [0]
            ckv = sb(st, "ckv", [128, 4, 512], F32)
            sq = sb(st, "sqa", [128, 4, 512], BF16)
            rstd = sb(st, "rstda", [128, 512], F32)
            ckvn = sb(st, "ckvn", [128, 4, 512], BF16)
            r_ckv, r_sq, r_rstd, r_ckvn = Res(), Res(), Res(), Res()
            pi_t = sb(st, "pi_a", [64, 512], I32)
            pf_t = sb(st, "pf_a", [64, 512], F32)
            pa_t = sb(st, "pa_a", [64, 512], F32)
            cs_t = sb(st, "cs_a", [64, 2, 512], F32)
            krf = sb(st, "krf", [64, 512], F32)
            krb = sb(st, "krb", [64, 512], BF16)
            kro = sb(st, "kro", [64, 512], F32)
            r_rope, r_cs, r_kr = Res(), Res(), Res()

            def load_slab(scr_g, nelem):
                i = wcnt[0] % 2
                wcnt[0] += 1
                ld("sp", wsl[i][:, 0:nelem], scr_g.rearrange("p k c -> p (k c)"), [rws[i]])
                return i

            def stage_out(psb, M, N, dst, scale=None, view=None):
                k = scnt[0] % 4
                scnt[0] += 1
                if scale is None:
                    S.op("act", lambda e: e.copy(out=stgb[k][0:M, 0:N], in_=ps[psb][0:M, 0:N]), reads=[rps[psb]],
                         writes=[rstg[k]])
                else:
                    S.op("act", lambda e: e.activation(out=stgb[k][0:M, 0:N], in_=ps[psb][0:M, 0:N], func=AF.Copy,
                                                       scale=scale), reads=[rps[psb]], writes=[rstg[k]])
                srcv = stgb[k][0:M, 0:N]
                if view is not None:
                    srcv = view(srcv)
                ld("pool", dst, srcv, [], reads=[rstg[k]])

            for g in range(16):
                xi = g % 2
                t0 = g * 512
                for q4 in range(4):
                    si = (g * 4 + q4) % 2
                    ld("sp", xst[si][:], xkT[q4 * 1024:(q4 + 1) * 1024, t0:t0 + 512].rearrange("(k p) n -> p k n", p=128),
                       [rxst[si]])
                    eng = "dve" if q4 % 2 == 0 else "pool"
                    S.op(eng, lambda e, si=si, q4=q4, xi=xi: e.tensor_copy(out=xkb[xi][:, q4 * 8:(q4 + 1) * 8, :],
                                                                          in_=xst[si][:]),
                         reads=[rxst[si]], writes=[rxk[xi]])
                xk = xkb[xi]
                for s in range(4):
                    wi = load_slab(wk_s[s], 16384)
                    wv_ = wsl[wi][:].rearrange("p (k c) -> p k c", c=512)
                    for c in range(4):
                        hm = s * 4 + c
                        pb = next_ps()
                        for kc in range(32):
                            S.op("pe", lambda e, wv_=wv_, kc=kc, c=c, pb=pb, xk=xk: e.matmul(
                                ps[pb][:, :], wv_[:, kc, c * 128:(c + 1) * 128], xk[:, kc, :],
                                start=(kc == 0), stop=(kc == 31)), reads=[rws[wi], rxk[xi]], writes=[rps[pb]])
                        stage_out(pb, 128, 512, kTda[hm, :, t0:t0 + 512])
                for s in range(4):
                    wi = load_slab(wv_s[s], 16384)
                    wv_ = wsl[wi][:].rearrange("p (k c) -> p k c", c=512)
                    for tb in range(4):
                        pb = next_ps()
                        for kc in range(32):
                            S.op("pe", lambda e, wv_=wv_, kc=kc, tb=tb, pb=pb, xk=xk: e.matmul(
                                ps[pb][:, :], xk[:, kc, tb * 128:(tb + 1) * 128], wv_[:, kc, :],
                                start=(kc == 0), stop=(kc == 31)), reads=[rws[wi], rxk[xi]], writes=[rps[pb]])
                        rbk = g * 4 + tb
                        stage_out(pb, 128, 512, vda[2 * s:2 * s + 2, :, rbk, :].rearrange("h p e -> p h e"),
                                  view=lambda a: a.rearrange("p (h e) -> p h e", h=2))
                wi = load_slab(wckv_s[0], 16384)
                wv_ = wsl[wi][:].rearrange("p (k c) -> p k c", c=512)
                for c in range(4):
                    pb = next_ps()
                    for kc in range(32):
                        S.op("pe", lambda e, wv_=wv_, kc=kc, c=c, pb=pb, xk=xk: e.matmul(
                            ps[pb][:, :], wv_[:, kc, c * 128:(c + 1) * 128], xk[:, kc, :],
                            start=(kc == 0), stop=(kc == 31)), reads=[rws[wi], rxk[xi]], writes=[rps[pb]])
                    S.op("dve", lambda e, c=c, pb=pb: e.tensor_copy(out=ckv[:, c, :], in_=ps[pb][:, :]),
                         reads=[rps[pb]], writes=[r_ckv])
                    S.op("act", lambda e, c=c, pb=pb: e.activation(out=sq[:, c, :], in_=ps[pb][:, :], func=AF.Square),
                         reads=[rps[pb]], writes=[r_sq])
                pb = next_ps()
                for c in range(4):
                    S.op("pe", lambda e, c=c, pb=pb: e.matmul(ps[pb][:, :], ones_b[:, :], sq[:, c, :], start=(c == 0),
                                                             stop=(c == 3)), reads=[r_sq, r_const], writes=[rps[pb]])
                S.op("act", lambda e, pb=pb: e.activation(out=rstd[:], in_=ps[pb][:, :], func=AF.Sqrt,
                                                          bias=eps_t[:, 0:1], scale=1.0 / 512.0),
                     reads=[rps[pb], r_const], writes=[r_rstd])
                S.op("dve", lambda e: e.reciprocal(out=rstd[:], in_=rstd[:]), reads=[r_rstd], writes=[r_rstd])
                for c in range(4):
                    S.op("dve", lambda e, c=c: e.scalar_tensor_tensor(out=ckvn[:, c, :], in0=ckv[:, c, :],
                                                                      scalar=vcol("kvn_g", c), in1=rstd[:],
                                                                      op0=ALU.mult, op1=ALU.mult),
                         reads=[r_ckv, r_rstd, r_const], writes=[r_ckvn])
                i_ = wcnt[0] % 2
                wcnt[0] += 1
                ld("sp", wsl[i_][:].rearrange("p (g f) -> p g f", g=8), wukv_s.rearrange("g p k c -> p g (k c)"),
                   [rws[i_]])
                wu = wsl[i_][:].rearrange("p (g k c) -> p g k c", g=8, k=4)
                for h in range(16):
                    pb = next_ps()
                    for kc in range(4):
                        S.op("pe", lambda e, h=h, kc=kc, pb=pb, wu=wu: e.matmul(
                            ps[pb][:, :], wu[:, h // 2, kc, (h % 2) * 256:(h % 2) * 256 + 128], ckvn[:, kc, :],
                            start=(kc == 0), stop=(kc == 3)), reads=[rws[i_], r_ckvn], writes=[rps[pb]])
                    stage_out(pb, 128, 512, kTm[h, :, t0:t0 + 512])
                for tb in range(4):
                    for s8 in range(8):
                        pb = next_ps()
                        for kc in range(4):
                            S.op("pe", lambda e, s8=s8, kc=kc, pb=pb, tb=tb, wu=wu: e.matmul(
                                ps[pb][:, :], ckvn[:, kc, tb * 128:(tb + 1) * 128], wu[:, s8, kc, :],
                                start=(kc == 0), stop=(kc == 3)), reads=[rws[i_], r_ckvn], writes=[rps[pb]])
                        k = scnt[0] % 4
                        scnt[0] += 1
                        S.op("act", lambda e, k=k, pb=pb: e.copy(
                            out=stgb[k][:, 0:256].rearrange("p (h e) -> p h e", h=2),
                            in_=ps[pb][:, :].rearrange("p (h t e) -> p h t e", h=2, t=2)[:, :, 1, :]),
                             reads=[rps[pb]], writes=[rstg[k]])
                        ld("pool", vm[2 * s8:2 * s8 + 2, :, g * 4 + tb, :].rearrange("h p e -> p h e"),
                           stgb[k][:, 0:256].rearrange("p (h e) -> p h e", h=2), [], reads=[rstg[k]])
                i_ = wcnt[0] % 2
                wcnt[0] += 1
                ld("sp", wsl[i_][:, 0:32 * 64], wkr_s[0].rearrange("p k c -> p (k c)"), [rws[i_]])
                wr = wsl[i_][:, 0:32 * 64].rearrange("p (k c) -> p k c", c=64)
                pb = next_ps()
                for kc in range(32):
                    S.op("pe", lambda e, kc=kc, pb=pb, wr=wr, xk=xk: e.matmul(ps[pb][0:64, :], wr[:, kc, :], xk[:, kc, :],
                                                                            start=(kc == 0), stop=(kc == 31)),
                         reads=[rws[i_], rxk[xi]], writes=[rps[pb]])
                rope_tables(st, (pi_t, pf_t, pa_t, r_rope), posk, t0, 512, cs_t, r_cs)
                S.op("dve", lambda e, pb=pb: e.tensor_copy(out=krf[:], in_=ps[pb][0:64, :]), reads=[rps[pb]],
                     writes=[r_kr])
                S.op("act", lambda e, pb=pb: e.copy(out=krb[:], in_=ps[pb][0:64, :]), reads=[rps[pb]], writes=[r_kr])
                pb2 = next_ps()
                S.op("pe", lambda e, pb2=pb2: e.matmul(ps[pb2][0:64, :], rot_t[:, :], krb[:, :], start=True, stop=True),
                     reads=[r_kr, r_const], writes=[rps[pb2]])
                S.op("dve", lambda e: e.tensor_tensor(out=krf[:], in0=krf[:], in1=cs_t[:, 0, :], op=ALU.mult),
                     reads=[r_kr, r_cs], writes=[r_kr])
                S.op("dve", lambda e, pb2=pb2: e.tensor_tensor(out=kro[:], in0=ps[pb2][0:64, :], in1=cs_t[:, 1, :],
                                                               op=ALU.mult), reads=[rps[pb2], r_cs], writes=[r_kr])
                k = scnt[0] % 4
                scnt[0] += 1
                S.op("dve", lambda e, k=k: e.tensor_tensor(out=stgb[k][0:64, :], in0=krf[:], in1=kro[:], op=ALU.add),
                     reads=[r_kr], writes=[rstg[k]])
                ld("pool", kTr[:, t0:t0 + 512], stgb[k][0:64, :], [], reads=[rstg[k]])
            S.flush()

        if upto == "A":
            return nc
        SC_DA = 128.0 ** -0.5
        SC_M = 192.0 ** -0.5
        with ExitStack() as st:
            xqb = [sb(st, "xqb%d" % i, [128, 32, W], BF16) for i in range(2)]
            rxq = [Res() for _ in range(2)]
            xst = [sb(st, "xsq%d" % i, [128, 8, W], F32) for i in range(2)]
            rxst = [Res() for _ in range(2)]
            wsl = [sb(st, "wsq%d" % i, [128, 16384], BF16) for i in range(2)]
            rws = [Res() for _ in range(2)]
            wcnt = [0]
            stgb = [sb(st, "stq%d" % i, [128, W], BF16) for i in range(4)]
            rstg = [Res() for _ in range(4)]
            scnt = [0]
            cq = sb(st, "cq", [128, 8, W], F32)
            sq = sb(st, "sqq", [128, 8, W], BF16)
            rstd = sb(st, "rstdq", [128, W], F32)
            cqn = sb(st, "cqn", [128, 8, W], BF16)
            r_cq, r_sq, r_rstd, r_cqn = Res(), Res(), Res(), Res()
            pi_t = sb(st, "pi_q", [64, W], I32)
            pf_t = sb(st, "pf_q", [64, W], F32)
            pa_t = sb(st, "pa_q", [64, W], F32)
            cs_t = sb(st, "cs_q", [64, 2, W], F32)
            qrf = sb(st, "qrf", [64, W], F32)
            qrb = sb(st, "qrb", [64, W], BF16)
            qro = sb(st, "qro", [64, W], F32)
            r_rope, r_cs, r_qr = Res(), Res(), Res()

            def stage_out_q(psb, M, dst, scale):
                k = scnt[0] % 4
                scnt[0] += 1
                S.op("act", lambda e: e.activation(out=stgb[k][0:M, :], in_=ps[psb][0:M, 0:W], func=AF.Copy,
                                                   scale=scale), reads=[rps[psb]], writes=[rstg[k]])
                ld("pool", dst, stgb[k][0:M, :], [], reads=[rstg[k]])

            for i in range(NT):
                xi = i % 2
                t0 = i * W
                for q4 in range(4):
                    si = (i * 4 + q4) % 2
                    ld("sp", xst[si][:], xqT[q4 * 1024:(q4 + 1) * 1024, t0:t0 + W].rearrange("(k p) n -> p k n", p=128),
                       [rxst[si]])
                    eng = "dve" if q4 % 2 == 0 else "pool"
                    S.op(eng, lambda e, si=si, q4=q4, xi=xi: e.tensor_copy(out=xqb[xi][:, q4 * 8:(q4 + 1) * 8, :],
                                                                          in_=xst[si][:]),
                         reads=[rxst[si]], writes=[rxq[xi]])
                xq = xqb[xi]
                rope_tables(st, (pi_t, pf_t, pa_t, r_rope), posq, t0, W, cs_t, r_cs)
                for s in range(4):
                    wi = wcnt[0] % 2
                    wcnt[0] += 1
                    ld("sp", wsl[wi][:], wq_s[s].rearrange("p k c -> p (k c)"), [rws[wi]])
                    wv_ = wsl[wi][:].rearrange("p (k c) -> p k c", c=512)
                    for c in range(4):
                        hm = s * 4 + c
                        pb = next_ps()
                        for kc in range(32):
                            S.op("pe", lambda e, wv_=wv_, kc=kc, c=c, pb=pb, xq=xq: e.matmul(
                                ps[pb][:, 0:W], wv_[:, kc, c * 128:(c + 1) * 128], xq[:, kc, :],
                                start=(kc == 0), stop=(kc == 31)), reads=[rws[wi], rxq[xi]], writes=[rps[pb]])
                        stage_out_q(pb, 128, qTda[hm, :, t0:t0 + W], SC_DA)
                for s in range(2):
                    wi = wcnt[0] % 2
                    wcnt[0] += 1
                    ld("sp", wsl[wi][:], wcq_s[s].rearrange("p k c -> p (k c)"), [rws[wi]])
                    wv_ = wsl[wi][:].rearrange("p (k c) -> p k c", c=512)
                    for c in range(4):
                        cc = s * 4 + c
                        pb = next_ps()
                        for kc in range(32):
                            S.op("pe", lambda e, wv_=wv_, kc=kc, c=c, pb=pb, xq=xq: e.matmul(
                                ps[pb][:, 0:W], wv_[:, kc, c * 128:(c + 1) * 128], xq[:, kc, :],
                                start=(kc == 0), stop=(kc == 31)), reads=[rws[wi], rxq[xi]], writes=[rps[pb]])
                        S.op("dve", lambda e, cc=cc, pb=pb: e.tensor_copy(out=cq[:, cc, :], in_=ps[pb][:, 0:W]),
                             reads=[rps[pb]], writes=[r_cq])
                        S.op("act", lambda e, cc=cc, pb=pb: e.activation(out=sq[:, cc, :], in_=ps[pb][:, 0:W],
                                                                         func=AF.Square),
                             reads=[rps[pb]], writes=[r_sq])
                pb = next_ps()
                for c in range(8):
                    S.op("pe", lambda e, c=c, pb=pb: e.matmul(ps[pb][:, 0:W], ones_b[:, :], sq[:, c, :], start=(c == 0),
                                                             stop=(c == 7)), reads=[r_sq, r_const], writes=[rps[pb]])
                S.op("act", lambda e, pb=pb: e.activation(out=rstd[:], in_=ps[pb][:, 0:W], func=AF.Sqrt,
                                                          bias=eps_t[:, 0:1], scale=1.0 / 1024.0),
                     reads=[rps[pb], r_const], writes=[r_rstd])
                S.op("dve", lambda e: e.reciprocal(out=rstd[:], in_=rstd[:]), reads=[r_rstd], writes=[r_rstd])
                for c in range(8):
                    S.op("dve", lambda e, c=c: e.scalar_tensor_tensor(out=cqn[:, c, :], in0=cq[:, c, :],
                                                                      scalar=vcol("qn_g", c), in1=rstd[:],
                                                                      op0=ALU.mult, op1=ALU.mult),
                         reads=[r_cq, r_rstd, r_const], writes=[r_cqn])
                for half in range(2):
                    wi = wcnt[0] % 2
                    wcnt[0] += 1
                    ld("sp", wsl[wi][:, 0:8 * 1536].rearrange("p (g f) -> p g f", g=8),
                       wuq_s[half * 8:half * 8 + 8].rearrange("g p k c -> p g (k c)"), [rws[wi]])
                    wu = wsl[wi][:, 0:8 * 1536].rearrange("p (g k c) -> p g k c", g=8, k=8)
                    for hh in range(8):
                        h = half * 8 + hh
                        pb = next_ps()
                        for kc in range(8):
                            S.op("pe", lambda e, hh=hh, kc=kc, pb=pb, wu=wu: e.matmul(
                                ps[pb][:, 0:W], wu[:, hh, kc, 0:128], cqn[:, kc, :], start=(kc == 0), stop=(kc == 7)),
                                 reads=[rws[wi], r_cqn], writes=[rps[pb]])
                        stage_out_q(pb, 128, qTm[h, :, t0:t0 + W], SC_M)
                        pb = next_ps()
                        for kc in range(8):
                            S.op("pe", lambda e, hh=hh, kc=kc, pb=pb, wu=wu: e.matmul(
                                ps[pb][0:64, 0:W], wu[:, hh, kc, 128:192], cqn[:, kc, :], start=(kc == 0),
                                stop=(kc == 7)), reads=[rws[wi], r_cqn], writes=[rps[pb]])
                        S.op("dve", lambda e, pb=pb: e.tensor_copy(out=qrf[:], in_=ps[pb][0:64, 0:W]), reads=[rps[pb]],
                             writes=[r_qr])
                        S.op("act", lambda e, pb=pb: e.copy(out=qrb[:], in_=ps[pb][0:64, 0:W]), reads=[rps[pb]],
                             writes=[r_qr])
                        pb2 = next_ps()
                        S.op("pe", lambda e, pb2=pb2: e.matmul(ps[pb2][0:64, 0:W], rot_t[:, :], qrb[:, :], start=True,
                                                               stop=True), reads=[r_qr, r_const], writes=[rps[pb2]])
                        S.op("dve", lambda e: e.scalar_tensor_tensor(out=qrf[:], in0=qrf[:], scalar=SC_M,
                                                                     in1=cs_t[:, 0, :], op0=ALU.mult, op1=ALU.mult),
                             reads=[r_qr, r_cs], writes=[r_qr])
                        S.op("dve", lambda e, pb2=pb2: e.scalar_tensor_tensor(out=qro[:], in0=ps[pb2][0:64, 0:W],
                                                                              scalar=SC_M, in1=cs_t[:, 1, :],
                                                                              op0=ALU.mult, op1=ALU.mult),
                             reads=[rps[pb2], r_cs], writes=[r_qr])
                        k = scnt[0] % 4
                        scnt[0] += 1
                        S.op("dve", lambda e, k=k: e.tensor_tensor(out=stgb[k][0:64, :], in0=qrf[:], in1=qro[:],
                                                                   op=ALU.add), reads=[r_qr], writes=[rstg[k]])
                        ld("pool", qTr[h, :, t0:t0 + W], stgb[k][0:64, :], [], reads=[rstg[k]])
            S.flush()

        if upto == "B":
            return nc
        with ExitStack() as st:
            kbuf = [sb(st, "kbuf%d" % i, [128, NKV], BF16) for i in range(2)]
            rk = [Res() for _ in range(2)]
            vbuf = [sb(st, "vbuf%d" % i, [128, NB * 256], BF16) for i in range(2)]
            rv = [Res() for _ in range(2)]
            krb_t = sb(st, "krbuf", [64, NKV], BF16)
            r_kr = Res()
            qbuf = [sb(st, "qbuf%d" % i, [128, NQ], BF16) for i in range(2)]
            rq = [Res() for _ in range(2)]
            qrbuf = [sb(st, "qrbuf%d" % i, [64, NQ], BF16) for i in range(2)]
            alqb = [sb(st, "alqb%d" % i, [3, NQ], BF16) for i in range(1)]
            r_alq = Res()
            alk_t = sb(st, "alk_t", [3, 8, 128], BF16)
            mask_t = sb(st, "mask_t", [128, NMASK, W], BF16)
            NPT = 8
            pt = [sb(st, "pt%d" % i, [128, W], BF16) for i in range(NPT)]
            rpt = [Res() for _ in range(NPT)]
            stmp = [sb(st, "stmp%d" % i, [128, W], F32) for i in range(2)]
            rstmp = [Res() for _ in range(2)]
            stcnt = [0]
            pcnt = [0]
            rden = sb(st, "rden", [128, W], F32)
            o1 = sb(st, "o1", [128, 2, W], F32)
            od = sb(st, "od", [128, 2, W], F32)
            osq = sb(st, "osq", [128, 2, W], BF16)
            orst = sb(st, "orst", [128, W], F32)
            ostg = [sb(st, "ostg%d" % i, [128, 2, W], BF16) for i in range(2)]
            rostg = [Res() for _ in range(2)]
            ocnt = [0]
            r_rden, r_o1, r_od, r_osq, r_orst, r_catt = Res(), Res(), Res(), Res(), Res(), Res()
            kval_t = sb(st, "kval_t", [128, NB], F32)
            abias_t = sb(st, "abias_t", [128, 8, NT, NB], F32)
            ld("sp", kval_t[:], kvalid, [r_catt])
            ld("sp", abias_t[:], abias, [r_catt])
            ld("sp", alk_t[:], alk, [r_catt])
            ld("sp", mask_t[:], masks_d, [r_catt])
            ld("sp", krb_t[:], kTr, [r_kr])
            bg = late_weights(make_converter(st, 3, 2, ("pool",), "c")) if BG_CONVERT else iter(())

            def bg_step(n):
                for _ in range(n):
                    if next(bg, "done") == "done":
                        return
            S_BANKS = (0, 1, 2, 3)
            O_SETS = ((4, 5, 6), (4, 5, 6))
            LN_BANK = 7
            SK = 3
            scnt = [0]
            setcnt = [0]
            kcnt = [0]
            vcnt = [0]
            qcnt = [0]

            def sweep(i, kT, rkres, qT, rqres, extra, V, rvres, ne, esz, biasfn, oset):
                blocks = tile_blocks(i)
                nblk = len(blocks)
                pks = {}
                for bi in range(nblk + SK):
                    if bi < nblk:
                        rb, full = blocks[bi]
                        sbk = S_BANKS[scnt[0] % len(S_BANKS)]
                        scnt[0] += 1
                        S.op("pe", lambda e, sbk=sbk, rb=rb: e.matmul(ps[sbk][:, 0:W], kT[:, rb * 128:(rb + 1) * 128],
                                                                       qT[:, i * W:(i + 1) * W], start=True,
                                                                       stop=(extra is None)),
                             reads=[rkres, rqres], writes=[rps[sbk]])
                        if extra is not None:
                            S.op("pe", lambda e, sbk=sbk, rb=rb: e.matmul(ps[sbk][:, 0:W], extra[0](rb),
                                                                           extra[1][:, i * W:(i + 1) * W],
                                                                           start=False, stop=True),
                                 reads=list(extra[2]), writes=[rps[sbk]])
                        pk = pcnt[0] % NPT
                        pcnt[0] += 1
                        pks[bi] = pk
                        if full:
                            S.op("act", lambda e, sbk=sbk, rb=rb, pk=pk: e.activation(
                                out=pt[pk][:], in_=ps[sbk][:, 0:W], func=AF.Exp, bias=biasfn(rb), scale=1.0),
                                 reads=[rps[sbk], r_catt], writes=[rpt[pk]])
                        else:
                            mi = MASK_IDX[(i, rb)]
                            tk = stcnt[0] % 2
                            stcnt[0] += 1
                            S.op("dve", lambda e, sbk=sbk, mi=mi, tk=tk: e.tensor_tensor(
                                out=stmp[tk][:], in0=ps[sbk][:, 0:W], in1=mask_t[:, mi, :], op=ALU.add),
                                 reads=[rps[sbk], r_catt], writes=[rstmp[tk]])
                            S.op("act", lambda e, tk=tk, rb=rb, pk=pk: e.activation(
                                out=pt[pk][:], in_=stmp[tk][:], func=AF.Exp, bias=biasfn(rb), scale=1.0),
                                 reads=[rstmp[tk], r_catt], writes=[rpt[pk]])
                    if bi % BG_EVERY == BG_EVERY - 1:
                        bg_step(1)
                    bj = bi - SK
                    if bj >= 0:
                        rb, full = blocks[bj]
                        pk = pks[bj]
                        for e_ in range(ne):
                            S.op("pe", lambda e, e_=e_, rb=rb, pk=pk, bj=bj: e.matmul(
                                ps[oset[e_]][:, 0:W], V[:, rb * esz + e_ * 128: rb * esz + (e_ + 1) * 128], pt[pk][:],
                                start=(bj == 0), stop=(bj == nblk - 1)), reads=[rvres, rpt[pk]],
                                 writes=[rps[oset[e_]]])
                        S.op("pe", lambda e, pk=pk, bj=bj: e.matmul(ps[oset[2]][:, 0:W], ones_b[:, :], pt[pk][:],
                                                                    start=(bj == 0), stop=(bj == nblk - 1)),
                             reads=[rpt[pk], r_const], writes=[rps[oset[2]]])
                S.op("dve", lambda e: e.tensor_scalar(out=rden[:], in0=ps[oset[2]][:, 0:W], scalar1=1e-30, scalar2=None,
                                                      op0=ALU.add), reads=[rps[oset[2]]], writes=[r_rden])
                S.op("dve", lambda e: e.reciprocal(out=rden[:], in_=rden[:]), reads=[r_rden], writes=[r_rden])

            for h in range(8):
                vi = vcnt[0] % 2
                vcnt[0] += 1
                ld("sp", vbuf[vi][:], vda[h].rearrange("p b e -> p (b e)"), [rv[vi]])
                for m in range(2):
                    hm = 2 * h + m
                    ki = kcnt[0] % 2
                    kcnt[0] += 1
                    ld("sp", kbuf[ki][:], kTda[hm], [rk[ki]])
                    qi = qcnt[0] % 2
                    qcnt[0] += 1
                    ld("sp", qbuf[qi][:], qTda[hm], [rq[qi]])
                    if h < ALIBI_MM_HEADS:
                        ld("sp", alqb[0][:], alq[h], [r_alq])
                    for i in range(NT):
                        oset = O_SETS[setcnt[0] % 2]
                        setcnt[0] += 1
                        sweep(i, kbuf[ki], rk[ki], qbuf[qi], rq[qi],
                              ((lambda rb, h=h: alk_t[:, h, :], alqb[0], (r_catt, r_alq))
                               if h < ALIBI_MM_HEADS else None),
                              vbuf[vi], rv[vi], 2, 256, (lambda rb, h=h, i=i: abias_t[:, h, i, rb:rb + 1]), oset)
                        if m == 0:
                            for e_ in range(2):
                                S.op("dve", lambda e, e_=e_, i=i, oset=oset: e.tensor_tensor(
                                    out=o1[:, e_, :], in0=ps[oset[e_]][:, 0:W], in1=rden[:], op=ALU.mult),
                                     reads=[rps[oset[e_]], r_rden], writes=[r_o1])
                        else:
                            pass
                        if m == 0:
                            ld("pool", yscr[0:2, :, i * W:(i + 1) * W].rearrange("c p n -> p c n"), o1[:], [],
                               reads=[r_o1])
                        else:
                            ld("sp", o1[:], yscr[0:2, :, i * W:(i + 1) * W].rearrange("c p n -> p c n"), [r_o1])
                            for e_ in range(2):
                                S.op("dve", lambda e, e_=e_, oset=oset: e.tensor_tensor(
                                    out=od[:, e_, :], in0=ps[oset[e_]][:, 0:W], in1=rden[:], op=ALU.mult),
                                     reads=[rps[oset[e_]], r_rden], writes=[r_od])
                                S.op("dve", lambda e, e_=e_: e.scalar_tensor_tensor(
                                    out=od[:, e_, :], in0=od[:, e_, :], scalar=nlam_t[:, 0:1], in1=o1[:, e_, :],
                                    op0=ALU.mult, op1=ALU.add), reads=[r_od, r_o1, r_const], writes=[r_od])
                                S.op("act", lambda e, e_=e_: e.activation(out=osq[:, e_, :], in_=od[:, e_, :],
                                                                          func=AF.Square), reads=[r_od], writes=[r_osq])
                            sbk = LN_BANK
                            for e_ in range(2):
                                S.op("pe", lambda e, e_=e_, sbk=sbk: e.matmul(ps[sbk][:, 0:W], ones_b[:, :], osq[:, e_, :],
                                                                             start=(e_ == 0), stop=(e_ == 1)),
                                     reads=[r_osq, r_const], writes=[rps[sbk]])
                            S.op("act", lambda e, sbk=sbk: e.activation(out=orst[:], in_=ps[sbk][:, 0:W], func=AF.Sqrt,
                                                                        bias=eps_t[:, 0:1], scale=1.0 / 256.0),
                                 reads=[rps[sbk], r_const], writes=[r_orst])
                            S.op("dve", lambda e: e.reciprocal(out=orst[:], in_=orst[:]), reads=[r_orst],
                                 writes=[r_orst])
                            ok = ocnt[0] % 2
                            ocnt[0] += 1
                            for e_ in range(2):
                                S.op("dve", lambda e, e_=e_, ok=ok: e.scalar_tensor_tensor(
                                    out=ostg[ok][:, e_, :], in0=od[:, e_, :], scalar=sgs_t[:, e_:e_ + 1], in1=orst[:],
                                    op0=ALU.mult, op1=ALU.mult), reads=[r_od, r_orst, r_const], writes=[rostg[ok]])
                            ld("pool", oaT[2 * h:2 * h + 2, :, i * W:(i + 1) * W].rearrange("c p n -> p c n"),
                               ostg[ok][:], [], reads=[rostg[ok]])
            for h in range(16):
                vi = vcnt[0] % 2
                vcnt[0] += 1
                ld("sp", vbuf[vi][:, 0:NB * 128], vm[h].rearrange("p b e -> p (b e)"), [rv[vi]])
                ki = kcnt[0] % 2
                kcnt[0] += 1
                ld("sp", kbuf[ki][:], kTm[h], [rk[ki]])
                qi = qcnt[0] % 2
                qcnt[0] += 1
                ld("sp", qbuf[qi][:], qTm[h], [rq[qi]])
                ld("sp", qrbuf[qi][:], qTr[h], [rq[qi]])
                for i in range(NT):
                    oset = O_SETS[setcnt[0] % 2]
                    setcnt[0] += 1
                    oset2 = (oset[0], oset[1], oset[2])
                    sweep(i, kbuf[ki], rk[ki], qbuf[qi], rq[qi],
                          (lambda rb: krb_t[:, rb * 128:(rb + 1) * 128], qrbuf[qi], (r_kr, rq[qi])),
                          vbuf[vi], rv[vi], 1, 128, (lambda rb: kval_t[:, rb:rb + 1]), oset2)
                    ok = ocnt[0] % 2
                    ocnt[0] += 1
                    S.op("dve", lambda e, ok=ok, oset=oset: e.tensor_tensor(out=ostg[ok][:, 0, :], in0=ps[oset[0]][:, 0:W],
                                                                            in1=rden[:], op=ALU.mult),
                         reads=[rps[oset[0]], r_rden], writes=[rostg[ok]])
                    ld("pool", obT[h, :, i * W:(i + 1) * W], ostg[ok][:, 0, :], [], reads=[rostg[ok]])
            bg_step(1 << 30)
            S.flush()

        if upto == "C":
            return nc
        with ExitStack() as st:
            oab = sb(st, "oab", [128, 16, W], BF16)
            obb = sb(st, "obb", [128, 16, W], BF16)
            xqb = sb(st, "xqd", [128, 32, W], BF16)
            xst = [sb(st, "xsd%d" % i, [128, 4, W], F32) for i in range(2)]
            rxst = [Res() for _ in range(2)]
            r_oa, r_ob, r_xq = Res(), Res(), Res()
            wsl = [sb(st, "wsd%d" % i, [128, 8192], BF16) for i in range(4)]
            rws = [Res() for _ in range(4)]
            wcnt = [0]
            mg = sb(st, "mg", [128, 32, W], BF16)
            r_mg = Res()
            gS = [sb(st, "gS%d" % i, [128, 2, W], F32) for i in range(1)]
            tS = [sb(st, "tS%d" % i, [128, 2, W], F32) for i in range(1)]
            gS2 = [sb(st, "gSb%d" % i, [128, 2, W], F32) for i in range(1)]
            uS = [sb(st, "uS%d" % i, [128, 2, W], F32) for i in range(1)]
            rgS = [Res() for _ in range(2)]
            rtS = [Res() for _ in range(2)]
            rgS2 = [Res() for _ in range(2)]
            ruS = [Res() for _ in range(2)]
            xres = [sb(st, "xres%d" % i, [128, W], F32) for i in range(3)]
            rxres = [Res() for _ in range(3)]
            ych = [sb(st, "ych%d" % i, [128, W], F32) for i in range(3)]
            rych = [Res() for _ in range(3)]
            ybf = [sb(st, "ybf%d" % i, [128, 2, W], BF16) for i in range(3)]
            rybf = [Res() for _ in range(3)]
            mean = sb(st, "mean", [128, W], F32)
            rstd = sb(st, "rstdd", [128, W], F32)
            msq = sb(st, "msq", [128, W], F32)
            r_stat = Res()
            hch = [sb(st, "hch%d" % i, [128, W], F32) for i in range(2)]
            hcb = [sb(st, "hcb%d" % i, [128, W], BF16) for i in range(2)]
            rhch = [Res() for _ in range(2)]
            r_y = [[Res() for _ in range(32)] for _ in range(NT)]
            PS_M1, PS_M2 = 6, 7

            def dps():
                b = psn[0] % 6
                psn[0] += 1
                return b

            def slab(scr_g, n):
                wi = wcnt[0] % 4
                wcnt[0] += 1
                ld("sp", wsl[wi][:, 0:n], scr_g.rearrange("p k c -> p (k c)"), [rws[wi]])
                return wi, wsl[wi][:, 0:n].rearrange("p (k c) -> p k c", c=256)

            def mm_chunk(wi, wv_, c, KC, rhs, rrhs):
                pb = dps()
                for kc in range(KC):
                    S.op("pe", lambda e, kc=kc: e.matmul(ps[pb][:, 0:W], wv_[:, kc, c * 128:(c + 1) * 128],
                                                         rhs[:, kc, :], start=(kc == 0), stop=(kc == KC - 1)),
                         reads=[rws[wi], rrhs], writes=[rps[pb]])
                return pb

            for i in range(NT):
                t0 = i * W
                ld("sp", oab[:], oaT[:, :, t0:t0 + W].rearrange("c p n -> p c n"), [r_oa])
                ld("sp", obb[:], obT[:, :, t0:t0 + W].rearrange("c p n -> p c n"), [r_ob])
                for q8 in range(8):
                    si = q8 % 2
                    ld("sp", xst[si][:], xqT[q8 * 512:(q8 + 1) * 512, t0:t0 + W].rearrange("(k p) n -> p k n", p=128),
                       [rxst[si]])
                    eng = "dve" if q8 % 2 == 0 else "pool"
                    S.op(eng, lambda e, si=si, q8=q8: e.tensor_copy(out=xqb[:, q8 * 4:(q8 + 1) * 4, :], in_=xst[si][:]),
                         reads=[rxst[si]], writes=[r_xq])
                for s in range(16):
                    p2 = 0
                    wi, wv_ = slab(wg_s[s], 32 * 256)
                    for c in range(2):
                        mc = 2 * s + c
                        pb = mm_chunk(wi, wv_, c, 32, xqb, r_xq)
                        S.op("act", lambda e, pb=pb, mc=mc, c=c, p2=p2: e.activation(
                            out=gS[p2][:, c, :], in_=ps[pb][:, 0:W], func=AF.Sigmoid, bias=vcol("b_gate", mc),
                            scale=1.0), reads=[rps[pb], r_const], writes=[rgS[p2]])
                    wi, wv_ = slab(wpa_s[s], 16 * 256)
                    for c in range(2):
                        pb = mm_chunk(wi, wv_, c, 16, oab, r_oa)
                        S.op("dve", lambda e, pb=pb, c=c, p2=p2: e.tensor_tensor(
                            out=tS[p2][:, c, :], in0=gS[p2][:, c, :], in1=ps[pb][:, 0:W], op=ALU.mult),
                             reads=[rps[pb], rgS[p2]], writes=[rtS[p2]])
                    wi, wv_ = slab(wg_s[16 + s], 32 * 256)
                    for c in range(2):
                        mc = 2 * s + c
                        pb = mm_chunk(wi, wv_, c, 32, xqb, r_xq)
                        S.op("act", lambda e, pb=pb, mc=mc, c=c, p2=p2: e.activation(
                            out=gS2[p2][:, c, :], in_=ps[pb][:, 0:W], func=AF.Sigmoid, bias=vcol("b_gate", 32 + mc),
                            scale=1.0), reads=[rps[pb], r_const], writes=[rgS2[p2]])
                    wi, wv_ = slab(wpb_s[s], 16 * 256)
                    for c in range(2):
                        mc = 2 * s + c
                        pb = mm_chunk(wi, wv_, c, 16, obb, r_ob)
                        S.op("dve", lambda e, pb=pb, c=c, p2=p2: e.tensor_tensor(
                            out=uS[p2][:, c, :], in0=gS2[p2][:, c, :], in1=ps[pb][:, 0:W], op=ALU.mult),
                             reads=[rps[pb], rgS2[p2]], writes=[ruS[p2]])
                        S.op("pool", lambda e, c=c, p2=p2, mc=mc: e.tensor_tensor(
                            out=mg[:, mc, :], in0=tS[p2][:, c, :], in1=uS[p2][:, c, :], op=ALU.add),
                             reads=[rtS[p2], ruS[p2]], writes=[r_mg])
                for s in range(16):
                    wi, wv_ = slab(wout_s[s], 32 * 256)
                    for c in range(2):
                        mc = 2 * s + c
                        po = mm_chunk(wi, wv_, c, 32, mg, r_mg)
                        xi = mc % 3
                        ld("sp", xres[xi][:], xqT[mc * 128:(mc + 1) * 128, t0:t0 + W], [rxres[xi]])
                        S.op("dve", lambda e, xi=xi, po=po: e.scalar_tensor_tensor(
                            out=ych[xi][:], in0=xres[xi][:], scalar=ALPHA, in1=ps[po][:, 0:W], op0=ALU.mult,
                            op1=ALU.add), reads=[rxres[xi], rps[po]], writes=[rych[xi]])
                        S.op("act", lambda e, xi=xi: e.copy(out=ybf[xi][:, 0, :], in_=ych[xi][:]), reads=[rych[xi]],
                             writes=[rybf[xi]])
                        S.op("act", lambda e, xi=xi: e.activation(out=ybf[xi][:, 1, :], in_=ych[xi][:], func=AF.Square),
                             reads=[rych[xi]], writes=[rybf[xi]])
                        S.op("pe", lambda e, xi=xi, mc=mc: e.matmul(ps[PS_M1][:, 0:W], ones_b[:, :], ybf[xi][:, 0, :],
                                                                    start=(mc == 0), stop=(mc == 31)),
                             reads=[rybf[xi], r_const], writes=[rps[PS_M1]])
                        S.op("pe", lambda e, xi=xi, mc=mc: e.matmul(ps[PS_M2][:, 0:W], ones_b[:, :], ybf[xi][:, 1, :],
                                                                    start=(mc == 0), stop=(mc == 31)),
                             reads=[rybf[xi], r_const], writes=[rps[PS_M2]])
                        ld("pool", yscr[mc, :, t0:t0 + W], ych[xi][:], [r_y[i][mc]], reads=[rych[xi]])
                S.op("dve", lambda e: e.tensor_scalar(out=mean[:], in0=ps[PS_M1][:, 0:W], scalar1=1.0 / D, scalar2=None,
                                                      op0=ALU.mult), reads=[rps[PS_M1]], writes=[r_stat])
                S.op("dve", lambda e: e.tensor_tensor(out=msq[:], in0=mean[:], in1=mean[:], op=ALU.mult),
                     reads=[r_stat], writes=[r_stat])
                S.op("dve", lambda e: e.scalar_tensor_tensor(out=rstd[:], in0=ps[PS_M2][:, 0:W], scalar=1.0 / D,
                                                             in1=msq[:], op0=ALU.mult, op1=ALU.subtract),
                     reads=[rps[PS_M2], r_stat], writes=[r_stat])
                S.op("act", lambda e: e.activation(out=rstd[:], in_=rstd[:], func=AF.Sqrt, bias=eps_t[:, 1:2],
                                                   scale=1.0), reads=[r_stat, r_const], writes=[r_stat])
                S.op("dve", lambda e: e.reciprocal(out=rstd[:], in_=rstd[:]), reads=[r_stat], writes=[r_stat])
                for mc in range(32):
                    xi = mc % 3
                    hi = mc % 2
                    ld("sp", ych[xi][:], yscr[mc, :, t0:t0 + W], [rych[xi]], reads=[r_y[i][mc]])
                    S.op("dve", lambda e, xi=xi: e.tensor_tensor(out=ych[xi][:], in0=ych[xi][:], in1=mean[:],
                                                                 op=ALU.subtract), reads=[rych[xi], r_stat],
                         writes=[rych[xi]])
                    S.op("pool", lambda e, xi=xi: e.tensor_tensor(out=ych[xi][:], in0=ych[xi][:], in1=rstd[:],
                                                                  op=ALU.mult), reads=[rych[xi], r_stat],
                         writes=[rych[xi]])
                    S.op("dve", lambda e, xi=xi, hi=hi, mc=mc: e.tensor_scalar(
                        out=hch[hi][:], in0=ych[xi][:], scalar1=vcol("ln1_g", mc), scalar2=vcol("ln1_b", mc),
                        op0=ALU.mult, op1=ALU.add), reads=[rych[xi], r_const], writes=[rhch[hi]])
                    S.op("act", lambda e, hi=hi: e.copy(out=hcb[hi][:], in_=hch[hi][:]), reads=[rhch[hi]],
                         writes=[rhch[hi]])
                    ld("pool", h1f[mc, :, t0:t0 + W], hch[hi][:], [], reads=[rhch[hi]])
                    ld("pool", h1b[mc, :, t0:t0 + W], hcb[hi][:], [], reads=[rhch[hi]])
            S.flush()

        if upto == "D":
            return nc
        with ExitStack() as st:
            hb = sb(st, "hb", [128, 32, W], BF16)
            r_hb = Res()
            wsl = [sb(st, "wse%d" % i, [128, 8192], BF16) for i in range(4)]
            rws = [Res() for _ in range(4)]
            wcnt = [0]
            abuf = sb(st, "abuf", [128, 86, W], BF16)
            r_a = Res()
            ucar = sb(st, "ucar", [128, 172, 2], F32)
            r_ucar = Res()
            ue = [sb(st, "ue%d" % i, [128, W + 2], F32) for i in range(4)]
            rue = [Res() for _ in range(4)]
            cgS = [sb(st, "cgS%d" % i, [128, 2, W], F32) for i in range(2)]
            rcgS = [Res() for _ in range(2)]
            cv = [sb(st, "cvv%d" % i, [128, W], F32) for i in range(2)]
            rcv = [Res() for _ in range(2)]
            hres = [sb(st, "hres%d" % i, [128, W], F32) for i in range(3)]
            rhres = [Res() for _ in range(3)]
            zch = [sb(st, "zch%d" % i, [128, W], F32) for i in range(3)]
            rzch = [Res() for _ in range(3)]
            zbf = [sb(st, "zbf%d" % i, [128, 2, W], BF16) for i in range(3)]
            rzbf = [Res() for _ in range(3)]
            mean = sb(st, "meane", [128, W], F32)
            rstd = sb(st, "rstde", [128, W], F32)
            msq = sb(st, "msqe", [128, W], F32)
            r_stat = Res()
            och = [sb(st, "och%d" % i, [128, W], F32) for i in range(2)]
            roch = [Res() for _ in range(2)]
            r_z = [[Res() for _ in range(32)] for _ in range(NT)]
            PS_M1, PS_M2 = 6, 7
            S.op("dve", lambda e: e.memset(ucar[:], 0.0), writes=[r_ucar])
            uecnt = [0]

            def dps():
                b = psn[0] % 6
                psn[0] += 1
                return b

            def slab(src2d, n, cw):
                wi = wcnt[0] % 4
                wcnt[0] += 1
                ld("sp", wsl[wi][:, 0:n], src2d, [rws[wi]])
                return wi, wsl[wi][:, 0:n].rearrange("p (k c) -> p k c", c=cw)

            def conv_chunk(pb, ch, out_ap, rout):
                k = uecnt[0] % 4
                uecnt[0] += 1
                S.op("pool", lambda e: e.tensor_copy(out=ue[k][:, 0:2], in_=ucar[:, ch, :]),
                     reads=[r_ucar], writes=[rue[k]])
                S.op("act", lambda e: e.copy(out=ue[k][:, 2:W + 2], in_=ps[pb][:, 0:W]), reads=[rps[pb]],
                     writes=[rue[k]])
                S.op("pool", lambda e: e.tensor_copy(out=ucar[:, ch, :], in_=ue[k][:, W:W + 2]),
                     reads=[rue[k]], writes=[r_ucar])
                S.op("dve", lambda e: e.tensor_scalar(out=out_ap, in0=ue[k][:, 2:W + 2],
                                                      scalar1=vcol("cw2", ch), scalar2=vcol("cb", ch),
                                                      op0=ALU.mult, op1=ALU.add),
                     reads=[rue[k], r_const], writes=[rout])
                S.op("dve", lambda e: e.scalar_tensor_tensor(out=out_ap, in0=ue[k][:, 1:W + 1],
                                                             scalar=vcol("cw1", ch), in1=out_ap,
                                                             op0=ALU.mult, op1=ALU.add),
                     reads=[rue[k], rout, r_const], writes=[rout])
                S.op("dve", lambda e: e.scalar_tensor_tensor(out=out_ap, in0=ue[k][:, 0:W],
                                                              scalar=vcol("cw0", ch), in1=out_ap,
                                                              op0=ALU.mult, op1=ALU.add),
                     reads=[rue[k], rout, r_const], writes=[rout])

            def up_chunk(wi, wv_, c):
                pb = dps()
                for kc in range(32):
                    S.op("pe", lambda e, kc=kc: e.matmul(ps[pb][:, 0:W], wv_[:, kc, c * 128:(c + 1) * 128],
                                                         hb[:, kc, :], start=(kc == 0), stop=(kc == 31)),
                         reads=[rws[wi], r_hb], writes=[rps[pb]])
                return pb

            for i in range(NT):
                t0 = i * W
                ld("sp", hb[:], h1b[:, :, t0:t0 + W].rearrange("c p n -> p c n"), [r_hb])
                if i == 0:
                    S.op("dve", lambda e: e.tensor_scalar(out=hb[:, :, 0:2], in0=hb[:, :, 0:2], scalar1=hv_t[:, 0:1],
                                                          scalar2=None, op0=ALU.mult), reads=[r_hb, r_const],
                         writes=[r_hb])
                for s in range(43):
                    p2 = s % 2
                    wi, wv_ = slab(wupg_s[s].rearrange("p k c -> p (k c)"), 32 * 256, 256)
                    for c in range(2):
                        fc = 2 * s + c
                        pb = up_chunk(wi, wv_, c)
                        conv_chunk(pb, fc, cgS[p2][:, c, :], rcgS[p2])
                        S.op("act", lambda e, c=c, p2=p2: e.activation(out=cgS[p2][:, c, :], in_=cgS[p2][:, c, :],
                                                                       func=AF.Silu), reads=[rcgS[p2]],
                             writes=[rcgS[p2]])
                    wi, wv_ = slab(wupv_s[s].rearrange("p k c -> p (k c)"), 32 * 256, 256)
                    for c in range(2):
                        fc = 2 * s + c
                        pb = up_chunk(wi, wv_, c)
                        ci = fc % 2
                        conv_chunk(pb, 86 + fc, cv[ci][:], rcv[ci])
                        S.op("dve", lambda e, c=c, p2=p2, ci=ci, fc=fc: e.tensor_tensor(
                            out=abuf[:, fc, :], in0=cgS[p2][:, c, :], in1=cv[ci][:], op=ALU.mult),
                             reads=[rcgS[p2], rcv[ci]], writes=[r_a])
                if adbg is not None and i == 1:
                    ld("pool", adbg.rearrange("c p n -> p c n"), abuf[:], [], reads=[r_a])
                for mc in range(32):
                    po = dps()
                    for half in range(2):
                        wi, vd = slab(wdn_s[mc, :, half * 43:(half + 1) * 43, :].rearrange("p k c -> p (k c)"),
                                      43 * 128, 128)
                        for kk in range(43):
                            kc = half * 43 + kk
                            S.op("pe", lambda e, kk=kk, kc=kc, vd=vd, po=po: e.matmul(
                                ps[po][:, 0:W], vd[:, kk, :], abuf[:, kc, :], start=(kc == 0), stop=(kc == 85)),
                                 reads=[rws[wi], r_a], writes=[rps[po]])
                    xi = mc % 3
                    ld("sp", hres[xi][:], h1f[mc, :, t0:t0 + W], [rhres[xi]])
                    S.op("dve", lambda e, xi=xi, po=po: e.scalar_tensor_tensor(
                        out=zch[xi][:], in0=hres[xi][:], scalar=ALPHA, in1=ps[po][:, 0:W], op0=ALU.mult, op1=ALU.add),
                         reads=[rhres[xi], rps[po]], writes=[rzch[xi]])
                    S.op("act", lambda e, xi=xi: e.copy(out=zbf[xi][:, 0, :], in_=zch[xi][:]), reads=[rzch[xi]],
                         writes=[rzbf[xi]])
                    S.op("act", lambda e, xi=xi: e.activation(out=zbf[xi][:, 1, :], in_=zch[xi][:], func=AF.Square),
                         reads=[rzch[xi]], writes=[rzbf[xi]])
                    S.op("pe", lambda e, xi=xi, mc=mc: e.matmul(ps[PS_M1][:, 0:W], ones_b[:, :], zbf[xi][:, 0, :],
                                                                start=(mc == 0), stop=(mc == 31)),
                         reads=[rzbf[xi], r_const], writes=[rps[PS_M1]])
                    S.op("pe", lambda e, xi=xi, mc=mc: e.matmul(ps[PS_M2][:, 0:W], ones_b[:, :], zbf[xi][:, 1, :],
                                                                start=(mc == 0), stop=(mc == 31)),
                         reads=[rzbf[xi], r_const], writes=[rps[PS_M2]])
                    ld("pool", zscr[mc, :, t0:t0 + W], zch[xi][:], [r_z[i][mc]], reads=[rzch[xi]])
                S.op("dve", lambda e: e.tensor_scalar(out=mean[:], in0=ps[PS_M1][:, 0:W], scalar1=1.0 / D, scalar2=None,
                                                      op0=ALU.mult), reads=[rps[PS_M1]], writes=[r_stat])
                S.op("dve", lambda e: e.tensor_tensor(out=msq[:], in0=mean[:], in1=mean[:], op=ALU.mult),
                     reads=[r_stat], writes=[r_stat])
                S.op("dve", lambda e: e.scalar_tensor_tensor(out=rstd[:], in0=ps[PS_M2][:, 0:W], scalar=1.0 / D,
                                                             in1=msq[:], op0=ALU.mult, op1=ALU.subtract),
                     reads=[rps[PS_M2], r_stat], writes=[r_stat])
                S.op("act", lambda e: e.activation(out=rstd[:], in_=rstd[:], func=AF.Sqrt, bias=eps_t[:, 1:2],
                                                   scale=1.0), reads=[r_stat, r_const], writes=[r_stat])
                S.op("dve", lambda e: e.reciprocal(out=rstd[:], in_=rstd[:]), reads=[r_stat], writes=[r_stat])
                lo = 2 if i == 0 else 0
                for mc in range(32):
                    xi = mc % 3
                    oi = mc % 2
                    ld("sp", zch[xi][:], zscr[mc, :, t0:t0 + W], [rzch[xi]], reads=[r_z[i][mc]])
                    S.op("dve", lambda e, xi=xi: e.tensor_tensor(out=zch[xi][:], in0=zch[xi][:], in1=mean[:],
                                                                 op=ALU.subtract), reads=[rzch[xi], r_stat],
                         writes=[rzch[xi]])
                    S.op("pool", lambda e, xi=xi: e.tensor_tensor(out=zch[xi][:], in0=zch[xi][:], in1=rstd[:],
                                                                  op=ALU.mult), reads=[rzch[xi], r_stat],
                         writes=[rzch[xi]])
                    S.op("dve", lambda e, xi=xi, oi=oi, mc=mc: e.tensor_scalar(
                        out=och[oi][:], in0=zch[xi][:], scalar1=vcol("ln2_g", mc), scalar2=vcol("ln2_b", mc),
                        op0=ALU.mult, op1=ALU.add), reads=[rzch[xi], r_const], writes=[roch[oi]])
                    ld("pool", outT[mc * 128:(mc + 1) * 128, t0 + lo - 2:t0 + W - 2], och[oi][:, lo:W], [],
                       reads=[roch[oi]])
            S.flush()
    return nc


_PROG = [None]
_DEBUG_HOOK = [None]


def _feat_cols(v):
    v = np.asarray(v, np.float32).reshape(-1)
    return np.ascontiguousarray(v.reshape(-1, 128).T)


def kernel(x, positions, w_in, b_gate, da_lambda_q1, da_lambda_k1, da_lambda_q2, da_lambda_k2, da_subln_g,
           mla_q_norm_g, mla_kv_norm_g, w_uq, w_ukv, w_proj_a, w_proj_b, w_out, ln1_g, ln1_b, w_up, conv_w, conv_b,
           w_down, ln2_g, ln2_b):
    x = np.asarray(x, np.float32)
    positions = np.asarray(positions, np.int32)
    bf = ml_dtypes.bfloat16
    if _PROG[0] is None:
        _PROG[0] = build_program()
    nc = _PROG[0]

    vecs = np.zeros((128, NV), np.float32)

    def put(name, v):
        c = _feat_cols(v)
        vecs[:, VEC[name]:VEC[name] + c.shape[1]] = c

    put("b_gate", b_gate[0]); put("ln1_g", ln1_g[0]); put("ln1_b", ln1_b[0]); put("ln2_g", ln2_g[0])
    put("ln2_b", ln2_b[0]); put("cw0", conv_w[0, 0]); put("cw1", conv_w[0, 1]); put("cw2", conv_w[0, 2])
    put("cb", conv_b[0]); put("subln", da_subln_g[0]); put("qn_g", mla_q_norm_g[0]); put("kvn_g", mla_kv_norm_g[0])
    lamv = np.concatenate([np.asarray(a, np.float32).reshape(-1) for a in
                           (da_lambda_q1[0], da_lambda_k1[0], da_lambda_q2[0], da_lambda_k2[0])])
    lamv = np.ascontiguousarray(np.broadcast_to(lamv[None, :], (128, 512)))
    inv = (np.float32(10000.0) ** (-np.arange(32, dtype=np.float32) / np.float32(32))).astype(np.float32)
    inv2 = np.concatenate([inv, inv]).reshape(64, 1).astype(np.float32)
    rot = np.zeros((64, 64), np.float32)
    for i in range(32):
        rot[32 + i, i] = -1.0
        rot[i, 32 + i] = 1.0
    slopes = 2.0 ** (-(np.arange(1, 9, dtype=np.float64)))
    alk = np.zeros((3, 8, 128), np.float32)
    alq = np.zeros((8, 3, NQ), np.float32)
    npr = np.arange(NQ) - (NQ - 1)
    a64 = np.floor(npr / 64.0) * 64.0
    b64 = npr - a64
    for h in range(8):
        alk[0, h, :] = 1.0
        alk[1, h, :] = 1.0
        alk[2, h, :] = -slopes[h] * np.arange(128)
        alq[h, 0, :] = -slopes[h] * a64
        alq[h, 1, :] = -slopes[h] * b64
        alq[h, 2, :] = 1.0
    masks = MASKS_NP.astype(bf)

    shared = {"w_in": np.ascontiguousarray(w_in[0], np.float32), "w_uq": np.ascontiguousarray(w_uq[0], np.float32),
              "w_ukv": np.ascontiguousarray(w_ukv[0], np.float32), "w_pa": np.ascontiguousarray(w_proj_a[0], np.float32),
              "w_pb": np.ascontiguousarray(w_proj_b[0], np.float32), "w_out": np.ascontiguousarray(w_out[0], np.float32),
              "w_up": np.ascontiguousarray(w_up[0], np.float32), "w_dn": np.ascontiguousarray(w_down[0], np.float32),
              "vecs": vecs, "lamv": lamv, "inv2": inv2, "rotm": rot.astype(bf), "alk": alk.astype(bf),
              "alq": alq.astype(bf), "masks": masks}
    in_maps = []
    for c in range(NCORE):
        b, j = c // 4, c % 4
        c0 = 2048 * j
        xb = x[b]
        pb = positions[b]
        xqT = np.zeros((D, NQ), np.float32)
        posq = np.zeros((NQ,), np.int32)
        lo = 2 if j == 0 else 0
        xqT[:, lo:] = xb[c0 - 2 + lo:c0 + 2048].T
        posq[lo:] = pb[c0 - 2 + lo:c0 + 2048]
        nvalid = c0 + 2048
        xkT = np.zeros((D, NKV), np.float32)
        posk = np.zeros((NKV,), np.int32)
        xkT[:, :nvalid] = xb[:nvalid][::-1].T
        posk[:nvalid] = pb[:nvalid][::-1]
        r = np.arange(NKV).reshape(NB, 128).T
        kvalid = np.where(r < nvalid, 0.0, NEG).astype(np.float32)
        abias = np.zeros((128, 8, NT, NB), np.float32)
        for h in range(8):
            for i in range(NT):
                if h < ALIBI_MM_HEADS:
                    abias[:, h, i, :] = kvalid + (-slopes[h] * 128.0 * np.arange(NB))[None, :]
                else:
                    nref = W * i + W - 1 - (NQ - 1)
                    abias[:, h, i, :] = kvalid - slopes[h] * (r.astype(np.float64) + nref)
        m = dict(shared)
        m.update({"xqT": xqT, "xkT": xkT,
                  "posq": np.ascontiguousarray(np.broadcast_to(posq[None, :], (64, NQ))),
                  "posk": np.ascontiguousarray(np.broadcast_to(posk[None, :], (64, NKV))),
                  "abias": abias.astype(np.float32), "kvalid": kvalid,
                  "hv": np.full((128, 1), 0.0 if j == 0 else 1.0, np.float32)})
        in_maps.append(m)
    if _DEBUG_HOOK[0] is not None:
        return _DEBUG_HOOK[0](in_maps)
    res = run_bass_kernel_spmd(nc, in_maps, core_ids=list(range(NCORE)))
    out = np.empty((2, SEQ, D), np.float32)
    for c in range(NCORE):
        b, j = c // 4, c % 4
        out[b, 2048 * j:2048 * (j + 1), :] = res.results[c]["outT"].T
    return out
```

```python
import math
import numpy as np
import ml_dtypes
from contextlib import ExitStack
import concourse.bass as bass
import concourse.mybir as mybir
from concourse.bass_utils import run_bass_kernel_spmd

F32 = mybir.dt.float32
BF16 = mybir.dt.bfloat16
I32 = mybir.dt.int32
AF = mybir.ActivationFunctionType
ALU = mybir.AluOpType

D = 4096
SEQ = 8192
NCORE = 8
W = 410
NT = 5
NQ = W * NT
NKV = 8192
NB = 64
DFF = 11008
LN_EPS = 1e-5
RMS_EPS = 1e-6
ALPHA = 2.0 ** 0.25
LAMBDA_INIT = 0.8 - 0.6 * math.exp(0.0)
OFF_Q, OFF_K, OFF_V, OFF_CQ, OFF_CKV, OFF_KR, OFF_G = 0, 2048, 4096, 6144, 7168, 7680, 7744
NEG = -30000.0
ALIBI_MM_HEADS = 2
BG_CONVERT = True
BG_EVERY = 6

ENGS = ("pe", "act", "dve", "pool", "sp")
NDMA = {"sp": 24, "pool": 24}


class Res:
    __slots__ = ("lw", "rd", "excl")

    def __init__(self, excl=False):
        self.lw = None
        self.rd = {}
        self.excl = excl


class Op:
    __slots__ = ("eng", "fn", "deps", "sig", "dma", "sem", "val", "dq")

    def __init__(self, eng, fn, dma):
        self.eng = eng
        self.fn = fn
        self.dma = dma
        self.deps = []
        self.sig = False
        self.sem = None
        self.val = 0
        self.dq = None


class Sched:
    def __init__(self, nc, stack):
        self.nc = nc
        self.ops = {e: [] for e in ENGS}
        self.esem = {e: stack.enter_context(nc.semaphore("es_" + e)) for e in ENGS}
        self.ecnt = {e: 0 for e in ENGS}
        self.dsem = {q: [stack.enter_context(nc.semaphore("ds_%s%d" % (q, i))) for i in range(n)]
                     for q, n in NDMA.items()}
        self.dcnt = {q: 0 for q in NDMA}
        self.dhist = {q: [] for q in NDMA}
        self.known = {e: {x: 0 for x in ENGS} for e in ENGS}
        self.dknown = {e: set() for e in ENGS}
        self.nops = 0

    def op(self, eng, fn, reads=(), writes=(), dma=False):
        o = Op(eng, fn, dma)
        deps = []
        if any(r.excl for r in reads):
            writes = list(writes) + [r for r in reads if r.excl]
            reads = [r for r in reads if not r.excl]
        for r in reads:
            if r.lw is not None:
                deps.append(r.lw)
        for w in writes:
            if w.lw is not None:
                deps.append(w.lw)
            deps.extend(w.rd.values())
        if dma:
            q = eng
            n = self.dcnt[q]
            P = len(self.dsem[q])
            o.dq = (q, n)
            if n >= P:
                deps.append(self.dhist[q][n - P])
            self.dhist[q].append(o)
            if len(self.dhist[q]) > 4 * P:
                pass
            self.dcnt[q] = n + 1
        seen = set()
        for d in deps:
            if eng == "pe" and d.eng == "pe" and not d.dma:
                continue
            if id(d) not in seen:
                seen.add(id(d))
                o.deps.append(d)
                d.sig = True
        for r in reads:
            if dma:
                r.rd[("d", id(o))] = o
            else:
                r.rd[eng] = o
        for w in writes:
            w.lw = o
            w.rd = {}
        self.ops[eng].append(o)
        self.nops += 1
        return o

    def flush(self):
        nc = self.nc
        for e in ENGS:
            c = self.ecnt[e]
            for o in self.ops[e]:
                if o.dma:
                    q, n = o.dq
                    P = len(self.dsem[q])
                    o.sem = self.dsem[q][n % P]
                    o.val = 16 * (n // P + 1)
                elif o.sig:
                    c += 1
                    o.sem = self.esem[e]
                    o.val = c
            self.ecnt[e] = c
        last_dma = []
        for q in NDMA:
            P = len(self.dsem[q])
            n = self.dcnt[q]
            for i in range(max(0, n - P), n):
                last_dma.append((self.dsem[q][i % P], 16 * (i // P + 1)))
        ecnt_final = {}
        for e in ENGS:
            lastc = None
            for o in reversed(self.ops[e]):
                if not o.dma:
                    lastc = o
                    break
            if lastc is not None and not lastc.sig:
                lastc.sig = True
                self.ecnt[e] += 1
                lastc.sem = self.esem[e]
                lastc.val = self.ecnt[e]
            ecnt_final[e] = self.ecnt[e]

        def emit_engine(e, eng):
            known = self.known[e]
            dknown = self.dknown[e]
            for o in self.ops[e]:
                for d in o.deps:
                    if d.dma:
                        key = (id(d.sem), d.val)
                        if key in dknown:
                            continue
                        dknown.add(key)
                        eng.wait_ge(d.sem, d.val)
                    else:
                        if known[d.eng] >= d.val:
                            continue
                        known[d.eng] = d.val
                        eng.wait_ge(d.sem, d.val)
                ins = o.fn(eng)
                if o.dma:
                    ins.then_inc(o.sem, 16)
                elif o.sig:
                    ins.then_inc(o.sem, 1)
            for x in ENGS:
                if known[x] < ecnt_final[x]:
                    known[x] = ecnt_final[x]
                    eng.wait_ge(self.esem[x], ecnt_final[x])
            for sem, val in last_dma:
                eng.wait_ge(sem, val)
            dknown.clear()

        with nc.Block() as block:
            @block.tensor
            def _(eng):
                emit_engine("pe", eng)

            @block.scalar
            def _(eng):
                emit_engine("act", eng)

            @block.vector
            def _(eng):
                emit_engine("dve", eng)

            @block.gpsimd
            def _(eng):
                emit_engine("pool", eng)

            @block.sync
            def _(eng):
                emit_engine("sp", eng)
        self.ops = {e: [] for e in ENGS}
        for q in NDMA:
            P = len(self.dsem[q])
            self.dhist[q] = self.dhist[q]


def tile_blocks(i):
    out = []
    nmin = W * i - (NQ - 1)
    nmax = W * i + W - 1 - (NQ - 1)
    for rb in range(NB):
        dmin = nmin + 128 * rb
        dmax = nmax + 128 * rb + 127
        if dmax < 0:
            continue
        out.append((rb, dmin >= 0))
    return out


def build_masks():
    masks = []
    index = {}
    for i in range(NT):
        for rb, full in tile_blocks(i):
            if not full:
                n = np.arange(W)[None, :] + W * i - (NQ - 1)
                ik = np.arange(128)[:, None]
                m = np.where((n + 128 * rb + ik) >= 0, 0.0, NEG).astype(np.float32)
                index[(i, rb)] = len(masks)
                masks.append(m)
    return np.stack(masks, axis=1), index


MASKS_NP, MASK_IDX = build_masks()
NMASK = MASKS_NP.shape[1]

VEC = {}
_o = 0
for _name, _n in (("b_gate", 64), ("ln1_g", 32), ("ln1_b", 32), ("ln2_g", 32), ("ln2_b", 32), ("cw0", 172),
                  ("cw1", 172), ("cw2", 172), ("cb", 172), ("subln", 2), ("qn_g", 8), ("kvn_g", 4)):
    VEC[_name] = _o
    _o += _n
NV = _o


def build_program(upto="E", dbg=()):
    nc = bass.Bass("TRN2", target_bir_lowering=False)
    dt = nc.dram_tensor

    def din(name, shape, dtype=F32):
        return dt(name, list(shape), dtype, kind="ExternalInput").ap()

    def dscr(name, shape, dtype=BF16):
        return dt(name, list(shape), dtype, kind=("ExternalOutput" if name in dbg else "Internal")).ap()

    xqT = din("xqT", [D, NQ])
    xkT = din("xkT", [D, NKV])
    posq = din("posq", [64, NQ], I32)
    posk = din("posk", [64, NKV], I32)
    w_in = din("w_in", [D, 15936])
    w_uq = din("w_uq", [1024, 3072])
    w_ukv = din("w_ukv", [512, 4096])
    w_pa = din("w_pa", [2048, D])
    w_pb = din("w_pb", [2048, D])
    w_out = din("w_out", [D, D])
    w_up = din("w_up", [D, 2 * DFF])
    w_dn = din("w_dn", [DFF, D])
    vecs = din("vecs", [128, NV])
    lamv = din("lamv", [128, 512])
    inv2 = din("inv2", [64, 1])
    rotm = din("rotm", [64, 64], BF16)
    alk = din("alk", [3, 8, 128], BF16)
    alq = din("alq", [8, 3, NQ], BF16)
    abias = din("abias", [128, 8, NT, NB])
    kvalid = din("kvalid", [128, NB])
    masks_d = din("masks", [128, NMASK, W], BF16)
    hv = din("hv", [128, 1])
    outT = dt("outT", [D, 2048], F32, kind="ExternalOutput").ap()

    def wscr(name, K, M, cw):
        ng = (M + cw - 1) // cw
        return dscr(name, [ng, 128, K // 128, cw])

    wq_s = wscr("wq_s", D, 2048, 512)
    wk_s = wscr("wk_s", D, 2048, 512)
    wv_s = wscr("wv_s", D, 2048, 512)
    wcq_s = wscr("wcq_s", D, 1024, 512)
    wckv_s = wscr("wckv_s", D, 512, 512)
    wkr_s = wscr("wkr_s", D, 64, 64)
    wg_s = wscr("wg_s", D, 8192, 256)
    wuq_s = wscr("wuq_s", 1024, 3072, 192)
    wukv_s = wscr("wukv_s", 512, 4096, 512)
    wpa_s = wscr("wpa_s", 2048, D, 256)
    wpb_s = wscr("wpb_s", 2048, D, 256)
    wout_s = wscr("wout_s", D, D, 256)
    wupg_s = wscr("wupg_s", D, DFF, 256)
    wupv_s = wscr("wupv_s", D, DFF, 256)
    wdn_s = wscr("wdn_s", DFF, D, 128)

    kTda = dscr("kTda", [16, 128, NKV])
    vda = dscr("vda", [8, 128, NB, 256])
    kTm = dscr("kTm", [16, 128, NKV])
    kTr = dscr("kTr", [64, NKV])
    vm = dscr("vm", [16, 128, NB, 128])
    qTda = dscr("qTda", [16, 128, NQ])
    qTm = dscr("qTm", [16, 128, NQ])
    qTr = dscr("qTr", [16, 64, NQ])
    oaT = dscr("oaT", [16, 128, NQ])
    obT = dscr("obT", [16, 128, NQ])
    yscr = dscr("yscr", [32, 128, NQ], F32)
    h1f = dscr("h1f", [32, 128, NQ], F32)
    h1b = dscr("h1b", [32, 128, NQ], BF16)
    zscr = dscr("zscr", [32, 128, NQ], F32)
    adbg = dscr("adbg", [86, 128, W], BF16) if "adbg" in dbg else None

    with ExitStack() as top:
        S = Sched(nc, top)
        sb = lambda st, name, shape, dtype: st.enter_context(nc.sbuf_tensor(name, list(shape), dtype))
        ps = [top.enter_context(nc.psum_tensor("ps%d" % i, [128, 512], F32)) for i in range(8)]
        rps = [Res(True) for _ in range(8)]
        psn = [0]

        def next_ps():
            b = psn[0] % 8
            psn[0] += 1
            return b

        vec_t = sb(top, "vec_t", [128, NV], F32)
        ones_b = sb(top, "ones_b", [128, 128], BF16)
        rot_t = sb(top, "rot_t", [64, 64], BF16)
        inv_t = sb(top, "inv_t", [64, 1], F32)
        hv_t = sb(top, "hv_t", [128, 1], F32)
        nlam_t = sb(top, "nlam_t", [128, 1], F32)
        sgs_t = sb(top, "sgs_t", [128, 2], F32)
        eps_t = sb(top, "eps_t", [128, 2], F32)
        r_const = Res()

        def vcol(name, c):
            k = VEC[name] + c
            return vec_t[:, k:k + 1]

        def ld(eng_q, out, in_, writes, reads=()):
            return S.op(eng_q, lambda e: e.dma_start(out=out, in_=in_), reads=reads, writes=writes, dma=True)

        with ExitStack() as st:
            lam_t = sb(st, "lam_t", [128, 512], F32)
            lt1 = sb(st, "lt1", [128, 256], F32)
            ls = sb(st, "ls", [128, 4], F32)
            rl = Res()
            ld("sp", vec_t[:], vecs, [r_const])
            ld("sp", rot_t[:], rotm, [r_const])
            ld("sp", inv_t[:], inv2, [r_const])
            ld("sp", hv_t[:], hv, [r_const])
            ld("sp", lam_t[:], lamv, [rl])
            S.op("dve", lambda e: e.memset(ones_b[:], 1.0), writes=[r_const])
            S.op("dve", lambda e: e.memset(eps_t[:, 0:1], RMS_EPS), writes=[r_const])
            S.op("dve", lambda e: e.memset(eps_t[:, 1:2], LN_EPS), writes=[r_const])
            S.op("dve", lambda e: e.tensor_tensor(out=lt1[:, 0:128], in0=lam_t[:, 0:128], in1=lam_t[:, 128:256],
                                                  op=ALU.mult), reads=[rl], writes=[rl])
            S.op("dve", lambda e: e.tensor_tensor(out=lt1[:, 128:256], in0=lam_t[:, 256:384], in1=lam_t[:, 384:512],
                                                  op=ALU.mult), reads=[rl], writes=[rl])
            S.op("dve", lambda e: e.reduce_sum(out=ls[:, 0:1], in_=lt1[:, 0:128], axis=mybir.AxisListType.X),
                 reads=[rl], writes=[rl])
            S.op("dve", lambda e: e.reduce_sum(out=ls[:, 1:2], in_=lt1[:, 128:256], axis=mybir.AxisListType.X),
                 reads=[rl], writes=[rl])
            S.op("act", lambda e: e.activation(out=ls[:, 2:4], in_=ls[:, 0:2], func=AF.Exp), reads=[rl], writes=[rl])
            S.op("dve", lambda e: e.tensor_tensor(out=ls[:, 0:1], in0=ls[:, 3:4], in1=ls[:, 2:3], op=ALU.subtract),
                 reads=[rl], writes=[rl])
            S.op("dve", lambda e: e.tensor_scalar(out=nlam_t[:], in0=ls[:, 0:1], scalar1=-LAMBDA_INIT, scalar2=None,
                                                  op0=ALU.add), reads=[rl], writes=[r_const])
            S.op("dve", lambda e: e.tensor_scalar(out=sgs_t[:], in0=vec_t[:, VEC["subln"]:VEC["subln"] + 2],
                                                  scalar1=1.0 - LAMBDA_INIT, scalar2=None, op0=ALU.mult),
                 reads=[r_const], writes=[r_const])
            S.flush()

        def make_converter(st, KH, nbuf, engines, tag):
            stg = [sb(st, "cvf%s%d" % (tag, i), [128, KH * 512], F32) for i in range(nbuf)]
            stb = [sb(st, "cvb%s%d" % (tag, i), [128, KH * 512], BF16) for i in range(nbuf)]
            rf = [Res() for _ in range(nbuf)]
            rb_ = [Res() for _ in range(nbuf)]
            cnt = [0]

            def convert(src, col0, K, M, cw, dst):
                KC = K // 128
                ng = (M + cw - 1) // cw
                kh = max(1, min(KC, (KH * 512) // cw))
                for g in range(ng):
                    wd = min(cw, M - g * cw)
                    for k0 in range(0, KC, kh):
                        k1 = min(KC, k0 + kh)
                        nk = k1 - k0
                        i = cnt[0] % nbuf
                        ce = engines[cnt[0] % len(engines)]
                        cnt[0] += 1
                        fv = stg[i][:, 0:nk * wd].rearrange("p (k c) -> p k c", c=wd)
                        bv = stb[i][:, 0:nk * wd].rearrange("p (k c) -> p k c", c=wd)
                        sv = src[k0 * 128:k1 * 128, col0 + g * cw:col0 + g * cw + wd].rearrange(
                            "(k p) c -> p k c", p=128)
                        ld("sp", fv, sv, [rf[i]])
                        if ce == "act":
                            S.op("act", lambda e, bv=bv, fv=fv: e.copy(out=bv, in_=fv), reads=[rf[i]], writes=[rb_[i]])
                        else:
                            S.op(ce, lambda e, bv=bv, fv=fv: e.tensor_copy(out=bv, in_=fv), reads=[rf[i]],
                                 writes=[rb_[i]])
                        ld("pool", dst[g, :, k0:k1, 0:wd], bv, [], reads=[rb_[i]])
                        yield
            return convert

        def late_weights(convert):
            yield from convert(w_in, OFF_G, D, 8192, 256, wg_s)
            yield from convert(w_pa, 0, 2048, D, 256, wpa_s)
            yield from convert(w_pb, 0, 2048, D, 256, wpb_s)
            yield from convert(w_out, 0, D, D, 256, wout_s)
            yield from convert(w_up, 0, D, DFF, 256, wupg_s)
            yield from convert(w_up, DFF, D, DFF, 256, wupv_s)
            yield from convert(w_dn, 0, DFF, D, 128, wdn_s)

        with ExitStack() as st:
            convert = make_converter(st, 16, 3, ("act", "dve", "pool"), "p")
            for args in ((w_in, OFF_K, D, 2048, 512, wk_s), (w_in, OFF_V, D, 2048, 512, wv_s),
                         (w_in, OFF_CKV, D, 512, 512, wckv_s), (w_in, OFF_KR, D, 64, 64, wkr_s),
                         (w_ukv, 0, 512, 4096, 512, wukv_s), (w_in, OFF_Q, D, 2048, 512, wq_s),
                         (w_in, OFF_CQ, D, 1024, 512, wcq_s), (w_uq, 0, 1024, 3072, 192, wuq_s)):
                for _ in convert(*args):
                    pass
            if not BG_CONVERT:
                for _ in late_weights(convert):
                    pass
            S.flush()

        if upto == "P":
            return nc
        def rope_tables(st, tag, posd, c0, n, cs_t, rcs):
            pi_t, pf_t, pa_t, rr = tag
            TWO_PI = 2.0 * math.pi
            C1 = 6.28125
            C2 = TWO_PI - C1
            ki_t = pi_t
            ld("sp", pi_t[:, 0:n], posd[:, c0:c0 + n], [rr])
            S.op("dve", lambda e: e.tensor_copy(out=pf_t[:, 0:n], in_=pi_t[:, 0:n]), reads=[rr], writes=[rr])
            S.op("dve", lambda e: e.tensor_scalar(out=pa_t[:, 0:n], in0=pf_t[:, 0:n], scalar1=inv_t[:, 0:1],
                                                  scalar2=None, op0=ALU.mult), reads=[rr, r_const], writes=[rr])
            S.op("dve", lambda e: e.tensor_scalar(out=ki_t[:, 0:n], in0=pa_t[:, 0:n], scalar1=1.0 / TWO_PI,
                                                  scalar2=None, op0=ALU.mult), reads=[rr], writes=[rr])
            S.op("dve", lambda e: e.tensor_copy(out=pf_t[:, 0:n], in_=ki_t[:, 0:n]), reads=[rr], writes=[rr])
            S.op("dve", lambda e: e.scalar_tensor_tensor(out=pa_t[:, 0:n], in0=pf_t[:, 0:n], scalar=-C1,
                                                         in1=pa_t[:, 0:n], op0=ALU.mult, op1=ALU.add),
                 reads=[rr], writes=[rr])
            S.op("dve", lambda e: e.scalar_tensor_tensor(out=pa_t[:, 0:n], in0=pf_t[:, 0:n], scalar=-C2,
                                                         in1=pa_t[:, 0:n], op0=ALU.mult, op1=ALU.add),
                 reads=[rr], writes=[rr])

            def wrap():
                S.op("dve", lambda e: e.tensor_scalar(out=pf_t[:, 0:n], in0=pa_t[:, 0:n], scalar1=math.pi,
                                                      scalar2=None, op0=ALU.is_gt), reads=[rr], writes=[rr])
                S.op("dve", lambda e: e.scalar_tensor_tensor(out=pa_t[:, 0:n], in0=pf_t[:, 0:n], scalar=-TWO_PI,
                                                             in1=pa_t[:, 0:n], op0=ALU.mult, op1=ALU.add),
                     reads=[rr], writes=[rr])
                S.op("dve", lambda e: e.tensor_scalar(out=pf_t[:, 0:n], in0=pa_t[:, 0:n], scalar1=-math.pi,
                                                      scalar2=None, op0=ALU.is_lt), reads=[rr], writes=[rr])
                S.op("dve", lambda e: e.scalar_tensor_tensor(out=pa_t[:, 0:n], in0=pf_t[:, 0:n], scalar=TWO_PI,
                                                             in1=pa_t[:, 0:n], op0=ALU.mult, op1=ALU.add),
                     reads=[rr], writes=[rr])

            wrap()
            S.op("act", lambda e: e.activation(out=cs_t[:, 1, 0:n], in_=pa_t[:, 0:n], func=AF.Sin), reads=[rr],
                 writes=[rcs, rr])
            S.op("dve", lambda e: e.tensor_scalar(out=pa_t[:, 0:n], in0=pa_t[:, 0:n], scalar1=0.5 * math.pi,
                                                  scalar2=None, op0=ALU.add), reads=[rr], writes=[rr])
            wrap()
            S.op("act", lambda e: e.activation(out=cs_t[:, 0, 0:n], in_=pa_t[:, 0:n], func=AF.Sin), reads=[rr],
                 writes=[rcs, rr])

        with ExitStack() as st:
            xkb = [sb(st, "xkb%d" % i, [128, 32, 512], BF16) for i in range(2)]
            rxk = [Res() for _ in range(2)]
            xst = [sb(st, "xst%d" % i, [128, 8, 512], F32) for i in range(2)]
            rxst = [Res() for _ in range(2)]
            wsl = [sb(st, "wsl%d" % i, [128, 16384], BF16) for i in range(2)]
            rws = [Res() for _ in range(2)]
            wcnt = [0]
            stgb = [sb(st, "stgb%d" % i, [128, 512], BF16) for i in range(4)]
            rstg = [Res() for _ in range(4)]
            scnt = [0]
            ckv = sb(st, "ckv", [128, 4, 512], F32)
            sq = sb(st, "sqa", [128, 4, 512], BF16)
            rstd = sb(st, "rstda", [128, 512], F32)
            ckvn = sb(st, "ckvn", [128, 4, 512], BF16)
            r_ckv, r_sq, r_rstd, r_ckvn = Res(), Res(), Res(), Res()
            pi_t = sb(st, "pi_a", [64, 512], I32)
            pf_t = sb(st, "pf_a", [64, 512], F32)
            pa_t = sb(st, "pa_a", [64, 512], F32)
            cs_t = sb(st, "cs_a", [64, 2, 512], F32)
            krf = sb(st, "krf", [64, 512], F32)
            krb = sb(st, "krb", [64, 512], BF16)
            kro = sb(st, "kro", [64, 512], F32)
            r_rope, r_cs, r_kr = Res(), Res(), Res()

            def load_slab(scr_g, nelem):
                i = wcnt[0] % 2
                wcnt[0] += 1
                ld("sp", wsl[i][:, 0:nelem], scr_g.rearrange("p k c -> p (k c)"), [rws[i]])
                return i

            def stage_out(psb, M, N, dst, scale=None, view=None):
                k = scnt[0] % 4
                scnt[0] += 1
                if scale is None:
                    S.op("act", lambda e: e.copy(out=stgb[k][0:M, 0:N], in_=ps[psb][0:M, 0:N]), reads=[rps[psb]],
                         writes=[rstg[k]])
                else:
                    S.op("act", lambda e: e.activation(out=stgb[k][0:M, 0:N], in_=ps[psb][0:M, 0:N], func=AF.Copy,
                                                       scale=scale), reads=[rps[psb]], writes=[rstg[k]])
                srcv = stgb[k][0:M, 0:N]
                if view is not None:
                    srcv = view(srcv)
                ld("pool", dst, srcv, [], reads=[rstg[k]])

            for g in range(16):
                xi = g % 2
                t0 = g * 512
                for q4 in range(4):
                    si = (g * 4 + q4) % 2
                    ld("sp", xst[si][:], xkT[q4 * 1024:(q4 + 1) * 1024, t0:t0 + 512].rearrange("(k p) n -> p k n", p=128),
                       [rxst[si]])
                    eng = "dve" if q4 % 2 == 0 else "pool"
                    S.op(eng, lambda e, si=si, q4=q4, xi=xi: e.tensor_copy(out=xkb[xi][:, q4 * 8:(q4 + 1) * 8, :],
                                                                          in_=xst[si][:]),
                         reads=[rxst[si]], writes=[rxk[xi]])
                xk = xkb[xi]
                for s in range(4):
                    wi = load_slab(wk_s[s], 16384)
                    wv_ = wsl[wi][:].rearrange("p (k c) -> p k c", c=512)
                    for c in range(4):
                        hm = s * 4 + c
                        pb = next_ps()
                        for kc in range(32):
                            S.op("pe", lambda e, wv_=wv_, kc=kc, c=c, pb=pb, xk=xk: e.matmul(
                                ps[pb][:, :], wv_[:, kc, c * 128:(c + 1) * 128], xk[:, kc, :],
                                start=(kc == 0), stop=(kc == 31)), reads=[rws[wi], rxk[xi]], writes=[rps[pb]])
                        stage_out(pb, 128, 512, kTda[hm, :, t0:t0 + 512])
                for s in range(4):
                    wi = load_slab(wv_s[s], 16384)
                    wv_ = wsl[wi][:].rearrange("p (k c) -> p k c", c=512)
                    for tb in range(4):
                        pb = next_ps()
                        for kc in range(32):
                            S.op("pe", lambda e, wv_=wv_, kc=kc, tb=tb, pb=pb, xk=xk: e.matmul(
                                ps[pb][:, :], xk[:, kc, tb * 128:(tb + 1) * 128], wv_[:, kc, :],
                                start=(kc == 0), stop=(kc == 31)), reads=[rws[wi], rxk[xi]], writes=[rps[pb]])
                        rbk = g * 4 + tb
                        stage_out(pb, 128, 512, vda[2 * s:2 * s + 2, :, rbk, :].rearrange("h p e -> p h e"),
                                  view=lambda a: a.rearrange("p (h e) -> p h e", h=2))
                wi = load_slab(wckv_s[0], 16384)
                wv_ = wsl[wi][:].rearrange("p (k c) -> p k c", c=512)
                for c in range(4):
                    pb = next_ps()
                    for kc in range(32):
                        S.op("pe", lambda e, wv_=wv_, kc=kc, c=c, pb=pb, xk=xk: e.matmul(
                            ps[pb][:, :], wv_[:, kc, c * 128:(c + 1) * 128], xk[:, kc, :],
                            start=(kc == 0), stop=(kc == 31)), reads=[rws[wi], rxk[xi]], writes=[rps[pb]])
                    S.op("dve", lambda e, c=c, pb=pb: e.tensor_copy(out=ckv[:, c, :], in_=ps[pb][:, :]),
                         reads=[rps[pb]], writes=[r_ckv])
                    S.op("act", lambda e, c=c, pb=pb: e.activation(out=sq[:, c, :], in_=ps[pb][:, :], func=AF.Square),
                         reads=[rps[pb]], writes=[r_sq])
                pb = next_ps()
                for c in range(4):
                    S.op("pe", lambda e, c=c, pb=pb: e.matmul(ps[pb][:, :], ones_b[:, :], sq[:, c, :], start=(c == 0),
                                                             stop=(c == 3)), reads=[r_sq, r_const], writes=[rps[pb]])
                S.op("act", lambda e, pb=pb: e.activation(out=rstd[:], in_=ps[pb][:, :], func=AF.Sqrt,
                                                          bias=eps_t[:, 0:1], scale=1.0 / 512.0),
                     reads=[rps[pb], r_const], writes=[r_rstd])
                S.op("dve", lambda e: e.reciprocal(out=rstd[:], in_=rstd[:]), reads=[r_rstd], writes=[r_rstd])
                for c in range(4):
                    S.op("dve", lambda e, c=c: e.scalar_tensor_tensor(out=ckvn[:, c, :], in0=ckv[:, c, :],
                                                                      scalar=vcol("kvn_g", c), in1=rstd[:],
                                                                      op0=ALU.mult, op1=ALU.mult),
                         reads=[r_ckv, r_rstd, r_const], writes=[r_ckvn])
                i_ = wcnt[0] % 2
                wcnt[0] += 1
                ld("sp", wsl[i_][:].rearrange("p (g f) -> p g f", g=8), wukv_s.rearrange("g p k c -> p g (k c)"),
                   [rws[i_]])
                wu = wsl[i_][:].rearrange("p (g k c) -> p g k c", g=8, k=4)
                for h in range(16):
                    pb = next_ps()
                    for kc in range(4):
                        S.op("pe", lambda e, h=h, kc=kc, pb=pb, wu=wu: e.matmul(
                            ps[pb][:, :], wu[:, h // 2, kc, (h % 2) * 256:(h % 2) * 256 + 128], ckvn[:, kc, :],
                            start=(kc == 0), stop=(kc == 3)), reads=[rws[i_], r_ckvn], writes=[rps[pb]])
                    stage_out(pb, 128, 512, kTm[h, :, t0:t0 + 512])
                for tb in range(4):
                    for s8 in range(8):
                        pb = next_ps()
                        for kc in range(4):
                            S.op("pe", lambda e, s8=s8, kc=kc, pb=pb, tb=tb, wu=wu: e.matmul(
                                ps[pb][:, :], ckvn[:, kc, tb * 128:(tb + 1) * 128], wu[:, s8, kc, :],
                                start=(kc == 0), stop=(kc == 3)), reads=[rws[i_], r_ckvn], writes=[rps[pb]])
                        k = scnt[0] % 4
                        scnt[0] += 1
                        S.op("act", lambda e, k=k, pb=pb: e.copy(
                            out=stgb[k][:, 0:256].rearrange("p (h e) -> p h e", h=2),
                            in_=ps[pb][:, :].rearrange("p (h t e) -> p h t e", h=2, t=2)[:, :, 1, :]),
                             reads=[rps[pb]], writes=[rstg[k]])
                        ld("pool", vm[2 * s8:2 * s8 + 2, :, g * 4 + tb, :].rearrange("h p e -> p h e"),
                           stgb[k][:, 0:256].rearrange("p (h e) -> p h e", h=2), [], reads=[rstg[k]])
                i_ = wcnt[0] % 2
                wcnt[0] += 1
                ld("sp", wsl[i_][:, 0:32 * 64], wkr_s[0].rearrange("p k c -> p (k c)"), [rws[i_]])
                wr = wsl[i_][:, 0:32 * 64].rearrange("p (k c) -> p k c", c=64)
                pb = next_ps()
                for kc in range(32):
                    S.op("pe", lambda e, kc=kc, pb=pb, wr=wr, xk=xk: e.matmul(ps[pb][0:64, :], wr[:, kc, :], xk[:, kc, :],
                                                                            start=(kc == 0), stop=(kc == 31)),
                         reads=[rws[i_], rxk[xi]], writes=[rps[pb]])
                rope_tables(st, (pi_t, pf_t, pa_t, r_rope), posk, t0, 512, cs_t, r_cs)
                S.op("dve", lambda e, pb=pb: e.tensor_copy(out=krf[:], in_=ps[pb][0:64, :]), reads=[rps[pb]],
                     writes=[r_kr])
                S.op("act", lambda e, pb=pb: e.copy(out=krb[:], in_=ps[pb][0:64, :]), reads=[rps[pb]], writes=[r_kr])
                pb2 = next_ps()
                S.op("pe", lambda e, pb2=pb2: e.matmul(ps[pb2][0:64, :], rot_t[:, :], krb[:, :], start=True, stop=True),
                     reads=[r_kr, r_const], writes=[rps[pb2]])
                S.op("dve", lambda e: e.tensor_tensor(out=krf[:], in0=krf[:], in1=cs_t[:, 0, :], op=ALU.mult),
                     reads=[r_kr, r_cs], writes=[r_kr])
                S.op("dve", lambda e, pb2=pb2: e.tensor_tensor(out=kro[:], in0=ps[pb2][0:64, :], in1=cs_t[:, 1, :],
                                                               op=ALU.mult), reads=[rps[pb2], r_cs], writes=[r_kr])
                k = scnt[0] % 4
                scnt[0] += 1
                S.op("dve", lambda e, k=k: e.tensor_tensor(out=stgb[k][0:64, :], in0=krf[:], in1=kro[:], op=ALU.add),
                     reads=[r_kr], writes=[rstg[k]])
                ld("pool", kTr[:, t0:t0 + 512], stgb[k][0:64, :], [], reads=[rstg[k]])
            S.flush()

        if upto == "A":
            return nc
        SC_DA = 128.0 ** -0.5
        SC_M = 192.0 ** -0.5
        with ExitStack() as st:
            xqb = [sb(st, "xqb%d" % i, [128, 32, W], BF16) for i in range(2)]
            rxq = [Res() for _ in range(2)]
            xst = [sb(st, "xsq%d" % i, [128, 8, W], F32) for i in range(2)]
            rxst = [Res() for _ in range(2)]
            wsl = [sb(st, "wsq%d" % i, [128, 16384], BF16) for i in range(2)]
            rws = [Res() for _ in range(2)]
            wcnt = [0]
            stgb = [sb(st, "stq%d" % i, [128, W], BF16) for i in range(4)]
            rstg = [Res() for _ in range(4)]
            scnt = [0]
            cq = sb(st, "cq", [128, 8, W], F32)
            sq = sb(st, "sqq", [128, 8, W], BF16)
            rstd = sb(st, "rstdq", [128, W], F32)
            cqn = sb(st, "cqn", [128, 8, W], BF16)
            r_cq, r_sq, r_rstd, r_cqn = Res(), Res(), Res(), Res()
            pi_t = sb(st, "pi_q", [64, W], I32)
            pf_t = sb(st, "pf_q", [64, W], F32)
            pa_t = sb(st, "pa_q", [64, W], F32)
            cs_t = sb(st, "cs_q", [64, 2, W], F32)
            qrf = sb(st, "qrf", [64, W], F32)
            qrb = sb(st, "qrb", [64, W], BF16)
            qro = sb(st, "qro", [64, W], F32)
            r_rope, r_cs, r_qr = Res(), Res(), Res()

            def stage_out_q(psb, M, dst, scale):
                k = scnt[0] % 4
                scnt[0] += 1
                S.op("act", lambda e: e.activation(out=stgb[k][0:M, :], in_=ps[psb][0:M, 0:W], func=AF.Copy,
                                                   scale=scale), reads=[rps[psb]], writes=[rstg[k]])
                ld("pool", dst, stgb[k][0:M, :], [], reads=[rstg[k]])

            for i in range(NT):
                xi = i % 2
                t0 = i * W
                for q4 in range(4):
                    si = (i * 4 + q4) % 2
                    ld("sp", xst[si][:], xqT[q4 * 1024:(q4 + 1) * 1024, t0:t0 + W].rearrange("(k p) n -> p k n", p=128),
                       [rxst[si]])
                    eng = "dve" if q4 % 2 == 0 else "pool"
                    S.op(eng, lambda e, si=si, q4=q4, xi=xi: e.tensor_copy(out=xqb[xi][:, q4 * 8:(q4 + 1) * 8, :],
                                                                          in_=xst[si][:]),
                         reads=[rxst[si]], writes=[rxq[xi]])
                xq = xqb[xi]
                rope_tables(st, (pi_t, pf_t, pa_t, r_rope), posq, t0, W, cs_t, r_cs)
                for s in range(4):
                    wi = wcnt[0] % 2
                    wcnt[0] += 1
                    ld("sp", wsl[wi][:], wq_s[s].rearrange("p k c -> p (k c)"), [rws[wi]])
                    wv_ = wsl[wi][:].rearrange("p (k c) -> p k c", c=512)
                    for c in range(4):
                        hm = s * 4 + c
                        pb = next_ps()
                        for kc in range(32):
                            S.op("pe", lambda e, wv_=wv_, kc=kc, c=c, pb=pb, xq=xq: e.matmul(
                                ps[pb][:, 0:W], wv_[:, kc, c * 128:(c + 1) * 128], xq[:, kc, :],
                                start=(kc == 0), stop=(kc == 31)), reads=[rws[wi], rxq[xi]], writes=[rps[pb]])
                        stage_out_q(pb, 128, qTda[hm, :, t0:t0 + W], SC_DA)
                for s in range(2):
                    wi = wcnt[0] % 2
                    wcnt[0] += 1
                    ld("sp", wsl[wi][:], wcq_s[s].rearrange("p k c -> p (k c)"), [rws[wi]])
                    wv_ = wsl[wi][:].rearrange("p (k c) -> p k c", c=512)
                    for c in range(4):
                        cc = s * 4 + c
                        pb = next_ps()
                        for kc in range(32):
                            S.op("pe", lambda e, wv_=wv_, kc=kc, c=c, pb=pb, xq=xq: e.matmul(
                                ps[pb][:, 0:W], wv_[:, kc, c * 128:(c + 1) * 128], xq[:, kc, :],
                                start=(kc == 0), stop=(kc == 31)), reads=[rws[wi], rxq[xi]], writes=[rps[pb]])
                        S.op("dve", lambda e, cc=cc, pb=pb: e.tensor_copy(out=cq[:, cc, :], in_=ps[pb][:, 0:W]),
                             reads=[rps[pb]], writes=[r_cq])
                        S.op("act", lambda e, cc=cc, pb=pb: e.activation(out=sq[:, cc, :], in_=ps[pb][:, 0:W],
                                                                         func=AF.Square),
                             reads=[rps[pb]], writes=[r_sq])
                pb = next_ps()
                for c in range(8):
                    S.op("pe", lambda e, c=c, pb=pb: e.matmul(ps[pb][:, 0:W], ones_b[:, :], sq[:, c, :], start=(c == 0),
                                                             stop=(c == 7)), reads=[r_sq, r_const], writes=[rps[pb]])
                S.op("act", lambda e, pb=pb: e.activation(out=rstd[:], in_=ps[pb][:, 0:W], func=AF.Sqrt,
                                                          bias=eps_t[:, 0:1], scale=1.0 / 1024.0),
                     reads=[rps[pb], r_const], writes=[r_rstd])
                S.op("dve", lambda e: e.reciprocal(out=rstd[:], in_=rstd[:]), reads=[r_rstd], writes=[r_rstd])
                for c in range(8):
                    S.op("dve", lambda e, c=c: e.scalar_tensor_tensor(out=cqn[:, c, :], in0=cq[:, c, :],
                                                                      scalar=vcol("qn_g", c), in1=rstd[:],
                                                                      op0=ALU.mult, op1=ALU.mult),
                         reads=[r_cq, r_rstd, r_const], writes=[r_cqn])
                for half in range(2):
                    wi = wcnt[0] % 2
                    wcnt[0] += 1
                    ld("sp", wsl[wi][:, 0:8 * 1536].rearrange("p (g f) -> p g f", g=8),
                       wuq_s[half * 8:half * 8 + 8].rearrange("g p k c -> p g (k c)"), [rws[wi]])
                    wu = wsl[wi][:, 0:8 * 1536].rearrange("p (g k c) -> p g k c", g=8, k=8)
                    for hh in range(8):
                        h = half * 8 + hh
                        pb = next_ps()
                        for kc in range(8):
                            S.op("pe", lambda e, hh=hh, kc=kc, pb=pb, wu=wu: e.matmul(
                                ps[pb][:, 0:W], wu[:, hh, kc, 0:128], cqn[:, kc, :], start=(kc == 0), stop=(kc == 7)),
                                 reads=[rws[wi], r_cqn], writes=[rps[pb]])
                        stage_out_q(pb, 128, qTm[h, :, t0:t0 + W], SC_M)
                        pb = next_ps()
                        for kc in range(8):
                            S.op("pe", lambda e, hh=hh, kc=kc, pb=pb, wu=wu: e.matmul(
                                ps[pb][0:64, 0:W], wu[:, hh, kc, 128:192], cqn[:, kc, :], start=(kc == 0),
                                stop=(kc == 7)), reads=[rws[wi], r_cqn], writes=[rps[pb]])
                        S.op("dve", lambda e, pb=pb: e.tensor_copy(out=qrf[:], in_=ps[pb][0:64, 0:W]), reads=[rps[pb]],
                             writes=[r_qr])
                        S.op("act", lambda e, pb=pb: e.copy(out=qrb[:], in_=ps[pb][0:64, 0:W]), reads=[rps[pb]],
                             writes=[r_qr])
                        pb2 = next_ps()
                        S.op("pe", lambda e, pb2=pb2: e.matmul(ps[pb2][0:64, 0:W], rot_t[:, :], qrb[:, :], start=True,
                                                               stop=True), reads=[r_qr, r_const], writes=[rps[pb2]])
                        S.op("dve", lambda e: e.scalar_tensor_tensor(out=qrf[:], in0=qrf[:], scalar=SC_M,
                                                                     in1=cs_t[:, 0, :], op0=ALU.mult, op1=ALU.mult),
                             reads=[r_qr, r_cs], writes=[r_qr])
                        S.op("dve", lambda e, pb2=pb2: e.scalar_tensor_tensor(out=qro[:], in0=ps[pb2][0:64, 0:W],
                                                                              scalar=SC_M, in1=cs_t[:, 1, :],
                                                                              op0=ALU.mult, op1=ALU.mult),
                             reads=[rps[pb2], r_cs], writes=[r_qr])
                        k = scnt[0] % 4
                        scnt[0] += 1
                        S.op("dve", lambda e, k=k: e.tensor_tensor(out=stgb[k][0:64, :], in0=qrf[:], in1=qro[:],
                                                                   op=ALU.add), reads=[r_qr], writes=[rstg[k]])
                        ld("pool", qTr[h, :, t0:t0 + W], stgb[k][0:64, :], [], reads=[rstg[k]])
            S.flush()

        if upto == "B":
            return nc
        with ExitStack() as st:
            kbuf = [sb(st, "kbuf%d" % i, [128, NKV], BF16) for i in range(2)]
            rk = [Res() for _ in range(2)]
            vbuf = [sb(st, "vbuf%d" % i, [128, NB * 256], BF16) for i in range(2)]
            rv = [Res() for _ in range(2)]
            krb_t = sb(st, "krbuf", [64, NKV], BF16)
            r_kr = Res()
            qbuf = [sb(st, "qbuf%d" % i, [128, NQ], BF16) for i in range(2)]
            rq = [Res() for _ in range(2)]
            qrbuf = [sb(st, "qrbuf%d" % i, [64, NQ], BF16) for i in range(2)]
            alqb = [sb(st, "alqb%d" % i, [3, NQ], BF16) for i in range(1)]
            r_alq = Res()
            alk_t = sb(st, "alk_t", [3, 8, 128], BF16)
            mask_t = sb(st, "mask_t", [128, NMASK, W], BF16)
            NPT = 8
            pt = [sb(st, "pt%d" % i, [128, W], BF16) for i in range(NPT)]
            rpt = [Res() for _ in range(NPT)]
            stmp = [sb(st, "stmp%d" % i, [128, W], F32) for i in range(2)]
            rstmp = [Res() for _ in range(2)]
            stcnt = [0]
            pcnt = [0]
            rden = sb(st, "rden", [128, W], F32)
            o1 = sb(st, "o1", [128, 2, W], F32)
            od = sb(st, "od", [128, 2, W], F32)
            osq = sb(st, "osq", [128, 2, W], BF16)
            orst = sb(st, "orst", [128, W], F32)
            ostg = [sb(st, "ostg%d" % i, [128, 2, W], BF16) for i in range(2)]
            rostg = [Res() for _ in range(2)]
            ocnt = [0]
            r_rden, r_o1, r_od, r_osq, r_orst, r_catt = Res(), Res(), Res(), Res(), Res(), Res()
            kval_t = sb(st, "kval_t", [128, NB], F32)
            abias_t = sb(st, "abias_t", [128, 8, NT, NB], F32)
            ld("sp", kval_t[:], kvalid, [r_catt])
            ld("sp", abias_t[:], abias, [r_catt])
            ld("sp", alk_t[:], alk, [r_catt])
            ld("sp", mask_t[:], masks_d, [r_catt])
            ld("sp", krb_t[:], kTr, [r_kr])
            bg = late_weights(make_converter(st, 3, 2, ("pool",), "c")) if BG_CONVERT else iter(())

            def bg_step(n):
                for _ in range(n):
                    if next(bg, "done") == "done":
                        return
            S_BANKS = (0, 1, 2, 3)
            O_SETS = ((4, 5, 6), (4, 5, 6))
            LN_BANK = 7
            SK = 3
            scnt = [0]
            setcnt = [0]
            kcnt = [0]
            vcnt = [0]
            qcnt = [0]

            def sweep(i, kT, rkres, qT, rqres, extra, V, rvres, ne, esz, biasfn, oset):
                blocks = tile_blocks(i)
                nblk = len(blocks)
                pks = {}

                def emit_scores(bi):
                    rb, full = blocks[bi]
                    sbk = S_BANKS[scnt[0] % len(S_BANKS)]
                    scnt[0] += 1
                    S.op("pe", lambda e: e.matmul(ps[sbk][:, 0:W], kT[:, rb * 128:(rb + 1) * 128],
                                                  qT[:, i * W:(i + 1) * W], start=True, stop=(extra is None)),
                         reads=[rkres, rqres], writes=[rps[sbk]])
                    if extra is not None:
                        S.op("pe", lambda e: e.matmul(ps[sbk][:, 0:W], extra[0](rb), extra[1][:, i * W:(i + 1) * W],
                                                      start=False, stop=True),
                             reads=list(extra[2]), writes=[rps[sbk]])
                    pk = pcnt[0] % NPT
                    pcnt[0] += 1
                    pks[bi] = pk
                    if full:
                        S.op("act", lambda e: e.activation(out=pt[pk][:], in_=ps[sbk][:, 0:W], func=AF.Exp,
                                                           bias=biasfn(rb), scale=1.0),
                             reads=[rps[sbk], r_catt], writes=[rpt[pk]])
                    else:
                        mi = MASK_IDX[(i, rb)]
                        tk = stcnt[0] % 2
                        stcnt[0] += 1
                        S.op("dve", lambda e: e.tensor_tensor(out=stmp[tk][:], in0=ps[sbk][:, 0:W],
                                                              in1=mask_t[:, mi, :], op=ALU.add),
                             reads=[rps[sbk], r_catt], writes=[rstmp[tk]])
                        S.op("act", lambda e: e.activation(out=pt[pk][:], in_=stmp[tk][:], func=AF.Exp,
                                                           bias=biasfn(rb), scale=1.0),
                             reads=[rstmp[tk], r_catt], writes=[rpt[pk]])

                def emit_pv(bj, also=()):
                    rb, full = blocks[bj]
                    pk = pks[bj]
                    first = True
                    for e_ in range(ne):
                        rd = [rvres, rpt[pk]] + ([rpt[pks[x]] for x in also] if first else [])
                        first = False
                        S.op("pe", lambda e, e_=e_: e.matmul(
                            ps[oset[e_]][:, 0:W], V[:, rb * esz + e_ * 128: rb * esz + (e_ + 1) * 128], pt[pk][:],
                            start=(bj == 0), stop=(bj == nblk - 1)), reads=rd, writes=[rps[oset[e_]]])
                    S.op("pe", lambda e: e.matmul(ps[oset[2]][:, 0:W], ones_b[:, :], pt[pk][:],
                                                  start=(bj == 0), stop=(bj == nblk - 1)),
                         reads=[rpt[pk], r_const], writes=[rps[oset[2]]])

                SKP = 4
                step = 0
                for bi in range(0, nblk + SKP, 2):
                    for x in (bi, bi + 1):
                        if x < nblk:
                            emit_scores(x)
                            step += 1
                            if step % BG_EVERY == 0:
                                bg_step(1)
                    pv = [x for x in (bi - SKP, bi - SKP + 1) if 0 <= x < nblk]
                    if pv:
                        emit_pv(pv[0], also=pv[1:])
                        for x in pv[1:]:
                            emit_pv(x)
                S.op("dve", lambda e: e.tensor_scalar(out=rden[:], in0=ps[oset[2]][:, 0:W], scalar1=1e-30, scalar2=None,
                                                      op0=ALU.add), reads=[rps[oset[2]]], writes=[r_rden])
                S.op("dve", lambda e: e.reciprocal(out=rden[:], in_=rden[:]), reads=[r_rden], writes=[r_rden])

            for h in range(8):
                vi = vcnt[0] % 2
                vcnt[0] += 1
                ld("sp", vbuf[vi][:], vda[h].rearrange("p b e -> p (b e)"), [rv[vi]])
                for m in range(2):
                    hm = 2 * h + m
                    ki = kcnt[0] % 2
                    kcnt[0] += 1
                    ld("sp", kbuf[ki][:], kTda[hm], [rk[ki]])
                    qi = qcnt[0] % 2
                    qcnt[0] += 1
                    ld("sp", qbuf[qi][:], qTda[hm], [rq[qi]])
                    if h < ALIBI_MM_HEADS:
                        ld("sp", alqb[0][:], alq[h], [r_alq])
                    for i in range(NT):
                        oset = O_SETS[setcnt[0] % 2]
                        setcnt[0] += 1
                        sweep(i, kbuf[ki], rk[ki], qbuf[qi], rq[qi],
                              ((lambda rb, h=h: alk_t[:, h, :], alqb[0], (r_catt, r_alq))
                               if h < ALIBI_MM_HEADS else None),
                              vbuf[vi], rv[vi], 2, 256, (lambda rb, h=h, i=i: abias_t[:, h, i, rb:rb + 1]), oset)
                        if m == 0:
                            for e_ in range(2):
                                S.op("dve", lambda e, e_=e_, i=i, oset=oset: e.tensor_tensor(
                                    out=o1[:, e_, :], in0=ps[oset[e_]][:, 0:W], in1=rden[:], op=ALU.mult),
                                     reads=[rps[oset[e_]], r_rden], writes=[r_o1])
                        else:
                            pass
                        if m == 0:
                            ld("pool", yscr[0:2, :, i * W:(i + 1) * W].rearrange("c p n -> p c n"), o1[:], [],
                               reads=[r_o1])
                        else:
                            ld("sp", o1[:], yscr[0:2, :, i * W:(i + 1) * W].rearrange("c p n -> p c n"), [r_o1])
                            for e_ in range(2):
                                S.op("dve", lambda e, e_=e_, oset=oset: e.tensor_tensor(
                                    out=od[:, e_, :], in0=ps[oset[e_]][:, 0:W], in1=rden[:], op=ALU.mult),
                                     reads=[rps[oset[e_]], r_rden], writes=[r_od])
                                S.op("dve", lambda e, e_=e_: e.scalar_tensor_tensor(
                                    out=od[:, e_, :], in0=od[:, e_, :], scalar=nlam_t[:, 0:1], in1=o1[:, e_, :],
                                    op0=ALU.mult, op1=ALU.add), reads=[r_od, r_o1, r_const], writes=[r_od])
                                S.op("act", lambda e, e_=e_: e.activation(out=osq[:, e_, :], in_=od[:, e_, :],
                                                                          func=AF.Square), reads=[r_od], writes=[r_osq])
                            sbk = LN_BANK
                            for e_ in range(2):
                                S.op("pe", lambda e, e_=e_, sbk=sbk: e.matmul(ps[sbk][:, 0:W], ones_b[:, :], osq[:, e_, :],
                                                                             start=(e_ == 0), stop=(e_ == 1)),
                                     reads=[r_osq, r_const], writes=[rps[sbk]])
                            S.op("act", lambda e, sbk=sbk: e.activation(out=orst[:], in_=ps[sbk][:, 0:W], func=AF.Sqrt,
                                                                        bias=eps_t[:, 0:1], scale=1.0 / 256.0),
                                 reads=[rps[sbk], r_const], writes=[r_orst])
                            S.op("dve", lambda e: e.reciprocal(out=orst[:], in_=orst[:]), reads=[r_orst],
                                 writes=[r_orst])
                            ok = ocnt[0] % 2
                            ocnt[0] += 1
                            for e_ in range(2):
                                S.op("dve", lambda e, e_=e_, ok=ok: e.scalar_tensor_tensor(
                                    out=ostg[ok][:, e_, :], in0=od[:, e_, :], scalar=sgs_t[:, e_:e_ + 1], in1=orst[:],
                                    op0=ALU.mult, op1=ALU.mult), reads=[r_od, r_orst, r_const], writes=[rostg[ok]])
                            ld("pool", oaT[2 * h:2 * h + 2, :, i * W:(i + 1) * W].rearrange("c p n -> p c n"),
                               ostg[ok][:], [], reads=[rostg[ok]])
            for h in range(16):
                vi = vcnt[0] % 2
                vcnt[0] += 1
                ld("sp", vbuf[vi][:, 0:NB * 128], vm[h].rearrange("p b e -> p (b e)"), [rv[vi]])
                ki = kcnt[0] % 2
                kcnt[0] += 1
                ld("sp", kbuf[ki][:], kTm[h], [rk[ki]])
                qi = qcnt[0] % 2
                qcnt[0] += 1
                ld("sp", qbuf[qi][:], qTm[h], [rq[qi]])
                ld("sp", qrbuf[qi][:], qTr[h], [rq[qi]])
                for i in range(NT):
                    oset = O_SETS[setcnt[0] % 2]
                    setcnt[0] += 1
                    oset2 = (oset[0], oset[1], oset[2])
                    sweep(i, kbuf[ki], rk[ki], qbuf[qi], rq[qi],
                          (lambda rb: krb_t[:, rb * 128:(rb + 1) * 128], qrbuf[qi], (r_kr, rq[qi])),
                          vbuf[vi], rv[vi], 1, 128, (lambda rb: kval_t[:, rb:rb + 1]), oset2)
                    ok = ocnt[0] % 2
                    ocnt[0] += 1
                    S.op("dve", lambda e, ok=ok, oset=oset: e.tensor_tensor(out=ostg[ok][:, 0, :], in0=ps[oset[0]][:, 0:W],
                                                                            in1=rden[:], op=ALU.mult),
                         reads=[rps[oset[0]], r_rden], writes=[rostg[ok]])
                    ld("pool", obT[h, :, i * W:(i + 1) * W], ostg[ok][:, 0, :], [], reads=[rostg[ok]])
            bg_step(1 << 30)
            S.flush()

        if upto == "C":
            return nc
        with ExitStack() as st:
            oab = sb(st, "oab", [128, 16, W], BF16)
            obb = sb(st, "obb", [128, 16, W], BF16)
            xqb = sb(st, "xqd", [128, 32, W], BF16)
            xst = [sb(st, "xsd%d" % i, [128, 4, W], F32) for i in range(2)]
            rxst = [Res() for _ in range(2)]
            r_oa, r_ob, r_xq = Res(), Res(), Res()
            wsl = [sb(st, "wsd%d" % i, [128, 8192], BF16) for i in range(4)]
            rws = [Res() for _ in range(4)]
            wcnt = [0]
            mg = sb(st, "mg", [128, 32, W], BF16)
            r_mg = Res()
            gS = [sb(st, "gS%d" % i, [128, 2, W], F32) for i in range(1)]
            tS = [sb(st, "tS%d" % i, [128, 2, W], F32) for i in range(1)]
            gS2 = [sb(st, "gSb%d" % i, [128, 2, W], F32) for i in range(1)]
            uS = [sb(st, "uS%d" % i, [128, 2, W], F32) for i in range(1)]
            rgS = [Res() for _ in range(2)]
            rtS = [Res() for _ in range(2)]
            rgS2 = [Res() for _ in range(2)]
            ruS = [Res() for _ in range(2)]
            xres = [sb(st, "xres%d" % i, [128, W], F32) for i in range(3)]
            rxres = [Res() for _ in range(3)]
            ych = [sb(st, "ych%d" % i, [128, W], F32) for i in range(3)]
            rych = [Res() for _ in range(3)]
            ybf = [sb(st, "ybf%d" % i, [128, 2, W], BF16) for i in range(3)]
            rybf = [Res() for _ in range(3)]
            mean = sb(st, "mean", [128, W], F32)
            rstd = sb(st, "rstdd", [128, W], F32)
            msq = sb(st, "msq", [128, W], F32)
            r_stat = Res()
            hch = [sb(st, "hch%d" % i, [128, W], F32) for i in range(2)]
            hcb = [sb(st, "hcb%d" % i, [128, W], BF16) for i in range(2)]
            rhch = [Res() for _ in range(2)]
            r_y = [[Res() for _ in range(32)] for _ in range(NT)]
            PS_M1, PS_M2 = 6, 7

            def dps():
                b = psn[0] % 6
                psn[0] += 1
                return b

            def slab(scr_g, n):
                wi = wcnt[0] % 4
                wcnt[0] += 1
                ld("sp", wsl[wi][:, 0:n], scr_g.rearrange("p k c -> p (k c)"), [rws[wi]])
                return wi, wsl[wi][:, 0:n].rearrange("p (k c) -> p k c", c=256)

            def mm_chunk(wi, wv_, c, KC, rhs, rrhs):
                pb = dps()
                for kc in range(KC):
                    S.op("pe", lambda e, kc=kc: e.matmul(ps[pb][:, 0:W], wv_[:, kc, c * 128:(c + 1) * 128],
                                                         rhs[:, kc, :], start=(kc == 0), stop=(kc == KC - 1)),
                         reads=[rws[wi], rrhs], writes=[rps[pb]])
                return pb

            for i in range(NT):
                t0 = i * W
                ld("sp", oab[:], oaT[:, :, t0:t0 + W].rearrange("c p n -> p c n"), [r_oa])
                ld("sp", obb[:], obT[:, :, t0:t0 + W].rearrange("c p n -> p c n"), [r_ob])
                for q8 in range(8):
                    si = q8 % 2
                    ld("sp", xst[si][:], xqT[q8 * 512:(q8 + 1) * 512, t0:t0 + W].rearrange("(k p) n -> p k n", p=128),
                       [rxst[si]])
                    eng = "dve" if q8 % 2 == 0 else "pool"
                    S.op(eng, lambda e, si=si, q8=q8: e.tensor_copy(out=xqb[:, q8 * 4:(q8 + 1) * 4, :], in_=xst[si][:]),
                         reads=[rxst[si]], writes=[r_xq])
                for s in range(16):
                    p2 = 0
                    wi, wv_ = slab(wg_s[s], 32 * 256)
                    for c in range(2):
                        mc = 2 * s + c
                        pb = mm_chunk(wi, wv_, c, 32, xqb, r_xq)
                        S.op("act", lambda e, pb=pb, mc=mc, c=c, p2=p2: e.activation(
                            out=gS[p2][:, c, :], in_=ps[pb][:, 0:W], func=AF.Sigmoid, bias=vcol("b_gate", mc),
                            scale=1.0), reads=[rps[pb], r_const], writes=[rgS[p2]])
                    wi, wv_ = slab(wpa_s[s], 16 * 256)
                    for c in range(2):
                        pb = mm_chunk(wi, wv_, c, 16, oab, r_oa)
                        S.op("dve", lambda e, pb=pb, c=c, p2=p2: e.tensor_tensor(
                            out=tS[p2][:, c, :], in0=gS[p2][:, c, :], in1=ps[pb][:, 0:W], op=ALU.mult),
                             reads=[rps[pb], rgS[p2]], writes=[rtS[p2]])
                    wi, wv_ = slab(wg_s[16 + s], 32 * 256)
                    for c in range(2):
                        mc = 2 * s + c
                        pb = mm_chunk(wi, wv_, c, 32, xqb, r_xq)
                        S.op("act", lambda e, pb=pb, mc=mc, c=c, p2=p2: e.activation(
                            out=gS2[p2][:, c, :], in_=ps[pb][:, 0:W], func=AF.Sigmoid, bias=vcol("b_gate", 32 + mc),
                            scale=1.0), reads=[rps[pb], r_const], writes=[rgS2[p2]])
                    wi, wv_ = slab(wpb_s[s], 16 * 256)
                    for c in range(2):
                        mc = 2 * s + c
                        pb = mm_chunk(wi, wv_, c, 16, obb, r_ob)
                        S.op("dve", lambda e, pb=pb, c=c, p2=p2: e.tensor_tensor(
                            out=uS[p2][:, c, :], in0=gS2[p2][:, c, :], in1=ps[pb][:, 0:W], op=ALU.mult),
                             reads=[rps[pb], rgS2[p2]], writes=[ruS[p2]])
                        S.op("pool", lambda e, c=c, p2=p2, mc=mc: e.tensor_tensor(
                            out=mg[:, mc, :], in0=tS[p2][:, c, :], in1=uS[p2][:, c, :], op=ALU.add),
                             reads=[rtS[p2], ruS[p2]], writes=[r_mg])
                for s in range(16):
                    wi, wv_ = slab(wout_s[s], 32 * 256)
                    for c in range(2):
                        mc = 2 * s + c
                        po = mm_chunk(wi, wv_, c, 32, mg, r_mg)
                        xi = mc % 3
                        ld("sp", xres[xi][:], xqT[mc * 128:(mc + 1) * 128, t0:t0 + W], [rxres[xi]])
                        S.op("dve", lambda e, xi=xi, po=po: e.scalar_tensor_tensor(
                            out=ych[xi][:], in0=xres[xi][:], scalar=ALPHA, in1=ps[po][:, 0:W], op0=ALU.mult,
                            op1=ALU.add), reads=[rxres[xi], rps[po]], writes=[rych[xi]])
                        S.op("act", lambda e, xi=xi: e.copy(out=ybf[xi][:, 0, :], in_=ych[xi][:]), reads=[rych[xi]],
                             writes=[rybf[xi]])
                        S.op("act", lambda e, xi=xi: e.activation(out=ybf[xi][:, 1, :], in_=ych[xi][:], func=AF.Square),
                             reads=[rych[xi]], writes=[rybf[xi]])
                        S.op("pe", lambda e, xi=xi, mc=mc: e.matmul(ps[PS_M1][:, 0:W], ones_b[:, :], ybf[xi][:, 0, :],
                                                                    start=(mc == 0), stop=(mc == 31)),
                             reads=[rybf[xi], r_const], writes=[rps[PS_M1]])
                        S.op("pe", lambda e, xi=xi, mc=mc: e.matmul(ps[PS_M2][:, 0:W], ones_b[:, :], ybf[xi][:, 1, :],
                                                                    start=(mc == 0), stop=(mc == 31)),
                             reads=[rybf[xi], r_const], writes=[rps[PS_M2]])
                        ld("pool", yscr[mc, :, t0:t0 + W], ych[xi][:], [r_y[i][mc]], reads=[rych[xi]])
                S.op("dve", lambda e: e.tensor_scalar(out=mean[:], in0=ps[PS_M1][:, 0:W], scalar1=1.0 / D, scalar2=None,
                                                      op0=ALU.mult), reads=[rps[PS_M1]], writes=[r_stat])
                S.op("dve", lambda e: e.tensor_tensor(out=msq[:], in0=mean[:], in1=mean[:], op=ALU.mult),
                     reads=[r_stat], writes=[r_stat])
                S.op("dve", lambda e: e.scalar_tensor_tensor(out=rstd[:], in0=ps[PS_M2][:, 0:W], scalar=1.0 / D,
                                                             in1=msq[:], op0=ALU.mult, op1=ALU.subtract),
                     reads=[rps[PS_M2], r_stat], writes=[r_stat])
                S.op("act", lambda e: e.activation(out=rstd[:], in_=rstd[:], func=AF.Sqrt, bias=eps_t[:, 1:2],
                                                   scale=1.0), reads=[r_stat, r_const], writes=[r_stat])
                S.op("dve", lambda e: e.reciprocal(out=rstd[:], in_=rstd[:]), reads=[r_stat], writes=[r_stat])
                for mc in range(32):
                    xi = mc % 3
                    hi = mc % 2
                    ld("sp", ych[xi][:], yscr[mc, :, t0:t0 + W], [rych[xi]], reads=[r_y[i][mc]])
                    S.op("dve", lambda e, xi=xi: e.tensor_tensor(out=ych[xi][:], in0=ych[xi][:], in1=mean[:],
                                                                 op=ALU.subtract), reads=[rych[xi], r_stat],
                         writes=[rych[xi]])
                    S.op("pool", lambda e, xi=xi: e.tensor_tensor(out=ych[xi][:], in0=ych[xi][:], in1=rstd[:],
                                                                  op=ALU.mult), reads=[rych[xi], r_stat],
                         writes=[rych[xi]])
                    S.op("dve", lambda e, xi=xi, hi=hi, mc=mc: e.tensor_scalar(
                        out=hch[hi][:], in0=ych[xi][:], scalar1=vcol("ln1_g", mc), scalar2=vcol("ln1_b", mc),
                        op0=ALU.mult, op1=ALU.add), reads=[rych[xi], r_const], writes=[rhch[hi]])
                    S.op("act", lambda e, hi=hi: e.copy(out=hcb[hi][:], in_=hch[hi][:]), reads=[rhch[hi]],
                         writes=[rhch[hi]])
                    ld("pool", h1f[mc, :, t0:t0 + W], hch[hi][:], [], reads=[rhch[hi]])
                    ld("pool", h1b[mc, :, t0:t0 + W], hcb[hi][:], [], reads=[rhch[hi]])
            S.flush()

        if upto == "D":
            return nc
        with ExitStack() as st:
            hb = sb(st, "hb", [128, 32, W], BF16)
            r_hb = Res()
            wsl = [sb(st, "wse%d" % i, [128, 8192], BF16) for i in range(4)]
            rws = [Res() for _ in range(4)]
            wcnt = [0]
            abuf = sb(st, "abuf", [128, 86, W], BF16)
            r_a = Res()
            ucar = sb(st, "ucar", [128, 172, 2], F32)
            r_ucar = Res()
            ue = [sb(st, "ue%d" % i, [128, W + 2], F32) for i in range(4)]
            rue = [Res() for _ in range(4)]
            cgS = [sb(st, "cgS%d" % i, [128, 2, W], F32) for i in range(2)]
            rcgS = [Res() for _ in range(2)]
            cv = [sb(st, "cvv%d" % i, [128, W], F32) for i in range(2)]
            rcv = [Res() for _ in range(2)]
            hres = [sb(st, "hres%d" % i, [128, W], F32) for i in range(3)]
            rhres = [Res() for _ in range(3)]
            zch = [sb(st, "zch%d" % i, [128, W], F32) for i in range(3)]
            rzch = [Res() for _ in range(3)]
            zbf = [sb(st, "zbf%d" % i, [128, 2, W], BF16) for i in range(3)]
            rzbf = [Res() for _ in range(3)]
            mean = sb(st, "meane", [128, W], F32)
            rstd = sb(st, "rstde", [128, W], F32)
            msq = sb(st, "msqe", [128, W], F32)
            r_stat = Res()
            och = [sb(st, "och%d" % i, [128, W], F32) for i in range(2)]
            roch = [Res() for _ in range(2)]
            r_z = [[Res() for _ in range(32)] for _ in range(NT)]
            PS_M1, PS_M2 = 6, 7
            S.op("dve", lambda e: e.memset(ucar[:], 0.0), writes=[r_ucar])
            uecnt = [0]

            def dps():
                b = psn[0] % 6
                psn[0] += 1
                return b

            def slab(src2d, n, cw):
                wi = wcnt[0] % 4
                wcnt[0] += 1
                ld("sp", wsl[wi][:, 0:n], src2d, [rws[wi]])
                return wi, wsl[wi][:, 0:n].rearrange("p (k c) -> p k c", c=cw)

            def conv_chunk(pb, ch, out_ap, rout):
                k = uecnt[0] % 4
                uecnt[0] += 1
                S.op("pool", lambda e: e.tensor_copy(out=ue[k][:, 0:2], in_=ucar[:, ch, :]),
                     reads=[r_ucar], writes=[rue[k]])
                S.op("act", lambda e: e.copy(out=ue[k][:, 2:W + 2], in_=ps[pb][:, 0:W]), reads=[rps[pb]],
                     writes=[rue[k]])
                S.op("pool", lambda e: e.tensor_copy(out=ucar[:, ch, :], in_=ue[k][:, W:W + 2]),
                     reads=[rue[k]], writes=[r_ucar])
                S.op("dve", lambda e: e.tensor_scalar(out=out_ap, in0=ue[k][:, 2:W + 2],
                                                      scalar1=vcol("cw2", ch), scalar2=vcol("cb", ch),
                                                      op0=ALU.mult, op1=ALU.add),
                     reads=[rue[k], r_const], writes=[rout])
                S.op("dve", lambda e: e.scalar_tensor_tensor(out=out_ap, in0=ue[k][:, 1:W + 1],
                                                             scalar=vcol("cw1", ch), in1=out_ap,
                                                             op0=ALU.mult, op1=ALU.add),
                     reads=[rue[k], rout, r_const], writes=[rout])
                S.op("dve", lambda e: e.scalar_tensor_tensor(out=out_ap, in0=ue[k][:, 0:W],
                                                              scalar=vcol("cw0", ch), in1=out_ap,
                                                              op0=ALU.mult, op1=ALU.add),
                     reads=[rue[k], rout, r_const], writes=[rout])

            def up_chunk(wi, wv_, c):
                pb = dps()
                for kc in range(32):
                    S.op("pe", lambda e, kc=kc: e.matmul(ps[pb][:, 0:W], wv_[:, kc, c * 128:(c + 1) * 128],
                                                         hb[:, kc, :], start=(kc == 0), stop=(kc == 31)),
                         reads=[rws[wi], r_hb], writes=[rps[pb]])
                return pb

            for i in range(NT):
                t0 = i * W
                ld("sp", hb[:], h1b[:, :, t0:t0 + W].rearrange("c p n -> p c n"), [r_hb])
                if i == 0:
                    S.op("dve", lambda e: e.tensor_scalar(out=hb[:, :, 0:2], in0=hb[:, :, 0:2], scalar1=hv_t[:, 0:1],
                                                          scalar2=None, op0=ALU.mult), reads=[r_hb, r_const],
                         writes=[r_hb])
                for s in range(43):
                    p2 = s % 2
                    wi, wv_ = slab(wupg_s[s].rearrange("p k c -> p (k c)"), 32 * 256, 256)
                    for c in range(2):
                        fc = 2 * s + c
                        pb = up_chunk(wi, wv_, c)
                        conv_chunk(pb, fc, cgS[p2][:, c, :], rcgS[p2])
                        S.op("act", lambda e, c=c, p2=p2: e.activation(out=cgS[p2][:, c, :], in_=cgS[p2][:, c, :],
                                                                       func=AF.Silu), reads=[rcgS[p2]],
                             writes=[rcgS[p2]])
                    wi, wv_ = slab(wupv_s[s].rearrange("p k c -> p (k c)"), 32 * 256, 256)
                    for c in range(2):
                        fc = 2 * s + c
                        pb = up_chunk(wi, wv_, c)
                        ci = fc % 2
                        conv_chunk(pb, 86 + fc, cv[ci][:], rcv[ci])
                        S.op("dve", lambda e, c=c, p2=p2, ci=ci, fc=fc: e.tensor_tensor(
                            out=abuf[:, fc, :], in0=cgS[p2][:, c, :], in1=cv[ci][:], op=ALU.mult),
                             reads=[rcgS[p2], rcv[ci]], writes=[r_a])
                if adbg is not None and i == 1:
                    ld("pool", adbg.rearrange("c p n -> p c n"), abuf[:], [], reads=[r_a])
                for mc in range(32):
                    po = dps()
                    for half in range(2):
                        wi, vd = slab(wdn_s[mc, :, half * 43:(half + 1) * 43, :].rearrange("p k c -> p (k c)"),
                                      43 * 128, 128)
                        for kk in range(43):
                            kc = half * 43 + kk
                            S.op("pe", lambda e, kk=kk, kc=kc, vd=vd, po=po: e.matmul(
                                ps[po][:, 0:W], vd[:, kk, :], abuf[:, kc, :], start=(kc == 0), stop=(kc == 85)),
                                 reads=[rws[wi], r_a], writes=[rps[po]])
                    xi = mc % 3
                    ld("sp", hres[xi][:], h1f[mc, :, t0:t0 + W], [rhres[xi]])
                    S.op("dve", lambda e, xi=xi, po=po: e.scalar_tensor_tensor(
                        out=zch[xi][:], in0=hres[xi][:], scalar=ALPHA, in1=ps[po][:, 0:W], op0=ALU.mult, op1=ALU.add),
                         reads=[rhres[xi], rps[po]], writes=[rzch[xi]])
                    S.op("act", lambda e, xi=xi: e.copy(out=zbf[xi][:, 0, :], in_=zch[xi][:]), reads=[rzch[xi]],
                         writes=[rzbf[xi]])
                    S.op("act", lambda e, xi=xi: e.activation(out=zbf[xi][:, 1, :], in_=zch[xi][:], func=AF.Square),
                         reads=[rzch[xi]], writes=[rzbf[xi]])
                    S.op("pe", lambda e, xi=xi, mc=mc: e.matmul(ps[PS_M1][:, 0:W], ones_b[:, :], zbf[xi][:, 0, :],
                                                                start=(mc == 0), stop=(mc == 31)),
                         reads=[rzbf[xi], r_const], writes=[rps[PS_M1]])
                    S.op("pe", lambda e, xi=xi, mc=mc: e.matmul(ps[PS_M2][:, 0:W], ones_b[:, :], zbf[xi][:, 1, :],
                                                                start=(mc == 0), stop=(mc == 31)),
                         reads=[rzbf[xi], r_const], writes=[rps[PS_M2]])
                    ld("pool", zscr[mc, :, t0:t0 + W], zch[xi][:], [r_z[i][mc]], reads=[rzch[xi]])
                S.op("dve", lambda e: e.tensor_scalar(out=mean[:], in0=ps[PS_M1][:, 0:W], scalar1=1.0 / D, scalar2=None,
                                                      op0=ALU.mult), reads=[rps[PS_M1]], writes=[r_stat])
                S.op("dve", lambda e: e.tensor_tensor(out=msq[:], in0=mean[:], in1=mean[:], op=ALU.mult),
                     reads=[r_stat], writes=[r_stat])
                S.op("dve", lambda e: e.scalar_tensor_tensor(out=rstd[:], in0=ps[PS_M2][:, 0:W], scalar=1.0 / D,
                                                             in1=msq[:], op0=ALU.mult, op1=ALU.subtract),
                     reads=[rps[PS_M2], r_stat], writes=[r_stat])
                S.op("act", lambda e: e.activation(out=rstd[:], in_=rstd[:], func=AF.Sqrt, bias=eps_t[:, 1:2],
                                                   scale=1.0), reads=[r_stat, r_const], writes=[r_stat])
                S.op("dve", lambda e: e.reciprocal(out=rstd[:], in_=rstd[:]), reads=[r_stat], writes=[r_stat])
                lo = 2 if i == 0 else 0
                for mc in range(32):
                    xi = mc % 3
                    oi = mc % 2
                    ld("sp", zch[xi][:], zscr[mc, :, t0:t0 + W], [rzch[xi]], reads=[r_z[i][mc]])
                    S.op("dve", lambda e, xi=xi: e.tensor_tensor(out=zch[xi][:], in0=zch[xi][:], in1=mean[:],
                                                                 op=ALU.subtract), reads=[rzch[xi], r_stat],
                         writes=[rzch[xi]])
                    S.op("pool", lambda e, xi=xi: e.tensor_tensor(out=zch[xi][:], in0=zch[xi][:], in1=rstd[:],
                                                                  op=ALU.mult), reads=[rzch[xi], r_stat],
                         writes=[rzch[xi]])
                    S.op("dve", lambda e, xi=xi, oi=oi, mc=mc: e.tensor_scalar(
                        out=och[oi][:], in0=zch[xi][:], scalar1=vcol("ln2_g", mc), scalar2=vcol("ln2_b", mc),
                        op0=ALU.mult, op1=ALU.add), reads=[rzch[xi], r_const], writes=[roch[oi]])
                    ld("pool", outT[mc * 128:(mc + 1) * 128, t0 + lo - 2:t0 + W - 2], och[oi][:, lo:W], [],
                       reads=[roch[oi]])
            S.flush()
    return nc


_PROG = [None]
_DEBUG_HOOK = [None]


def _feat_cols(v):
    v = np.asarray(v, np.float32).reshape(-1)
    return np.ascontiguousarray(v.reshape(-1, 128).T)


def kernel(x, positions, w_in, b_gate, da_lambda_q1, da_lambda_k1, da_lambda_q2, da_lambda_k2, da_subln_g,
           mla_q_norm_g, mla_kv_norm_g, w_uq, w_ukv, w_proj_a, w_proj_b, w_out, ln1_g, ln1_b, w_up, conv_w, conv_b,
           w_down, ln2_g, ln2_b):
    x = np.asarray(x, np.float32)
    positions = np.asarray(positions, np.int32)
    bf = ml_dtypes.bfloat16
    if _PROG[0] is None:
        _PROG[0] = build_program()
    nc = _PROG[0]

    vecs = np.zeros((128, NV), np.float32)

    def put(name, v):
        c = _feat_cols(v)
        vecs[:, VEC[name]:VEC[name] + c.shape[1]] = c

    put("b_gate", b_gate[0]); put("ln1_g", ln1_g[0]); put("ln1_b", ln1_b[0]); put("ln2_g", ln2_g[0])
    put("ln2_b", ln2_b[0]); put("cw0", conv_w[0, 0]); put("cw1", conv_w[0, 1]); put("cw2", conv_w[0, 2])
    put("cb", conv_b[0]); put("subln", da_subln_g[0]); put("qn_g", mla_q_norm_g[0]); put("kvn_g", mla_kv_norm_g[0])
    lamv = np.concatenate([np.asarray(a, np.float32).reshape(-1) for a in
                           (da_lambda_q1[0], da_lambda_k1[0], da_lambda_q2[0], da_lambda_k2[0])])
    lamv = np.ascontiguousarray(np.broadcast_to(lamv[None, :], (128, 512)))
    inv = (np.float32(10000.0) ** (-np.arange(32, dtype=np.float32) / np.float32(32))).astype(np.float32)
    inv2 = np.concatenate([inv, inv]).reshape(64, 1).astype(np.float32)
    rot = np.zeros((64, 64), np.float32)
    for i in range(32):
        rot[32 + i, i] = -1.0
        rot[i, 32 + i] = 1.0
    slopes = 2.0 ** (-(np.arange(1, 9, dtype=np.float64)))
    alk = np.zeros((3, 8, 128), np.float32)
    alq = np.zeros((8, 3, NQ), np.float32)
    npr = np.arange(NQ) - (NQ - 1)
    a64 = np.floor(npr / 64.0) * 64.0
    b64 = npr - a64
    for h in range(8):
        alk[0, h, :] = 1.0
        alk[1, h, :] = 1.0
        alk[2, h, :] = -slopes[h] * np.arange(128)
        alq[h, 0, :] = -slopes[h] * a64
        alq[h, 1, :] = -slopes[h] * b64
        alq[h, 2, :] = 1.0
    masks = MASKS_NP.astype(bf)

    shared = {"w_in": np.ascontiguousarray(w_in[0], np.float32), "w_uq": np.ascontiguousarray(w_uq[0], np.float32),
              "w_ukv": np.ascontiguousarray(w_ukv[0], np.float32), "w_pa": np.ascontiguousarray(w_proj_a[0], np.float32),
              "w_pb": np.ascontiguousarray(w_proj_b[0], np.float32), "w_out": np.ascontiguousarray(w_out[0], np.float32),
              "w_up": np.ascontiguousarray(w_up[0], np.float32), "w_dn": np.ascontiguousarray(w_down[0], np.float32),
              "vecs": vecs, "lamv": lamv, "inv2": inv2, "rotm": rot.astype(bf), "alk": alk.astype(bf),
              "alq": alq.astype(bf), "masks": masks}
    in_maps = []
    for c in range(NCORE):
        b, j = c // 4, c % 4
        c0 = 2048 * j
        xb = x[b]
        pb = positions[b]
        xqT = np.zeros((D, NQ), np.float32)
        posq = np.zeros((NQ,), np.int32)
        lo = 2 if j == 0 else 0
        xqT[:, lo:] = xb[c0 - 2 + lo:c0 + 2048].T
        posq[lo:] = pb[c0 - 2 + lo:c0 + 2048]
        nvalid = c0 + 2048
        xkT = np.zeros((D, NKV), np.float32)
        posk = np.zeros((NKV,), np.int32)
        xkT[:, :nvalid] = xb[:nvalid][::-1].T
        posk[:nvalid] = pb[:nvalid][::-1]
        r = np.arange(NKV).reshape(NB, 128).T
        kvalid = np.where(r < nvalid, 0.0, NEG).astype(np.float32)
        abias = np.zeros((128, 8, NT, NB), np.float32)
        for h in range(8):
            for i in range(NT):
                if h < ALIBI_MM_HEADS:
                    abias[:, h, i, :] = kvalid + (-slopes[h] * 128.0 * np.arange(NB))[None, :]
                else:
                    nref = W * i + W - 1 - (NQ - 1)
                    abias[:, h, i, :] = kvalid - slopes[h] * (r.astype(np.float64) + nref)
        m = dict(shared)
        m.update({"xqT": xqT, "xkT": xkT,
                  "posq": np.ascontiguousarray(np.broadcast_to(posq[None, :], (64, NQ))),
                  "posk": np.ascontiguousarray(np.broadcast_to(posk[None, :], (64, NKV))),
                  "abias": abias.astype(np.float32), "kvalid": kvalid,
                  "hv": np.full((128, 1), 0.0 if j == 0 else 1.0, np.float32)})
        in_maps.append(m)
    if _DEBUG_HOOK[0] is not None:
        return _DEBUG_HOOK[0](in_maps)
    res = run_bass_kernel_spmd(nc, in_maps, core_ids=list(range(NCORE)))
    out = np.empty((2, SEQ, D), np.float32)
    for c in range(NCORE):
        b, j = c // 4, c % 4
        out[b, 2048 * j:2048 * (j + 1), :] = res.results[c]["outT"].T
    return out
```
